# Optimizing a Trainium2 kernel written in Bass

```python
import math
import jax, jax.numpy as jnp
from jax import lax
import numpy as np

D_MODEL = 1024
BATCH = 4
SEQ = 8192
DEPTH = 1
DEC_BATCH = 2
DEC_SEQ = 16384
PAST_LEN = 128

MIX_WIDTH = D_MODEL
ATTN_WIDTH = MIX_WIDTH // 2
A_HEADS = 4
V_DIM = ATTN_WIDTH // A_HEADS
QK_DIM = V_DIM // 2
ATTN_QK_WIDTH = A_HEADS * 2 * QK_DIM
HGRN_WIDTH = MIX_WIDTH - ATTN_WIDTH
HGRN_HEADS = 4
HGRN_DK = HGRN_WIDTH // HGRN_HEADS
HGRN_DV = HGRN_WIDTH // HGRN_HEADS
D_FF = 4 * D_MODEL
N_BUCKETS = 32
MAX_DISTANCE = 128
Q_BLOCK = 128
CHUNK = 64
EPS = 1e-6
SPLIT_SIZES = (ATTN_QK_WIDTH, ATTN_QK_WIDTH, ATTN_WIDTH,
               HGRN_WIDTH, HGRN_WIDTH, HGRN_WIDTH, HGRN_WIDTH, HGRN_WIDTH)
SPLIT_POINTS = tuple(int(v) for v in np.cumsum(SPLIT_SIZES)[:-1])
PROJ_WIDTH = int(sum(SPLIT_SIZES))

kernel_name = "hymba_diffattn_hgrn2_encoder"


def rms_norm(x, w):
    xf = x.astype(jnp.float32)
    xf = xf * lax.rsqrt(jnp.mean(xf * xf, axis=-1, keepdims=True) + EPS)
    return (xf * w.astype(jnp.float32)).astype(x.dtype)


def t5_bucket(rel):
    nb = N_BUCKETS // 2
    max_exact = nb // 2
    ret = jnp.where(rel > 0, nb, 0)
    n = jnp.abs(rel)
    nf = jnp.maximum(n, 1).astype(jnp.float32)
    large = max_exact + (jnp.log(nf / max_exact) / math.log(MAX_DISTANCE / max_exact)
                         * (nb - max_exact)).astype(jnp.int32)
    large = jnp.minimum(large, nb - 1)
    return ret + jnp.where(n < max_exact, n, large)


def diff_attention(q, k, v, lam, rel_bias):
    B, S = q.shape[0], q.shape[1]
    nblk = S // Q_BLOCK
    scale = QK_DIM ** -0.5
    k_pos = jnp.arange(S)
    qb = q.reshape(B, nblk, Q_BLOCK, A_HEADS, 2, QK_DIM).transpose(1, 0, 2, 3, 4, 5)
    starts = jnp.arange(nblk) * Q_BLOCK

    def block(args):
        q_blk, start = args
        q_pos = start + jnp.arange(Q_BLOCK)
        bucket = t5_bucket(k_pos[None, :] - q_pos[:, None])
        bias = jnp.take(rel_bias, bucket, axis=0)
        bias = jnp.transpose(bias, (2, 0, 1)).astype(jnp.float32)
        logits = jnp.einsum('bqhcd,bkhcd->bhcqk', q_blk, k).astype(jnp.float32) * scale
        logits = logits + bias[None, :, None]
        p = jax.nn.softmax(logits, axis=-1)
        w = p[:, :, 0] - lam * p[:, :, 1]
        return jnp.einsum('bhqk,bkhd->bqhd', w, v)

    out = lax.map(block, (qb, starts))
    return out.transpose(1, 0, 2, 3, 4).reshape(B, S, A_HEADS, V_DIM)


def gla_chunkwise(q, k, v, log_f):
    B, H, S, K = q.shape
    V = v.shape[-1]
    N = S // CHUNK
    q = q.reshape(B, H, N, CHUNK, K)
    k = k.reshape(B, H, N, CHUNK, K)
    v = v.reshape(B, H, N, CHUNK, V)
    b = jnp.cumsum(log_f.reshape(B, H, N, CHUNK, K), axis=3)
    b_last = b[:, :, :, -1:, :]
    q_t = q * jnp.exp(b)
    k_t = k * jnp.exp(-b)
    mask = jnp.tril(jnp.ones((CHUNK, CHUNK), dtype=jnp.float32))
    A = jnp.einsum('bhnck,bhnsk->bhncs', q_t, k_t) * mask
    o_intra = jnp.einsum('bhncs,bhnsv->bhncv', A, v)
    dS = jnp.einsum('bhnck,bhncv->bhnkv', k * jnp.exp(b_last - b), v)
    decay = jnp.exp(b_last[:, :, :, 0, :])

    def step(s_prev, inp):
        ds_n, dec_n = inp
        return dec_n[..., None] * s_prev + ds_n, s_prev

    s0 = jnp.zeros((B, H, K, V), dtype=jnp.float32)
    _, s_prevs = lax.scan(step, s0, (jnp.moveaxis(dS, 2, 0), jnp.moveaxis(decay, 2, 0)))
    s_prevs = jnp.moveaxis(s_prevs, 0, 2)
    o_inter = jnp.einsum('bhnck,bhnkv->bhncv', q_t, s_prevs)
    return (o_intra + o_inter).reshape(B, H, S, V)


def encoder_layer(x, l, attn_norm_w, w_in, qk_norm_w, diff_lambda, diff_subln_w,
                  rel_bias, hgrn_lb, hgrn_norm_w, w_out, mlp_norm_w, w_mlp_in, w_mlp_out):
    B, S, _ = x.shape
    f32 = jnp.float32
    h = rms_norm(x, attn_norm_w[l])
    proj = h @ w_in[l]
    q_a, k_a, v_a, q_h, f_fw, f_bw, i_h, g_h = jnp.split(proj, SPLIT_POINTS, axis=-1)

    q_a = rms_norm(q_a.reshape(B, S, A_HEADS, 2, QK_DIM), qk_norm_w[l, 0])
    k_a = rms_norm(k_a.reshape(B, S, A_HEADS, 2, QK_DIM), qk_norm_w[l, 1])
    v_a = v_a.reshape(B, S, A_HEADS, V_DIM)
    lam_p = diff_lambda[l].astype(f32)
    lam_init = 0.8 - 0.6 * math.exp(-0.3 * l)
    lam = jnp.exp(jnp.sum(lam_p[0] * lam_p[1])) - jnp.exp(jnp.sum(lam_p[2] * lam_p[3])) + lam_init
    o_a = diff_attention(q_a, k_a, v_a, lam, rel_bias)
    o_a = (rms_norm(o_a, diff_subln_w[l]) * (1.0 - lam_init)).astype(x.dtype)
    o_a = o_a.reshape(B, S, ATTN_WIDTH)

    def to_heads(t):
        return t.reshape(B, S, HGRN_HEADS, -1).transpose(0, 2, 1, 3)

    lb = jnp.cumsum(jax.nn.softmax(hgrn_lb.astype(f32), axis=1), axis=1)[:, l]
    qh = to_heads(jax.nn.silu(q_h.astype(f32)) * (HGRN_DK ** -0.5))
    vh = to_heads(i_h.astype(f32))

    def gates(f_raw, lb_d):
        f = lb_d + (1.0 - lb_d) * jax.nn.sigmoid(f_raw.astype(f32))
        return to_heads(1.0 - f), to_heads(jnp.log(f))

    k_fw, lf_fw = gates(f_fw, lb[0])
    k_bw, lf_bw = gates(f_bw, lb[1])
    o_fw = gla_chunkwise(qh, k_fw, vh, lf_fw)
    flip = lambda t: jnp.flip(t, axis=2)
    o_bw = flip(gla_chunkwise(flip(qh), flip(k_bw), flip(vh), flip(lf_bw)))
    o_h = (o_fw + o_bw).transpose(0, 2, 1, 3)
    o_h = rms_norm(o_h, hgrn_norm_w[l]) * jax.nn.silu(g_h.astype(f32).reshape(B, S, HGRN_HEADS, HGRN_DV))
    o_h = o_h.astype(x.dtype).reshape(B, S, HGRN_WIDTH)

    x = x + jnp.concatenate([o_a, o_h], axis=-1) @ w_out[l]

    h2 = rms_norm(x, mlp_norm_w[l])
    x = x + jnp.square(jax.nn.relu(h2 @ w_mlp_in[l])) @ w_mlp_out[l]
    return x


def encoder_trunk(x, attn_norm_w, w_in, qk_norm_w, diff_lambda, diff_subln_w,
                  rel_bias, hgrn_lb, hgrn_norm_w, w_out, mlp_norm_w, w_mlp_in, w_mlp_out):
    for l in range(DEPTH):
        x = encoder_layer(x, l, attn_norm_w, w_in, qk_norm_w, diff_lambda, diff_subln_w,
                          rel_bias, hgrn_lb, hgrn_norm_w, w_out, mlp_norm_w, w_mlp_in, w_mlp_out)
    return x


def setup_inputs(seed: int = 0) -> dict:
    key = jax.random.key(seed)
    ks = jax.random.split(key, 14)
    n = jax.random.normal
    f32 = jnp.float32
    return {
        "x_prompt": n(ks[0], (BATCH, SEQ, D_MODEL), f32),
        "x_sample": n(ks[1], (DEC_BATCH, DEC_SEQ, D_MODEL), f32),
        "attn_norm_w": 1.0 + 0.02 * n(ks[2], (DEPTH, D_MODEL), f32),
        "w_in": n(ks[3], (DEPTH, D_MODEL, PROJ_WIDTH), f32) * D_MODEL ** -0.5,
        "qk_norm_w": 1.0 + 0.02 * n(ks[4], (DEPTH, 2, QK_DIM), f32),
        "diff_lambda": 0.1 * n(ks[5], (DEPTH, 4, QK_DIM), f32),
        "diff_subln_w": 1.0 + 0.02 * n(ks[6], (DEPTH, V_DIM), f32),
        "rel_bias": 0.5 * n(ks[7], (N_BUCKETS, A_HEADS), f32),
        "hgrn_lb": 0.5 * n(ks[8], (2, DEPTH + 1, HGRN_WIDTH), f32),
        "hgrn_norm_w": 1.0 + 0.02 * n(ks[9], (DEPTH, HGRN_DV), f32),
        "w_out": n(ks[10], (DEPTH, MIX_WIDTH, D_MODEL), f32) * MIX_WIDTH ** -0.5,
        "mlp_norm_w": 1.0 + 0.02 * n(ks[11], (DEPTH, D_MODEL), f32),
        "w_mlp_in": n(ks[12], (DEPTH, D_MODEL, D_FF), f32) * D_MODEL ** -0.5,
        "w_mlp_out": n(ks[13], (DEPTH, D_FF, D_MODEL), f32) * D_FF ** -0.5,
    }


def reference(x_prompt, x_sample, attn_norm_w, w_in, qk_norm_w, diff_lambda, diff_subln_w,
              rel_bias, hgrn_lb, hgrn_norm_w, w_out, mlp_norm_w, w_mlp_in, w_mlp_out):
    y_prompt = encoder_trunk(x_prompt, attn_norm_w, w_in, qk_norm_w, diff_lambda, diff_subln_w,
                             rel_bias, hgrn_lb, hgrn_norm_w, w_out, mlp_norm_w, w_mlp_in, w_mlp_out)
    y_sample = encoder_trunk(x_sample, attn_norm_w, w_in, qk_norm_w, diff_lambda, diff_subln_w,
                             rel_bias, hgrn_lb, hgrn_norm_w, w_out, mlp_norm_w, w_mlp_in, w_mlp_out)
    return (y_prompt, y_sample)
```

```python
import bisect
import math
from contextlib import ExitStack

import numpy as np
import concourse.bass as bass
import concourse.mybir as mybir
from concourse.bass_utils import run_bass_kernel_spmd

F32 = mybir.dt.float32
BF16 = mybir.dt.bfloat16
AF = mybir.ActivationFunctionType
ALU = mybir.AluOpType
AX = mybir.AxisListType

D = 1024
NH = 4
PROJ = 4096
DFF = 4096
EPS = 1e-6
NEG = -30000.0
C_QA, C_KA, C_VA, C_QH, C_FF, C_FB, C_IH, C_GH = [512 * i for i in range(8)]
LAM_INIT = 0.8 - 0.6 * math.exp(-0.3 * 0)
NFLAG = 24
FL_DIRF, FL_CONT, FL_ENDF, FL_ENDB, FL_LASTM, FL_PREVM, FL_NEXTM = 0, 4, 8, 12, 16, 20, 22


class Eng:
    def __init__(self, K, name, eng):
        self.name, self.eng = name, eng
        self.sem = K.nc.alloc_semaphore("e_" + name)
        self.ninstr = 0
        self.sigpos = []
        self.waited = {}


class Buf:
    def __init__(self, name, t=None, dram=False):
        self.name, self.t, self.dram = name, t, dram
        self.w = {}
        self.r = {}
        self.dsem = None
        self.dcount = 0
        self.dwrites = {}

    def __getitem__(self, k):
        return self.t[k]


class Kern:
    def __init__(self, nc):
        self.nc = nc
        self.PE = Eng(self, "pe", nc.tensor)
        self.ACT = Eng(self, "act", nc.scalar)
        self.DVE = Eng(self, "dve", nc.vector)
        self.POOL = Eng(self, "pool", nc.gpsimd)
        self.SP = Eng(self, "sp", nc.sync)
        self.engs = [self.PE, self.ACT, self.DVE, self.POOL, self.SP]
        self.nsem = 5
        self.dma_bufs = []

    def _resolve(self, ev):
        if ev[0] == "d":
            return ev[1], ev[2]
        _, E, pos = ev
        i = bisect.bisect_left(E.sigpos, pos)
        assert i < len(E.sigpos), f"no covering signal on {E.name}"
        return E.sem, i + 1

    def _wait(self, E, deps, raw=()):
        need = {}
        rawset = set(id(e) for e in raw)
        for ev in list(deps) + list(raw):
            if ev is None:
                continue
            if ev[0] == "c" and ev[1] is E and E is not self.POOL and (E is self.PE or id(ev) not in rawset):
                continue
            sem, val = self._resolve(ev[:3])
            k = id(sem)
            if k not in need or need[k][1] < val:
                need[k] = (sem, val)
        for k, (sem, val) in need.items():
            if E.waited.get(k, 0) < val:
                E.eng.wait_ge(sem, val)
                E.waited[k] = val

    @staticmethod
    def _deps(r, w):
        raw, other = [], []
        for b in r:
            if b.dram:
                raw += list(b.dwrites.values())
            else:
                raw += list(b.w.values())
        for b in w:
            if b.dram:
                continue
            other += list(b.w.values())
            other += list(b.r.values())
        return raw, other

    def op(self, E, fn, r=(), w=(), sig=True, small=False):
        raw, other = self._deps(r, w)
        self._wait(E, other, raw)
        ins = fn()
        pos = E.ninstr
        E.ninstr += 1
        if sig:
            ins.then_inc(E.sem, 1)
            E.sigpos.append(pos)
        ev = ("c", E, pos, small)
        for b in r:
            if not b.dram:
                b.r[E.name] = ev
        for b in w:
            b.w[E.name] = ev
            b.r = {}
        return ins

    def dma(self, Q, out, in_, r, w, sb):
        raw, other = self._deps(r, w)
        self._wait(Q, other, raw)
        if sb.dsem is None:
            sb.dsem = self.nc.alloc_semaphore("d_" + sb.name)
            self.nsem += 1
            self.dma_bufs.append(sb)
        Q.eng.dma_start(out=out, in_=in_).then_inc(sb.dsem, 16)
        sb.dcount += 16
        ev = ("d", sb.dsem, sb.dcount)
        for b in r:
            if not b.dram:
                b.r["dma" + str(id(sb.dsem))] = ev
        for b in w:
            if b.dram:
                b.dwrites[id(sb.dsem)] = ev
            else:
                b.w["dma" + str(id(sb.dsem))] = ev
                b.r = {}

    def barrier(self):
        for E in self.engs:
            self.finish(E)
        for E in self.engs:
            for E2 in self.engs:
                if E2 is E or not E2.sigpos:
                    continue
                assert E2.sigpos[-1] == E2.ninstr - 1, f"last instr on {E2.name} not signalled"
                val = len(E2.sigpos)
                if E.waited.get(id(E2.sem), 0) < val:
                    E.eng.wait_ge(E2.sem, val)
                    E.waited[id(E2.sem)] = val

    def finish(self, Q):
        for sb in self.dma_bufs:
            k = id(sb.dsem)
            if sb.dcount and Q.waited.get(k, 0) < sb.dcount:
                Q.eng.wait_ge(sb.dsem, sb.dcount)
                Q.waited[k] = sb.dcount


class Cfg:
    def __init__(self, TO=4096, dbg=False):
        self.TO = TO
        self.NTO = TO // 128
        self.jobs = [dict(name="P", nseg=1, seg0=0, j=0), dict(name="S", nseg=3, seg0=1, j=1)]
        for J in self.jobs:
            J["NF"] = J["nseg"] * self.NTO
            J["NK"] = J["NF"] + 2 + self.NTO
        self.dbg = dbg
        self.serial = False


def build(cfg):
    nc = bass.Bass("TRN2", target_bir_lowering=False)
    K = Kern(nc)
    PE, ACT, DVE, POOL, SP = K.PE, K.ACT, K.DVE, K.POOL, K.SP
    NTO, TO = cfg.NTO, cfg.TO
    ctr = [0]

    def sb(name, shape, dt, stack=None):
        ctr[0] += 1
        nm = f"{name}_{ctr[0]}"
        if stack is None:
            t = nc.alloc_sbuf_tensor(nm, list(shape), dt)
        else:
            t = stack.enter_context(nc.sbuf_tensor(nm, list(shape), dt))
        return Buf(nm, t)

    def ps(name, shape, dt, stack):
        ctr[0] += 1
        nm = f"{name}_{ctr[0]}"
        t = stack.enter_context(nc.psum_tensor(nm, list(shape), dt))
        return Buf(nm, t)

    def dram_in(name, shape, dt=F32):
        return Buf(name, nc.dram_tensor(name, list(shape), dt, kind="ExternalInput").ap(), dram=True)

    def dram_out(name, shape, dt=F32):
        return Buf(name, nc.dram_tensor(name, list(shape), dt, kind="ExternalOutput").ap(), dram=True)

    def dram_scr(name, shape, dt):
        kind = "ExternalOutput" if cfg.dbg else "Internal"
        return Buf(name, nc.dram_tensor(name, list(shape), dt, kind=kind).ap(), dram=True)

    for J in cfg.jobs:
        n = J["name"]
        J["x"] = dram_in("x_" + n, [J["NK"] * 128, D])
        J["y"] = dram_out("y_" + n, [TO, D])
        J["QT"] = dram_scr("QT_" + n, [NH, 128, TO], BF16)
        J["KT"] = dram_scr("KT_" + n, [NH, 128, J["NK"] * 128], BF16)
        J["V"] = dram_scr("V_" + n, [NH, 128, J["NK"], 129], BF16)
        J["HG"] = dram_scr("HG_" + n, [NTO, 128, 5, 512], F32)
        J["HGo"] = dram_scr("HGo_" + n, [J["NF"], 128, 2, 512], F32)
        J["OF"] = dram_scr("OF_" + n, [NTO, 128, 512], F32)
        J["OB"] = dram_scr("OB_" + n, [NTO, 128, 512], F32)
        J["O"] = dram_scr("O_" + n, [NTO, 128, D], BF16)
        if cfg.dbg:
            J["dSF"] = dram_scr("dSF_" + n, [128, 512], F32)
            J["dSB"] = dram_scr("dSB_" + n, [128, 512], F32)
        J["X1"] = dram_scr("X1_" + n, [NTO, 128, D], F32)
        J["H2T"] = dram_scr("H2T_" + n, [8, 128, TO], BF16)
    w_in_d = dram_in("w_in", [D, PROJ])
    w_out_d = dram_in("w_out", [D, D])
    w_mi_d = dram_in("w_mlp_in", [D, DFF])
    w_mo_d = dram_in("w_mlp_out", [DFF, D])
    attn_nw_d = dram_in("attn_norm_w", [1, D])
    mlp_nw_d = dram_in("mlp_norm_w", [1, D])
    qk_nw_d = dram_in("qk_norm_w", [2, 64])
    dlam_d = dram_in("diff_lambda", [4, 64])
    subln_d = dram_in("diff_subln_w", [1, 128])
    relb_d = dram_in("rel_bias", [32, 4])
    hlb_d = dram_in("hgrn_lb", [2, 2, 512])
    hnw_d = dram_in("hgrn_norm_w", [1, 128])
    flags_d = dram_in("flags", [1, NFLAG])
    idxm_d = dram_in("idxm", [128, 384])

    ident_f = sb("identf", [128, 128], F32)
    ident = sb("ident", [128, 128], BF16)
    blk1_f = sb("blk1f", [128, 128], F32)
    blk1 = sb("blk1", [128, 128], BF16)
    tri = {k: sb("tri" + k, [128, 128], F32) for k in ("fi", "fx", "bi", "bx")}
    ones_c = sb("ones", [128, 1], F32)
    eps_c = sb("eps", [128, 1], F32)
    eps64_c = sb("eps64", [128, 1], F32)
    flg = sb("flg", [128, NFLAG], F32)
    nflg = sb("nflg", [128, NFLAG], F32)
    relb = sb("relb", [128, 128], F32)
    kscale = sb("kscale", [128, 1], F32)
    hbias = sb("hbias", [128, 16], F32)
    lam_t = sb("lam", [128, 4], F32)
    subln = sb("subln", [128, 128], F32)
    hnw = sb("hnw", [128, 512], F32)
    oml = [sb(f"oml{d}", [128, 512], F32) for d in range(2)]
    bfar = sb("bfar", [128, 32], F32)
    anw = sb("anw", [128, 8], F32)
    mnw = sb("mnw", [128, 8], F32)

    st_setup = ExitStack()
    dl = sb("dl", [128, 256], F32, st_setup)
    lbt = sb("lbt", [128, 2048], F32, st_setup)
    qkb = sb("qkb", [128, 128], F32, st_setup)
    wprod = sb("wprod", [128, 128], F32, st_setup)
    wbc = [sb("wbc", [128, D], F32, st_setup) for _ in range(2)]
    junk_s = sb("junk_s", [128, 64], F32, st_setup)

    def pool_op(fn, w, r=()):
        K.op(POOL, fn, r=r, w=w)

    g = nc.gpsimd
    pool_op(lambda: g.memset(ident_f[:], 1.0), [ident_f])
    pool_op(lambda: g.affine_select(out=ident_f[:], in_=ident_f[:], pattern=[[-1, 128]], compare_op=ALU.is_equal,
                                    fill=0.0, base=0, channel_multiplier=1), [ident_f], [ident_f])
    pool_op(lambda: g.memset(blk1_f[:], 0.0), [blk1_f])
    pool_op(lambda: g.memset(blk1_f[0:64, 0:64], 1.0), [blk1_f])
    pool_op(lambda: g.memset(blk1_f[64:128, 64:128], 1.0), [blk1_f])
    for k, (cm, pat, base, cmp) in dict(fi=(-1, 1, 0, ALU.is_ge), fx=(1, -1, 0, ALU.is_gt),
                                         bi=(1, -1, 0, ALU.is_ge), bx=(-1, 1, 0, ALU.is_gt)).items():
        t = tri[k]
        pool_op(lambda t=t: g.memset(t[:], 0.0), [t])
        pool_op(lambda t=t: g.memset(t[0:64, 0:64], 1.0), [t])
        pool_op(lambda t=t: g.memset(t[64:128, 64:128], 1.0), [t])
        pool_op(lambda t=t, cm=cm, pat=pat, cmp=cmp: g.affine_select(
            out=t[:], in_=t[:], pattern=[[pat, 128]], compare_op=cmp, fill=0.0, base=0, channel_multiplier=cm), [t], [t])
    pool_op(lambda: g.memset(ones_c[:], 1.0), [ones_c])
    pool_op(lambda: g.memset(eps_c[:], EPS), [eps_c])
    pool_op(lambda: g.memset(eps64_c[:], 64.0 * EPS), [eps64_c])

    def bload(dst, src_ap, n):
        K.dma(SP, dst[:, 0:n], src_ap.to_broadcast([128, n]), r=[], w=[dst], sb=dst)

    bload(flg, flags_d[0:1, :], NFLAG)
    bload(relb, relb_d.t.rearrange("(o b) h -> o (b h)", o=1), 128)
    bload(dl, dlam_d.t.rearrange("(o b) h -> o (b h)", o=1), 256)
    bload(subln, subln_d[0:1, :], 128)
    bload(lbt, hlb_d.t.rearrange("(o a) b c -> o (a b c)", o=1), 2048)
    for h in range(4):
        K.dma(SP, hnw[:, h * 128:(h + 1) * 128], hnw_d[0:1, :].to_broadcast([128, 128]), r=[], w=[hnw], sb=hnw)
    bload(qkb, qk_nw_d.t.rearrange("(o b) h -> o (b h)", o=1), 128)
    bload(wbc[0], attn_nw_d[0:1, :], D)
    bload(wbc[1], mlp_nw_d[0:1, :], D)

    v = nc.vector
    a = nc.scalar
    K.op(DVE, lambda: v.tensor_copy(out=ident[:], in_=ident_f[:]), r=[ident_f], w=[ident])
    K.op(DVE, lambda: v.tensor_copy(out=blk1[:], in_=blk1_f[:]), r=[blk1_f], w=[blk1])
    K.op(DVE, lambda: v.tensor_scalar(out=nflg[:], in0=flg[:], scalar1=-1.0, scalar2=1.0, op0=ALU.mult, op1=ALU.add),
         r=[flg], w=[nflg], small=True)
    for half in range(2):
        K.op(DVE, lambda half=half: v.scalar_tensor_tensor(out=wprod[:, half * 64:(half + 1) * 64], in0=qkb[:, 0:64], scalar=8.0,
                                                           in1=qkb[:, 64:128], op0=ALU.mult, op1=ALU.mult),
             r=[qkb], w=[wprod], small=True)
    K.op(DVE, lambda: v.tensor_tensor(out=wprod[:, :], in0=wprod[:, :], in1=ident_f[:, :], op=ALU.mult),
         r=[wprod, ident_f], w=[wprod], small=True)
    K.op(DVE, lambda: v.tensor_reduce(out=kscale[:, 0:1], in_=wprod[:, :], axis=AX.X, op=ALU.add), r=[wprod], w=[kscale], small=True)
    with ExitStack() as st0:
        wtmp0 = sb("wtmp0", [128, 8, 128], F32, st0)
        for wi, dst in ((0, anw), (1, mnw)):
            K.op(DVE, lambda wi=wi: v.tensor_tensor(out=wtmp0[:, :, :], in0=wbc[wi][:, :].rearrange("p (c j) -> p c j", c=8),
                                                    in1=ident_f[:, :].unsqueeze(1).to_broadcast([128, 8, 128]), op=ALU.mult),
                 r=[wbc[wi], ident_f], w=[wtmp0])
            K.op(DVE, lambda dst=dst: v.tensor_reduce(out=dst[:, :], in_=wtmp0[:, :, :], axis=AX.X, op=ALU.add),
                 r=[wtmp0], w=[dst], small=True)
    for jj_ in range(2):
        for which, (bk, fm_) in enumerate(((15, FL_PREVM), (31, FL_NEXTM))):
            c0_ = (jj_ * 2 + which) * 4
            K.op(DVE, lambda c0_=c0_, bk=bk, fm_=fm_, jj_=jj_: v.tensor_scalar(
                out=hbias[:, c0_:c0_ + 4], in0=relb[:, bk * 4:bk * 4 + 4], scalar1=flg[:, fm_ + jj_:fm_ + jj_ + 1], scalar2=None,
                op0=ALU.add), r=[relb, flg], w=[hbias], small=True)
    for i in range(2):
        K.op(DVE, lambda i=i: v.tensor_tensor(out=junk_s[:], in0=dl[:, 128 * i:128 * i + 64],
                                              in1=dl[:, 128 * i + 64:128 * i + 128], op=ALU.mult),
             r=[dl], w=[junk_s], small=True)
        K.op(DVE, lambda i=i: v.tensor_reduce(out=lam_t[:, i:i + 1], in_=junk_s[:], axis=AX.X, op=ALU.add),
             r=[junk_s], w=[lam_t], small=True)
    K.op(ACT, lambda: a.activation(out=lam_t[:, 2:4], in_=lam_t[:, 0:2], func=AF.Exp), r=[lam_t], w=[lam_t], small=True)
    K.op(DVE, lambda: v.tensor_tensor(out=lam_t[:, 0:1], in0=lam_t[:, 3:4], in1=lam_t[:, 2:3], op=ALU.subtract),
         r=[lam_t], w=[lam_t], small=True)
    K.op(DVE, lambda: v.tensor_scalar(out=lam_t[:, 0:1], in0=lam_t[:, 0:1], scalar1=-LAM_INIT, scalar2=None, op0=ALU.add),
         r=[lam_t], w=[lam_t], small=True)
    for d_ in range(2):
        K.op(DVE, lambda d_=d_: v.tensor_tensor(out=oml[d_][:], in0=lbt[:, d_ * 1024:d_ * 1024 + 512],
                                                in1=lbt[:, d_ * 1024 + 512:d_ * 1024 + 1024], op=ALU.subtract),
             r=[lbt], w=[oml[d_]])
        K.op(ACT, lambda d_=d_: a.activation(out=oml[d_][:], in_=oml[d_][:], func=AF.Exp), r=[oml[d_]], w=[oml[d_]], small=True)
        K.op(DVE, lambda d_=d_: v.tensor_scalar(out=oml[d_][:], in0=oml[d_][:], scalar1=1.0, scalar2=None, op0=ALU.add),
             r=[oml[d_]], w=[oml[d_]], small=True)
        K.op(DVE, lambda d_=d_: v.reciprocal(out=oml[d_][:], in_=oml[d_][:]), r=[oml[d_]], w=[oml[d_]], small=True)
    for s in range(4):
        K.op(DVE, lambda s=s: v.tensor_scalar(out=bfar[:, 4 * s:4 * s + 4], in0=relb[:, 60:64],
                                              scalar1=flg[:, FL_DIRF + s:FL_DIRF + s + 1], scalar2=None, op0=ALU.mult),
             r=[relb, flg], w=[bfar], small=True)
        K.op(DVE, lambda s=s: v.scalar_tensor_tensor(out=bfar[:, 4 * s:4 * s + 4], in0=relb[:, 124:128],
                                                     scalar=nflg[:, FL_DIRF + s:FL_DIRF + s + 1], in1=bfar[:, 4 * s:4 * s + 4],
                                                     op0=ALU.mult, op1=ALU.add), r=[relb, nflg, bfar], w=[bfar], small=True)
        K.op(DVE, lambda s=s: v.tensor_scalar(out=bfar[:, 16 + 4 * s:16 + 4 * s + 4], in0=bfar[:, 4 * s:4 * s + 4],
                                              scalar1=flg[:, FL_LASTM + s:FL_LASTM + s + 1], scalar2=None, op0=ALU.add),
             r=[bfar, flg], w=[bfar], small=True)

    K.barrier()
    st_setup.close()

    with ExitStack() as st:
        win = sb("win", [128, 8, PROJ], BF16, st)
        wsel = sb("wsel", [128, 8, 512], BF16, st)
        xt = [sb("xt", [128, D], F32, st) for _ in range(4)]
        hb = [sb("hb", [128, D], BF16, st) for _ in range(8)]
        sqj = sb("sqj", [128, D], BF16, st)
        ss = [sb("ss", [128, 8], F32, st) for _ in range(3)]
        hT = [sb("hT", [128, 8, 512], BF16, st) for _ in range(2)]
        sqb = [sb("sqb", [128, 512], BF16, st) for _ in range(2)]
        rs = [sb("rs", [128, 512], F32, st) for _ in range(2)]
        ktst = [sb("ktst", [128, NH, 512], BF16, st) for _ in range(2)]
        qtst = [sb("qtst", [128, NH, 512], BF16, st) for _ in range(2)]
        vst = [sb("vst", [128, NH, 4, 129], BF16, st) for _ in range(2)]
        hgst = [sb("hgst", [128, 512], F32, st) for _ in range(6)]
        sg = [sb("sg", [128, 512], F32, st) for _ in range(2)]
        tp_ps = [ps("tp", [128, 8, 128], BF16, st) for _ in range(2)]
        fm_ps = [ps("fm", [128, 512], F32, st) for _ in range(2)]
        ssb_ps = ps("ssb", [128, 512], F32, st)
        tm_ps = [ps("tm", [128, 512], F32, st) for _ in range(3)]

        for b in vst:
            K.op(POOL, lambda b=b: g.memset(b[:], 1.0), w=[b])

        w_view = w_in_d.t.rearrange("(c p) n -> p c n", p=128)
        cnt = 0
        for c in range(8):
            for q4 in range(4):
                col = q4 * 1024
                stg = xt[cnt % 3]
                K.dma(SP, stg[:, :], w_view[:, c, col:col + 1024], r=[w_in_d], w=[stg], sb=stg)
                if cnt % 2 == 0:
                    K.op(DVE, lambda c=c, col=col, stg=stg: v.tensor_scalar(
                        out=win[:, c, col:col + 1024], in0=stg[:, :], scalar1=anw[:, c:c + 1], scalar2=None,
                        op0=ALU.mult), r=[stg, anw], w=[win])
                else:
                    K.op(ACT, lambda c=c, col=col, stg=stg: a.activation(
                        out=win[:, c, col:col + 1024], in_=stg[:, :], func=AF.Copy, scale=anw[:, c:c + 1]),
                        r=[stg, anw], w=[win])
                cnt += 1

        gi = [0]
        sti = [0]
        hgi = [0]
        fmi = [0]
        tmi = [0]

        def norm_a(J, rows):
            nt = len(rows)
            i0 = gi[0]
            gi[0] += 4
            s_ = ss[(i0 // 4) % 3]
            for t, trow in enumerate(rows):
                x_ = xt[(i0 + t) % 4]
                K.dma(SP, x_[:, :], J["x"].t[trow * 128:(trow + 1) * 128, :], r=[J["x"]], w=[x_], sb=x_)
                K.op(ACT, lambda x_=x_, t=t: a.activation(out=sqj[:, :], in_=x_[:, :], func=AF.Square, accum_out=s_[:, t:t + 1]),
                     r=[x_], w=[sqj, s_])
            K.op(ACT, lambda: a.activation(out=s_[:, 4:4 + nt], in_=s_[:, 0:nt], func=AF.Ln, bias=eps_c[:], scale=1.0 / D),
                 r=[s_, eps_c], w=[s_])
            K.op(ACT, lambda: a.activation(out=s_[:, 4:4 + nt], in_=s_[:, 4:4 + nt], func=AF.Exp, scale=-0.5), r=[s_], w=[s_])
            for t in range(nt):
                x_, h_ = xt[(i0 + t) % 4], hb[(i0 + t) % 8]
                K.op(DVE, lambda x_=x_, h_=h_, t=t: v.tensor_scalar(out=h_[:, :], in0=x_[:, :], scalar1=s_[:, 4 + t:5 + t], scalar2=None,
                                                                    op0=ALU.mult), r=[x_, s_], w=[h_])
            return i0

        def norm_b(i0, nt, hslot):
            for t in range(nt):
                i = i0 + t
                h_ = hb[i % 8]
                tp = tp_ps[i % 2]
                for c in range(8):
                    K.op(PE, lambda c=c, h_=h_, tp=tp: nc.tensor.transpose(tp[:, c, :], h_[:, c * 128:(c + 1) * 128], ident[:]),
                         r=[h_, ident], w=[tp], sig=(c == 7))
                K.op(DVE, lambda tp=tp, t=t: v.tensor_copy(out=hT[hslot][:, :, t * 128:(t + 1) * 128], in_=tp[:, :, :]),
                     r=[tp], w=[hT[hslot]])

        def fm_block(hslot, N, col, is_k, dst, h):
            i = fmi[0]
            fmi[0] += 1
            p_ = fm_ps[i % 2]
            for c in range(8):
                K.op(PE, lambda c=c: nc.tensor.matmul(p_[:, 0:N], lhsT=win[:, c, col:col + 128], rhs=hT[hslot][:, c, 0:N],
                                                      start=(c == 0), stop=(c == 7)),
                     r=[win, hT[hslot]], w=[p_], sig=(c == 7))
            q_, r_ = sqb[i % 2], rs[i % 2]
            K.op(ACT, lambda: a.activation(out=q_[:, 0:N], in_=p_[:, 0:N], func=AF.Square), r=[p_], w=[q_])
            K.op(PE, lambda: nc.tensor.matmul(ssb_ps[:, 0:N], lhsT=blk1[:], rhs=q_[:, 0:N], start=True, stop=True),
                 r=[blk1, q_], w=[ssb_ps])
            K.op(ACT, lambda: a.activation(out=r_[:, 0:N], in_=ssb_ps[:, 0:N], func=AF.Ln, bias=eps64_c[:], scale=1.0),
                 r=[ssb_ps, eps64_c], w=[r_])
            K.op(ACT, lambda: a.activation(out=r_[:, 0:N], in_=r_[:, 0:N], func=AF.Exp, scale=-0.5), r=[r_], w=[r_],
                 small=(N < 256))
            if is_k:
                K.op(DVE, lambda: v.scalar_tensor_tensor(out=dst[:, h, 0:N], in0=p_[:, 0:N], scalar=kscale[:, 0:1],
                                                         in1=r_[:, 0:N], op0=ALU.mult, op1=ALU.mult),
                     r=[p_, kscale, r_], w=[dst])
            else:
                K.op(DVE, lambda: v.tensor_tensor(out=dst[:, h, 0:N], in0=p_[:, 0:N], in1=r_[:, 0:N], op=ALU.mult),
                     r=[p_, r_], w=[dst])

        def tm_matmul(hslot, t, wbuf, col):
            i = tmi[0]
            tmi[0] += 1
            p_ = tm_ps[i % 3]
            for c in range(8):
                K.op(PE, lambda c=c: nc.tensor.matmul(p_[:, :], lhsT=hT[hslot][:, c, t * 128:(t + 1) * 128],
                                                      rhs=wbuf[:, c, col:col + 512], start=(c == 0), stop=(c == 7)),
                     r=[hT[hslot], wbuf], w=[p_], sig=(c == 7))
            return p_

        def hg_store(src_ap_fn, eng, dst_ap, dstbuf, extra_r=()):
            i = hgi[0]
            hgi[0] += 1
            s_ = hgst[i % 6]
            src_ap_fn(s_)
            K.dma(POOL, dst_ap, s_[:, :], r=[s_], w=[dstbuf], sb=s_)

        def silu_block(p_, scale, dst_ap, dstbuf):
            i = hgi[0]
            s1 = sg[i % 2]
            K.op(ACT, lambda: a.activation(out=s1[:, :], in_=p_[:, :], func=AF.Exp, scale=-1.0), r=[p_], w=[s1])
            K.op(ACT, lambda: a.activation(out=s1[:, :], in_=s1[:, :], func=AF.Ln, bias=ones_c[:], scale=1.0),
                 r=[s1, ones_c], w=[s1])
            K.op(ACT, lambda: a.activation(out=s1[:, :], in_=s1[:, :], func=AF.Exp, scale=-1.0), r=[s1], w=[s1])

            def ev(s_):
                K.op(DVE, lambda: v.scalar_tensor_tensor(out=s_[:, :], in0=p_[:, :], scalar=scale, in1=s1[:, :],
                                                         op0=ALU.mult, op1=ALU.mult), r=[p_, s1], w=[s_])
            hg_store(ev, DVE, dst_ap, dstbuf)

        def copy_block(p_, eng, dst_ap, dstbuf):
            def ev(s_):
                if eng is DVE:
                    K.op(DVE, lambda: v.tensor_copy(out=s_[:, :], in_=p_[:, :]), r=[p_], w=[s_])
                else:
                    K.op(ACT, lambda: a.copy(out=s_[:, :], in_=p_[:, :]), r=[p_], w=[s_])
            hg_store(ev, eng, dst_ap, dstbuf)

        def super_tile(J, kind, rows, kt0, own0=None, seg=None, far0=None, i0=None, nxt=None):
            si = sti[0]
            sti[0] += 1
            hslot = si % 2
            nt = len(rows)
            N = nt * 128
            norm_b(i0, nt, hslot)
            nxt_i0 = norm_a(*nxt) if nxt is not None else None
            kst, qst, vs_ = ktst[si % 2], qtst[si % 2], vst[si % 2]
            for h in range(NH):
                fm_block(hslot, N, C_KA + h * 128, True, kst, h)
            for h in range(NH):
                K.dma(POOL, J["KT"].t[h, :, kt0 * 128:kt0 * 128 + N], kst[:, h, 0:N], r=[kst], w=[J["KT"]], sb=kst)
            if kind == "own":
                for h in range(NH):
                    fm_block(hslot, N, C_QA + h * 128, False, qst, h)
                for h in range(NH):
                    K.dma(POOL, J["QT"].t[h, :, own0 * 128:own0 * 128 + N], qst[:, h, 0:N], r=[qst], w=[J["QT"]], sb=qst)
            for t in range(nt):
                p_ = tm_matmul(hslot, t, win, C_VA)
                K.op(DVE, lambda p_=p_, t=t: v.tensor_copy(out=vs_[:, :, t, 0:128], in_=p_[:, :].rearrange("p (h d) -> p h d", h=NH)),
                     r=[p_], w=[vs_])
                if kind == "own":
                    hgd = J["HG"]
                    ti = own0 + t
                    p_ = tm_matmul(hslot, t, win, C_QH)
                    silu_block(p_, 128.0 ** -0.5, hgd.t[ti, :, 0, :], hgd)
                    p_ = tm_matmul(hslot, t, win, C_FF)
                    copy_block(p_, DVE, hgd.t[ti, :, 1, :], hgd)
                    p_ = tm_matmul(hslot, t, win, C_FB)
                    copy_block(p_, DVE, hgd.t[ti, :, 2, :], hgd)
                    p_ = tm_matmul(hslot, t, win, C_IH)
                    copy_block(p_, DVE, hgd.t[ti, :, 3, :], hgd)
                    p_ = tm_matmul(hslot, t, win, C_GH)
                    silu_block(p_, 1.0, hgd.t[ti, :, 4, :], hgd)
                elif kind == "far":
                    hgd = J["HGo"]
                    ti = far0 + t
                    p_ = tm_matmul(hslot, t, wsel, 0)
                    copy_block(p_, DVE, hgd.t[ti, :, 0, :], hgd)
                    p_ = tm_matmul(hslot, t, win, C_IH)
                    copy_block(p_, DVE, hgd.t[ti, :, 1, :], hgd)
            for h in range(NH):
                K.dma(POOL, J["V"].t[h, :, kt0:kt0 + nt, :], vs_[:, h, 0:nt, :], r=[vs_], w=[J["V"]], sb=vs_)
            return nxt_i0

        stl = []
        for J in cfg.jobs:
            NF = J["NF"]
            for s_ in range(J["nseg"]):
                for t0 in range(0, NTO, 4):
                    f0 = s_ * NTO + t0
                    stl.append(dict(J=J, kind="far", rows=list(range(f0, f0 + 4)), kt0=f0, own0=None, far0=f0, seg=s_,
                                    first=(t0 == 0)))
            stl.append(dict(J=J, kind="halo", rows=[NF, NF + 1], kt0=NF, own0=None, far0=None, seg=None, first=False))
            for t0 in range(0, NTO, 4):
                stl.append(dict(J=J, kind="own", rows=list(range(NF + 2 + t0, NF + 2 + t0 + 4)), kt0=NF + 2 + t0, own0=t0,
                                far0=None, seg=None, first=False))
        i0 = norm_a(stl[0]["J"], stl[0]["rows"])
        for k_, T in enumerate(stl):
            if T["first"]:
                fs = FL_DIRF + T["J"]["seg0"] + T["seg"]
                K.op(DVE, lambda fs=fs: v.tensor_scalar(out=wsel[:, :, :], in0=win[:, :, C_FF:C_FF + 512],
                                                        scalar1=flg[:, fs:fs + 1], scalar2=None, op0=ALU.mult),
                     r=[win, flg], w=[wsel])
                K.op(DVE, lambda fs=fs: v.scalar_tensor_tensor(out=wsel[:, :, :], in0=win[:, :, C_FB:C_FB + 512],
                                                               scalar=nflg[:, fs:fs + 1], in1=wsel[:, :, :],
                                                               op0=ALU.mult, op1=ALU.add), r=[win, nflg, wsel], w=[wsel])
            nxt = (stl[k_ + 1]["J"], stl[k_ + 1]["rows"]) if k_ + 1 < len(stl) else None
            i0 = super_tile(T["J"], T["kind"], T["rows"], T["kt0"], own0=T["own0"], seg=T["seg"], far0=T["far0"], i0=i0, nxt=nxt)

    K.barrier()

    with ExitStack() as st:
        def make_bufs():
            NB = 2
            S = sb("S", [128, 512], F32, st)
            Sbf = [sb("Sbf", [128, 512], BF16, st) for _ in range(2)]
            SF = sb("SF", [128, 512], F32, st)
            SB = sb("SB", [128, 512], F32, st)
            omlsel = sb("omlsel", [128, 512], F32, st)
            trixsel = sb("trixsel", [128, 128], F32, st)
            zin = [sb("zin", [128, 512], F32, st) for _ in range(NB)]
            vin = [sb("vin", [128, 512], F32, st) for _ in range(NB)]
            qin = [sb("qin", [128, 512], F32, st) for _ in range(NB)]
            e1 = [sb("e1", [128, 512], F32, st) for _ in range(NB)]
            kf = [sb("kf", [128, 512], F32, st) for _ in range(NB)]
            lf = [sb("lf", [128, 512], F32, st) for _ in range(NB)]
            eb = [sb("eb", [128, 512], F32, st) for _ in range(NB)]
            enb = [sb("enb", [128, 512], F32, st) for _ in range(NB)]
            ebl = [sb("ebl", [128, 512], F32, st) for _ in range(NB)]
            qt = [sb("qt", [128, 512], BF16, st) for _ in range(NB)]
            kt_ = [sb("ktt", [128, 512], BF16, st) for _ in range(NB)]
            kk = [sb("kk", [128, 512], BF16, st) for _ in range(NB)]
            vbf = [sb("vbf", [128, 512], BF16, st) for _ in range(NB)]
            dcy = [sb("dcy", [128, 8], F32, st) for _ in range(NB)]
            qkT = [sb("qkT", [128, 8, 128], BF16, st) for _ in range(NB)]
            atsb = [sb("atsb", [128, NH, 128], BF16, st) for _ in range(NB)]
            ost = [sb("ost", [128, 512], F32, st) for _ in range(NB)]
            pp_ps = ps("ppps", [128, 512], F32, st)
            tpb_ps = ps("tpb", [128, 8, 128], BF16, st)
            ao_ps = ps("aops", [128, 512], F32, st)
            ds_ps = ps("dsps", [128, 512], F32, st)
            bi = [0]
            sbi = [0]

            def hg_prep(J, own, src, tix, trii, trix, omlb, d_idx):
                i = bi[0] % NB
                bi[0] += 1
                z_, v_, q_ = zin[i], vin[i], qin[i]
                if own:
                    K.dma(SP, z_[:, :], src.t[tix, :, 1 + d_idx, :], r=[src], w=[z_], sb=z_)
                    K.dma(SP, v_[:, :], src.t[tix, :, 3, :], r=[src], w=[v_], sb=v_)
                    K.dma(SP, q_[:, :], src.t[tix, :, 0, :], r=[src], w=[q_], sb=q_)
                else:
                    K.dma(SP, z_[:, :], src.t[tix, :, 0, :], r=[src], w=[z_], sb=z_)
                    K.dma(SP, v_[:, :], src.t[tix, :, 1, :], r=[src], w=[v_], sb=v_)
                yield
                e_, k_, l_ = e1[i], kf[i], lf[i]
                K.op(ACT, lambda: a.activation(out=e_[:, :], in_=z_[:, :], func=AF.Exp), r=[z_], w=[e_])
                K.op(DVE, lambda: v.tensor_copy(out=vbf[i][:, :], in_=v_[:, :]), r=[v_], w=[vbf[i]])
                yield
                K.op(ACT, lambda: a.activation(out=e_[:, :], in_=e_[:, :], func=AF.Ln, bias=ones_c[:], scale=1.0),
                     r=[e_, ones_c], w=[e_])
                yield
                K.op(ACT, lambda: a.activation(out=e_[:, :], in_=e_[:, :], func=AF.Exp, scale=-1.0), r=[e_], w=[e_])
                yield
                K.op(DVE, lambda: v.tensor_tensor(out=k_[:, :], in0=e_[:, :], in1=omlb[:, :], op=ALU.mult),
                     r=[e_, omlb], w=[k_])
                yield
                K.op(ACT, lambda: a.activation(out=l_[:, :], in_=k_[:, :], func=AF.Ln, bias=ones_c[:], scale=-1.0),
                     r=[k_, ones_c], w=[l_])
                yield
                K.op(PE, lambda: nc.tensor.matmul(pp_ps[:, :], lhsT=trix[:], rhs=l_[:, :], start=True, stop=True),
                     r=[trix, l_], w=[pp_ps])
                yield
                K.op(ACT, lambda: a.activation(out=ebl[i][:, :], in_=pp_ps[:, :], func=AF.Exp), r=[pp_ps], w=[ebl[i]])
                yield
                K.op(DVE, lambda: v.tensor_tensor(out=kk[i][:, :], in0=k_[:, :], in1=ebl[i][:, :], op=ALU.mult),
                     r=[k_, ebl[i]], w=[kk[i]])
                if own:
                    K.op(PE, lambda: nc.tensor.matmul(pp_ps[:, :], lhsT=trii[:], rhs=l_[:, :], start=True, stop=True),
                         r=[trii, l_], w=[pp_ps])
                    yield
                    K.op(ACT, lambda: a.activation(out=eb[i][:, :], in_=pp_ps[:, :], func=AF.Exp), r=[pp_ps], w=[eb[i]])
                    K.op(ACT, lambda: a.activation(out=enb[i][:, :], in_=pp_ps[:, :], func=AF.Exp, scale=-1.0),
                         r=[pp_ps], w=[enb[i]])
                    yield
                    K.op(DVE, lambda: v.tensor_tensor(out=qt[i][:, :], in0=q_[:, :], in1=eb[i][:, :], op=ALU.mult),
                         r=[q_, eb[i]], w=[qt[i]])
                    K.op(DVE, lambda: v.tensor_tensor(out=kt_[i][:, :], in0=k_[:, :], in1=enb[i][:, :], op=ALU.mult),
                         r=[k_, enb[i]], w=[kt_[i]])
                    yield
                    for h in range(NH):
                        K.op(PE, lambda h=h: nc.tensor.transpose(tpb_ps[:, h, :], qt[i][:, h * 128:(h + 1) * 128], ident[:]),
                             r=[qt[i], ident], w=[tpb_ps], sig=False)
                    for h in range(NH):
                        K.op(PE, lambda h=h: nc.tensor.transpose(tpb_ps[:, 4 + h, :], kt_[i][:, h * 128:(h + 1) * 128], ident[:]),
                             r=[kt_[i], ident], w=[tpb_ps], sig=(h == NH - 1))
                    yield
                    K.op(DVE, lambda: v.tensor_copy(out=qkT[i][:, :, :], in_=tpb_ps[:, :, :]), r=[tpb_ps], w=[qkT[i]])
                    yield
                    for h in range(NH):
                        K.op(PE, lambda h=h: nc.tensor.matmul(pp_ps[:, h * 128:(h + 1) * 128], lhsT=qkT[i][:, 4 + h, :],
                                                              rhs=qkT[i][:, h, :], start=True, stop=True),
                             r=[qkT[i]], w=[pp_ps], sig=(h == NH - 1))
                    yield
                    K.op(DVE, lambda: v.tensor_tensor(out=atsb[i][:, :, :], in0=pp_ps[:, :].rearrange("p (h c) -> p h c", h=NH),
                                                      in1=trii[:, :].unsqueeze(1).to_broadcast([128, NH, 128]), op=ALU.mult),
                         r=[pp_ps, trii], w=[atsb[i]])
                yield
                for j in range(2):
                    for h in range(NH):
                        K.op(PE, lambda j=j, h=h: nc.tensor.matmul(
                            pp_ps[:, j * 4 + h:j * 4 + h + 1], lhsT=l_[64 * j:64 * j + 64, h * 128:(h + 1) * 128],
                            rhs=ones_c[64 * j:64 * j + 64, 0:1], start=True, stop=True),
                            r=[l_, ones_c], w=[pp_ps], sig=(j == 1 and h == NH - 1))
                yield
                K.op(ACT, lambda: a.activation(out=dcy[i][:, :], in_=pp_ps[:, 0:8], func=AF.Exp), r=[pp_ps], w=[dcy[i]])
                yield
                return i

            def hg_recur(J, own, i, tix, chunk_order, trii, d_idx):
                if own:
                    for h in range(NH):
                        hs = slice(h * 128, (h + 1) * 128)
                        K.op(PE, lambda h=h, hs=hs: nc.tensor.matmul(ao_ps[:, hs], lhsT=atsb[i][:, h, :], rhs=vbf[i][:, hs],
                                                                     start=(h == 0), stop=False, skip_group_check=True),
                             r=[atsb[i], vbf[i]], w=[ao_ps], sig=(h == NH - 1))
                    yield
                for j in chunk_order:
                    pr = slice(64 * j, 64 * j + 64)
                    if own:
                        sb_cur = Sbf[sbi[0] % 2]
                        for h in range(NH):
                            hs = slice(h * 128, (h + 1) * 128)
                            K.op(PE, lambda h=h, hs=hs: nc.tensor.matmul(ao_ps[pr, hs], lhsT=qkT[i][:, h, pr], rhs=sb_cur[:, hs],
                                                                         start=False, stop=True, skip_group_check=True),
                                 r=[qkT[i], sb_cur], w=[ao_ps], sig=False)
                    for h in range(NH):
                        hs = slice(h * 128, (h + 1) * 128)
                        K.op(PE, lambda hs=hs: nc.tensor.matmul(ds_ps[:, hs], lhsT=kk[i][pr, hs], rhs=vbf[i][pr, hs],
                                                                start=True, stop=True), r=[kk[i], vbf[i]], w=[ds_ps],
                             sig=(h == NH - 1))
                    yield
                    for h in range(NH):
                        hs = slice(h * 128, (h + 1) * 128)
                        K.op(DVE, lambda h=h, hs=hs: v.scalar_tensor_tensor(
                            out=S[:, hs], in0=S[:, hs], scalar=dcy[i][:, j * 4 + h:j * 4 + h + 1], in1=ds_ps[:, hs],
                            op0=ALU.mult, op1=ALU.add), r=[S, dcy[i], ds_ps], w=[S])
                    if own:
                        sbi[0] += 1
                        nxt = Sbf[sbi[0] % 2]
                        K.op(DVE, lambda nxt=nxt: v.tensor_copy(out=nxt[:, :], in_=S[:, :]), r=[S], w=[nxt])
                    yield
                if own:
                    K.op(ACT, lambda: a.copy(out=ost[i][:, :], in_=ao_ps[:, :]), r=[ao_ps], w=[ost[i]])
                    dst = J["OF"] if d_idx == 0 else J["OB"]
                    K.dma(POOL, dst.t[tix, :, :], ost[i][:, :], r=[ost[i]], w=[dst], sb=ost[i])
                    yield

            def run_items(J, items):
                pend = None
                for it in items + [None]:
                    g_prep = hg_prep(J, it["own"], it["src"], it["tix"], it["trii"], it["trix"], it["oml"], it["d_idx"]) \
                        if it is not None else None
                    g_rec = None
                    if pend is not None:
                        pit, pi = pend
                        if pit.get("pre"):
                            pit["pre"]()
                        g_rec = hg_recur(J, pit["own"], pi, pit["tix"], pit["chunk_order"], pit["trii"], pit["d_idx"])
                    slot = None
                    while g_prep is not None or g_rec is not None:
                        if g_prep is not None:
                            try:
                                next(g_prep)
                            except StopIteration as e_:
                                slot = e_.value
                                g_prep = None
                        if g_rec is not None:
                            try:
                                next(g_rec)
                            except StopIteration:
                                g_rec = None
                        yield
                    if pend is not None and pend[0].get("post"):
                        pend[0]["post"]()
                    pend = (it, slot) if it is not None else None

            def pass_others(J, done):
                K.op(POOL, lambda: g.memset(S[:, :], 0.0), w=[S])
                K.op(POOL, lambda: g.memset(SF[:, :], 0.0), w=[SF])
                K.op(POOL, lambda: g.memset(SB[:, :], 0.0), w=[SB])
                yield
                for s in range(J["nseg"]):
                    sg_ = J["seg0"] + s
                    fF, fC = FL_DIRF + sg_, FL_CONT + sg_
                    K.op(DVE, lambda fF=fF: v.tensor_scalar(out=omlsel[:, :], in0=oml[0][:, :], scalar1=flg[:, fF:fF + 1],
                                                            scalar2=None, op0=ALU.mult), r=[oml[0], flg], w=[omlsel])
                    K.op(DVE, lambda fF=fF: v.scalar_tensor_tensor(out=omlsel[:, :], in0=oml[1][:, :], scalar=nflg[:, fF:fF + 1],
                                                                   in1=omlsel[:, :], op0=ALU.mult, op1=ALU.add),
                         r=[oml[1], nflg, omlsel], w=[omlsel])
                    K.op(DVE, lambda fF=fF: v.tensor_scalar(out=trixsel[:, :], in0=tri["fx"][:, :], scalar1=flg[:, fF:fF + 1],
                                                            scalar2=None, op0=ALU.mult), r=[tri["fx"], flg], w=[trixsel])
                    K.op(DVE, lambda fF=fF: v.scalar_tensor_tensor(out=trixsel[:, :], in0=tri["bx"][:, :], scalar=nflg[:, fF:fF + 1],
                                                                   in1=trixsel[:, :], op0=ALU.mult, op1=ALU.add),
                         r=[tri["bx"], nflg, trixsel], w=[trixsel])
                    items = [dict(own=False, src=J["HGo"], tix=s * NTO + t, chunk_order=(0, 1), trii=None, trix=trixsel,
                                  oml=omlsel, d_idx=0) for t in range(NTO)]

                    def pre(fC=fC):
                        K.op(DVE, lambda: v.tensor_scalar(out=S[:, :], in0=S[:, :], scalar1=flg[:, fC:fC + 1], scalar2=None,
                                                          op0=ALU.mult), r=[S, flg], w=[S])

                    def post(sg_=sg_):
                        fEF, fEB = FL_ENDF + sg_, FL_ENDB + sg_
                        K.op(DVE, lambda: v.scalar_tensor_tensor(out=SF[:, :], in0=S[:, :], scalar=flg[:, fEF:fEF + 1],
                                                                 in1=SF[:, :], op0=ALU.mult, op1=ALU.add), r=[S, flg, SF], w=[SF])
                        K.op(DVE, lambda: v.scalar_tensor_tensor(out=SB[:, :], in0=S[:, :], scalar=flg[:, fEB:fEB + 1],
                                                                 in1=SB[:, :], op0=ALU.mult, op1=ALU.add), r=[S, flg, SB], w=[SB])
                    items[0]["pre"] = pre
                    items[-1]["post"] = post
                    yield from run_items(J, items)
                done[J["name"]] = (SF, SB)

            def pass_own(J, d_idx, done):
                while J["name"] not in done:
                    yield
                S0 = done[J["name"]][d_idx]
                order, co, ti_, tx_ = [(range(NTO), (0, 1), "fi", "fx"), (range(NTO - 1, -1, -1), (1, 0), "bi", "bx")][d_idx]
                items = [dict(own=True, src=J["HG"], tix=t, chunk_order=co, trii=tri[ti_], trix=tri[tx_], oml=oml[d_idx],
                              d_idx=d_idx) for t in order]

                def pre():
                    K.op(DVE, lambda: v.tensor_copy(out=S[:, :], in_=S0[:, :]), r=[S0], w=[S])
                    K.op(DVE, lambda: v.tensor_copy(out=Sbf[sbi[0] % 2][:, :], in_=S[:, :]), r=[S], w=[Sbf[sbi[0] % 2]])
                items[0]["pre"] = pre
                yield from run_items(J, items)

            return pass_others, pass_own

        def chain(*gens):
            for g_ in gens:
                yield from g_

        JP, JS = cfg.jobs
        done = {}
        po1, pw1 = make_bufs()
        po2, pw2 = make_bufs()
        queue = [(JP, 0), (JS, 0), (JP, 1), (JS, 1)]

        def worker(po, pw, J0):
            yield from po(J0, done)
            while queue:
                Jn, d_ = queue.pop(0)
                yield from pw(Jn, d_, done)

        active = [worker(po1, pw1, JP), worker(po2, pw2, JS)]
        while active:
            for gth in list(active):
                try:
                    next(gth)
                except StopIteration:
                    active.remove(gth)

    K.barrier()

    with ExitStack() as st:
        LKmax = max(J["NK"] for J in cfg.jobs)
        wt = sb("wt", [128, NH, 1152], F32, st)
        idxm = sb("idxm", [128, 384], F32, st)
        wtmp = sb("wtmp", [128, 384], F32, st)
        ktb2 = [sb("ktb", [128, LKmax * 128], BF16, st) for _ in range(2)]
        vab = sb("vab", [128, LKmax, 129], BF16, st)
        qtb2 = [sb("qtb", [128, TO], BF16, st) for _ in range(2)]
        accsb = [sb("accsb", [128, 8, 129], F32, st) for _ in range(2)]
        epi = [0]
        NPB = 3
        pT = [sb("pT", [128, 1024], BF16, st) for _ in range(NPB)]
        stmp = [sb("stmp", [128, 2, 512], F32, st) for _ in range(2)]
        nrm = [sb("nrm", [128, 16], F32, st) for _ in range(2)]
        otmp = [sb("otmp", [128, 512], F32, st) for _ in range(2)]
        osqa = [sb("osqa", [128, 512], F32, st) for _ in range(2)]
        oab = [sb("oab", [128, 4, 128], BF16, st) for _ in range(2)]
        s_ps = [ps("sps", [128, 1024], F32, st) for _ in range(2)]
        acc_ps = [ps("acc", [128, 3, 129], F32, st) for _ in range(3)]

        K.dma(SP, idxm[:, :], idxm_d.t[:, :], r=[idxm_d], w=[idxm], sb=idxm)
        for h in range(NH):
            K.op(DVE, lambda h=h: v.tensor_scalar(out=wt[:, h, 0:384], in0=idxm[:, :], scalar1=0.0,
                                                  scalar2=relb[:, 31 * 4 + h:31 * 4 + h + 1], op0=ALU.mult, op1=ALU.add),
                 r=[idxm, relb], w=[wt])
            K.op(DVE, lambda h=h: v.tensor_scalar(out=wt[:, h, 768:1152], in0=idxm[:, :], scalar1=0.0,
                                                  scalar2=relb[:, 15 * 4 + h:15 * 4 + h + 1], op0=ALU.mult, op1=ALU.add),
                 r=[idxm, relb], w=[wt])
            for b in range(32):
                dst = wt[:, h, 384:768] if b == 0 else wtmp[:, :]
                K.op(DVE, lambda h=h, b=b, dst=dst: v.tensor_scalar(out=dst, in0=idxm[:, :], scalar1=float(b),
                                                                    scalar2=relb[:, b * 4 + h:b * 4 + h + 1],
                                                                    op0=ALU.is_equal, op1=ALU.mult),
                     r=[idxm, relb], w=[wt if b == 0 else wtmp])
                if b > 0:
                    K.op(DVE, lambda h=h: v.tensor_tensor(out=wt[:, h, 384:768], in0=wt[:, h, 384:768], in1=wtmp[:, :], op=ALU.add),
                         r=[wt, wtmp], w=[wt])

        it = [0]
        NG = TO // 512
        units = [(J, h) for J in cfg.jobs for h in range(NH)]

        def load_kq(u):
            J, h = units[u]
            kb_, qb_ = ktb2[u % 2], qtb2[u % 2]
            K.dma(SP, kb_[:, 0:J["NK"] * 128], J["KT"].t[h, :, :], r=[J["KT"]], w=[kb_], sb=kb_)
            K.dma(SP, qb_[:, :], J["QT"].t[h, :, :], r=[J["QT"]], w=[qb_], sb=qb_)

        def epilogue(J, h, gq, asb):
            e = epi[0] % 2
            epi[0] += 1
            n_, o_, q_, ob = nrm[e], otmp[e], osqa[e], oab[e]
            o4 = o_[:, :].rearrange("p (a d) -> p a d", a=4)
            q4 = q_[:, :].rearrange("p (a d) -> p a d", a=4)
            K.op(DVE, lambda: v.reciprocal(out=n_[:, 0:8], in_=asb[:, :, 128]), r=[asb], w=[n_])
            K.op(DVE, lambda: v.tensor_scalar(out=n_[:, 4:8], in0=n_[:, 4:8], scalar1=lam_t[:, 0:1], scalar2=None, op0=ALU.mult),
                 r=[n_, lam_t], w=[n_])
            yield
            K.op(DVE, lambda: v.tensor_tensor(out=o4, in0=asb[:, 0:4, 0:128], in1=n_[:, 0:4].unsqueeze(2).to_broadcast([128, 4, 128]),
                                              op=ALU.mult), r=[asb, n_], w=[o_])
            K.op(DVE, lambda: v.tensor_tensor(out=q4, in0=asb[:, 4:8, 0:128], in1=n_[:, 4:8].unsqueeze(2).to_broadcast([128, 4, 128]),
                                              op=ALU.mult), r=[asb, n_], w=[q_])
            yield
            K.op(DVE, lambda: v.tensor_tensor(out=o_[:, :], in0=o_[:, :], in1=q_[:, :], op=ALU.add), r=[o_, q_], w=[o_])
            yield
            K.op(DVE, lambda: v.tensor_tensor(out=q_[:, :], in0=o_[:, :], in1=o_[:, :], op=ALU.mult), r=[o_], w=[q_])
            yield
            K.op(DVE, lambda: v.tensor_reduce(out=n_[:, 8:12], in_=q4, axis=AX.X, op=ALU.add), r=[q_], w=[n_])
            yield
            K.op(ACT, lambda: a.activation(out=n_[:, 12:16], in_=n_[:, 8:12], func=AF.Ln, bias=eps_c[:], scale=1.0 / 128),
                 r=[n_, eps_c], w=[n_])
            yield
            K.op(ACT, lambda: a.activation(out=n_[:, 12:16], in_=n_[:, 12:16], func=AF.Exp, scale=-0.5), r=[n_], w=[n_])
            yield
            K.op(DVE, lambda: v.tensor_tensor(out=o4, in0=o4, in1=n_[:, 12:16].unsqueeze(2).to_broadcast([128, 4, 128]), op=ALU.mult),
                 r=[o_, n_], w=[o_])
            yield
            K.op(DVE, lambda: v.scalar_tensor_tensor(out=ob[:, :, :], in0=o4, scalar=1.0 - LAM_INIT,
                                                     in1=subln[:, :].unsqueeze(1).to_broadcast([128, 4, 128]),
                                                     op0=ALU.mult, op1=ALU.mult), r=[o_, subln], w=[ob])
            yield
            K.dma(POOL, J["O"].t[gq * 4:(gq + 1) * 4, :, h * 128:(h + 1) * 128].rearrange("t p c -> p t c"), ob[:, :, :],
                  r=[ob], w=[J["O"]], sb=ob)

        pending = []
        load_kq(0)
        for u, (J, h) in enumerate(units):
            NK, NF = J["NK"], J["NF"]
            jj = J["j"]
            ktb, qtb = ktb2[u % 2], qtb2[u % 2]
            K.dma(SP, vab[:, 0:NK, :], J["V"].t[h, :, :, :], r=[J["V"]], w=[vab], sb=vab)
            if u + 1 < len(units):
                load_kq(u + 1)
            if True:
                for gq in range(NG):
                    sched = []
                    for kt in range(NK):
                        if kt < NF:
                            s = kt // NTO
                            last = (kt % NTO) == NTO - 1
                            col = (16 if last else 0) + 4 * (J["seg0"] + s) + h
                            sched.append((kt, "const", bfar[:, col:col + 1]))
                        else:
                            if kt == NF:
                                nb, mask = -1, flg[:, FL_PREVM + jj:FL_PREVM + jj + 1]
                            elif kt == NF + 1:
                                nb, mask = NTO, flg[:, FL_NEXTM + jj:FL_NEXTM + jj + 1]
                            else:
                                nb, mask = kt - NF - 2, None
                            dlt = nb - 4 * gq
                            if -1 <= dlt <= 4:
                                sched.append((kt, "tile", (128 * (4 - dlt), mask)))
                            else:
                                side = relb[:, 15 * 4 + h:15 * 4 + h + 1] if dlt < 0 else relb[:, 31 * 4 + h:31 * 4 + h + 1]
                                if mask is not None:
                                    which = 0 if kt == NF else 1
                                    c0_ = (jj * 2 + which) * 4 + h
                                    sched.append((kt, "const", hbias[:, c0_:c0_ + 1]))
                                else:
                                    sched.append((kt, "const", side))
                    nsch = len(sched)

                    def qk(n):
                        kt = sched[n][0]
                        i = it[0] + n
                        sp_ = s_ps[i % 2]
                        K.op(PE, lambda: nc.tensor.matmul(sp_[:, 0:512], lhsT=ktb[0:64, kt * 128:(kt + 1) * 128],
                                                          rhs=qtb[0:64, gq * 512:(gq + 1) * 512], start=True, stop=True),
                             r=[ktb, qtb], w=[sp_], sig=False)
                        K.op(PE, lambda: nc.tensor.matmul(sp_[:, 512:1024], lhsT=ktb[64:128, kt * 128:(kt + 1) * 128],
                                                          rhs=qtb[64:128, gq * 512:(gq + 1) * 512], start=True, stop=True),
                             r=[ktb, qtb], w=[sp_])

                    def expo(n):
                        kt, mode, arg = sched[n]
                        i = it[0] + n
                        sp_, p_ = s_ps[i % 2], pT[i % NPB]
                        if mode == "const":
                            K.op(ACT, lambda: a.activation(out=p_[:, :], in_=sp_[:, :], func=AF.Exp, bias=arg, scale=1.0),
                                 r=[sp_, bfar, relb, hbias], w=[p_])
                        else:
                            c0, mask = arg
                            tb = stmp[i % 2]
                            K.op(DVE, lambda: v.tensor_tensor(
                                out=tb[:, :, :], in0=sp_[:, :].rearrange("p (c q) -> p c q", c=2),
                                in1=wt[:, h, c0:c0 + 512].unsqueeze(1).to_broadcast([128, 2, 512]), op=ALU.add),
                                r=[sp_, wt], w=[tb])
                            if mask is None:
                                K.op(ACT, lambda: a.activation(out=p_[:, :], in_=tb[:, :, :].rearrange("p c q -> p (c q)"),
                                                               func=AF.Exp), r=[tb], w=[p_])
                            else:
                                K.op(ACT, lambda: a.activation(out=p_[:, :], in_=tb[:, :, :].rearrange("p c q -> p (c q)"),
                                                               func=AF.Exp, bias=mask, scale=1.0), r=[tb, flg], w=[p_])

                    def pv(n):
                        kt = sched[n][0]
                        i = it[0] + n
                        p_ = pT[i % NPB]
                        for c in range(2):
                            for qs in range(4):
                                ai = c * 4 + qs
                                acc = acc_ps[ai // 3]
                                K.op(PE, lambda c=c, qs=qs, ai=ai, acc=acc: nc.tensor.matmul(
                                    acc[:, ai % 3, :], lhsT=p_[:, c * 512 + qs * 128:c * 512 + (qs + 1) * 128], rhs=vab[:, kt, :],
                                    start=(n == 0 and ai % 3 == 0), stop=(n == nsch - 1), skip_group_check=True),
                                    r=[p_, vab], w=[acc], sig=(ai == 7))

                    qk(0)
                    for n in range(nsch):
                        if n + 1 < nsch:
                            qk(n + 1)
                        expo(n)
                        pv(n)
                        if n >= 3 and pending:
                            try:
                                next(pending[0])
                            except StopIteration:
                                pending.pop(0)
                    it[0] += nsch
                    while pending:
                        for _ in pending[0]:
                            pass
                        pending.pop(0)
                    asb = accsb[(u * NG + gq) % 2]
                    for b_ in range(3):
                        nacc = 3 if b_ < 2 else 2
                        K.op(DVE, lambda b_=b_, nacc=nacc: v.tensor_copy(out=asb[:, 3 * b_:3 * b_ + nacc, :], in_=acc_ps[b_][:, 0:nacc, :]),
                             r=[acc_ps[b_]], w=[asb])
                    pending.append(epilogue(J, h, gq, asb))
        while pending:
            for _ in pending[0]:
                pass
            pending.pop(0)

    K.barrier()

    with ExitStack() as st:
        wo = sb("wo", [128, 8, D], BF16, st)
        xin = [sb("xin", [128, D], F32, st) for _ in range(4)]
        oin = [sb("oin", [128, D], BF16, st) for _ in range(4)]
        oT = [sb("oT", [128, 8, 128], BF16, st) for _ in range(4)]
        ofd = [sb("ofd", [128, 512], F32, st) for _ in range(4)]
        obd = [sb("obd", [128, 512], F32, st) for _ in range(4)]
        gsd = [sb("gsd", [128, 512], F32, st) for _ in range(4)]
        ss4d = [sb("ss4d", [128, 8], F32, st) for _ in range(4)]
        x1 = [sb("x1", [128, D], F32, st) for _ in range(4)]
        ssd = [sb("ssd", [128, 2], F32, st) for _ in range(4)]
        sqd = sb("sqd", [128, D], BF16, st)
        h2 = [sb("h2", [128, D], BF16, st) for _ in range(4)]
        h2Tst = [sb("h2Tst", [128, 8, 512], BF16, st) for _ in range(2)]
        tpx_ps = [ps("tpx", [128, 8, 128], BF16, st) for _ in range(4)]
        prh_ps = [ps("prh", [128, 512], F32, st) for _ in range(4)]

        wo_v = w_out_d.t.rearrange("(c p) n -> p c n", p=128)
        for c in range(8):
            s_ = xin[c % 4]
            K.dma(SP, s_[:, :], wo_v[:, c, :], r=[w_out_d], w=[s_], sb=s_)
            if c % 2 == 0:
                K.op(DVE, lambda c=c, s_=s_: v.tensor_copy(out=wo[:, c, :], in_=s_[:, :]), r=[s_], w=[wo])
            else:
                K.op(ACT, lambda c=c, s_=s_: a.copy(out=wo[:, c, :], in_=s_[:, :]), r=[s_], w=[wo])

        def d1_tile(J, i, ti, hst, t):
            NF = J["NF"]
            xi, oi, oT_, x1_ = xin[i % 4], oin[i % 4], oT[i % 4], x1[i % 4]
            K.dma(SP, xi[:, :], J["x"].t[(NF + 2 + ti) * 128:(NF + 3 + ti) * 128, :], r=[J["x"]], w=[xi], sb=xi)
            K.dma(SP, oi[:, 0:512], J["O"].t[ti, :, 0:512], r=[J["O"]], w=[oi], sb=oi)
            of_, ob_, gs_, s4 = ofd[i % 4], obd[i % 4], gsd[i % 4], ss4d[i % 4]
            K.dma(SP, of_[:, :], J["OF"].t[ti, :, :], r=[J["OF"]], w=[of_], sb=of_)
            K.dma(SP, ob_[:, :], J["OB"].t[ti, :, :], r=[J["OB"]], w=[ob_], sb=ob_)
            K.dma(SP, gs_[:, :], J["HG"].t[ti, :, 4, :], r=[J["HG"]], w=[gs_], sb=gs_)
            yield
            K.op(DVE, lambda: v.tensor_tensor(out=of_[:, :], in0=of_[:, :], in1=ob_[:, :], op=ALU.add), r=[of_, ob_], w=[of_])
            K.op(DVE, lambda: v.tensor_tensor(out=ob_[:, :], in0=of_[:, :], in1=of_[:, :], op=ALU.mult), r=[of_], w=[ob_])
            yield
            K.op(DVE, lambda: v.tensor_reduce(out=s4[:, 0:4], in_=ob_[:, :].rearrange("p (h d) -> p h d", h=NH),
                                              axis=AX.X, op=ALU.add), r=[ob_], w=[s4])
            yield
            K.op(ACT, lambda: a.activation(out=s4[:, 4:8], in_=s4[:, 0:4], func=AF.Ln, bias=eps_c[:], scale=1.0 / 128),
                 r=[s4, eps_c], w=[s4])
            yield
            K.op(ACT, lambda: a.activation(out=s4[:, 4:8], in_=s4[:, 4:8], func=AF.Exp, scale=-0.5), r=[s4], w=[s4])
            yield
            o3 = of_[:, :].rearrange("p (h d) -> p h d", h=NH)
            K.op(DVE, lambda: v.tensor_tensor(out=o3, in0=o3, in1=s4[:, 4:8].unsqueeze(2).to_broadcast([128, NH, 128]),
                                              op=ALU.mult), r=[of_, s4], w=[of_])
            K.op(DVE, lambda: v.tensor_tensor(out=of_[:, :], in0=of_[:, :], in1=hnw[:, :], op=ALU.mult), r=[of_, hnw], w=[of_])
            yield
            K.op(DVE, lambda: v.tensor_tensor(out=oi[:, 512:1024], in0=of_[:, :], in1=gs_[:, :], op=ALU.mult),
                 r=[of_, gs_], w=[oi])
            yield
            tp = tpx_ps[i % 4]
            for c in range(8):
                K.op(PE, lambda c=c: nc.tensor.transpose(tp[:, c, :], oi[:, c * 128:(c + 1) * 128], ident[:]),
                     r=[oi, ident], w=[tp], sig=(c == 7))
            yield
            K.op(DVE, lambda: v.tensor_copy(out=oT_[:, :, :], in_=tp[:, :, :]), r=[tp], w=[oT_])
            yield
            pp = prh_ps[i % 4]
            for half in range(2):
                hc = slice(half * 512, (half + 1) * 512)
                for c in range(8):
                    K.op(PE, lambda c=c, hc=hc: nc.tensor.matmul(pp[:, :], lhsT=oT_[:, c, :], rhs=wo[:, c, hc],
                                                                 start=(c == 0), stop=(c == 7)), r=[oT_, wo], w=[pp], sig=(c == 7))
                yield
                K.op(DVE, lambda hc=hc: v.tensor_tensor(out=x1_[:, hc], in0=pp[:, :], in1=xi[:, hc], op=ALU.add),
                     r=[pp, xi], w=[x1_])
                yield
            K.dma(POOL, J["X1"].t[ti, :, :], x1_[:, :], r=[x1_], w=[J["X1"]], sb=x1_)
            yield
            s_ = ssd[i % 4]
            K.op(ACT, lambda: a.activation(out=sqd[:, :], in_=x1_[:, :], func=AF.Square, accum_out=s_[:, 0:1]),
                 r=[x1_], w=[sqd, s_])
            yield
            K.op(ACT, lambda: a.activation(out=s_[:, 1:2], in_=s_[:, 0:1], func=AF.Ln, bias=eps_c[:], scale=1.0 / D),
                 r=[s_, eps_c], w=[s_])
            yield
            K.op(ACT, lambda: a.activation(out=s_[:, 1:2], in_=s_[:, 1:2], func=AF.Exp, scale=-0.5), r=[s_], w=[s_])
            yield
            h2_ = h2[i % 4]
            K.op(ACT, lambda: a.activation(out=h2_[:, :], in_=x1_[:, :], func=AF.Copy, scale=s_[:, 1:2]), r=[x1_, s_], w=[h2_])
            yield
            tp2 = tpx_ps[i % 4]
            for c in range(8):
                K.op(PE, lambda c=c: nc.tensor.transpose(tp2[:, c, :], h2_[:, c * 128:(c + 1) * 128], ident[:]),
                     r=[h2_, ident], w=[tp2], sig=(c == 7))
            yield
            K.op(DVE, lambda: v.tensor_copy(out=hst[:, :, t * 128:(t + 1) * 128], in_=tp2[:, :, :]), r=[tp2], w=[hst])
            if t == 3:
                t0_ = ti - 3
                for c in range(8):
                    K.dma(POOL, J["H2T"].t[c, :, t0_ * 128:(t0_ + 4) * 128], hst[:, c, :], r=[hst], w=[J["H2T"]], sb=hst)

        tiles = []
        for J in cfg.jobs:
            for ti in range(NTO):
                tiles.append((J, ti))
        gens = [d1_tile(J, i, ti, h2Tst[(i // 4) % 2], ti % 4) for i, (J, ti) in enumerate(tiles)]
        live = []
        nxt_i = 0
        while live or nxt_i < len(gens):
            while len(live) < 4 and nxt_i < len(gens):
                live.append(gens[nxt_i])
                nxt_i += 1
            for gth in list(live):
                try:
                    next(gth)
                except StopIteration:
                    live.remove(gth)

    K.barrier()

    with ExitStack() as st:
        wmi = sb("wmi", [128, 8, DFF], BF16, st)
        wmo = sb("wmo", [128, 32, D], BF16, st)
        h2T = [sb("h2T", [128, 8, 512], BF16, st) for _ in range(2)]
        aT = sb("aT", [128, 32, 512], BF16, st)
        rl = [sb("rl", [128, 512], F32, st) for _ in range(2)]
        x1in = [sb("x1in", [128, D], F32, st) for _ in range(2)]
        yo = [sb("yo", [128, D], F32, st) for _ in range(1)]
        mi_ps = [ps("mid", [128, 512], F32, st) for _ in range(2)]
        yo_ps = [ps("yod", [128, 1024], F32, st) for _ in range(2)]

        stgs = x1in + yo
        cnt = 0

        def wload(dst, dbuf, src_ap, srcbuf, scale_ap, scale_buf):
            nonlocal cnt
            s_ = stgs[cnt % 3]
            K.dma(SP, s_[:, :], src_ap, r=[srcbuf], w=[s_], sb=s_)
            if cnt % 2 == 0:
                if scale_ap is None:
                    K.op(DVE, lambda: v.tensor_copy(out=dst, in_=s_[:, :]), r=[s_], w=[dbuf])
                else:
                    K.op(DVE, lambda: v.tensor_scalar(out=dst, in0=s_[:, :], scalar1=scale_ap, scalar2=None, op0=ALU.mult),
                         r=[s_, scale_buf], w=[dbuf])
            else:
                if scale_ap is None:
                    K.op(ACT, lambda: a.copy(out=dst, in_=s_[:, :]), r=[s_], w=[dbuf])
                else:
                    K.op(ACT, lambda: a.activation(out=dst, in_=s_[:, :], func=AF.Copy, scale=scale_ap), r=[s_, scale_buf], w=[dbuf])
            cnt += 1

        wmi_v = w_mi_d.t.rearrange("(c p) n -> p c n", p=128)
        wmo_v = w_mo_d.t.rearrange("(c p) n -> p c n", p=128)
        for c in range(8):
            for q4 in range(4):
                wload(wmi[:, c, q4 * 1024:(q4 + 1) * 1024], wmi, wmi_v[:, c, q4 * 1024:(q4 + 1) * 1024], w_mi_d, mnw[:, c:c + 1], mnw)
        for c in range(32):
            wload(wmo[:, c, :], wmo, wmo_v[:, c, :], w_mo_d, None, None)

        di = [0]
        sti2 = [0]
        for J in cfg.jobs:
            for t0 in range(0, NTO, 4):
                hT_ = h2T[sti2[0] % 2]
                sti2[0] += 1
                for c in range(8):
                    K.dma(SP, hT_[:, c, :], J["H2T"].t[c, :, t0 * 128:(t0 + 4) * 128], r=[J["H2T"]], w=[hT_], sb=hT_)
                for f in range(32):
                    mp = mi_ps[f % 2]
                    for c in range(8):
                        K.op(PE, lambda c=c, f=f: nc.tensor.matmul(mp[:, :], lhsT=wmi[:, c, f * 128:(f + 1) * 128], rhs=hT_[:, c, :],
                                                                   start=(c == 0), stop=(c == 7)), r=[wmi, hT_], w=[mp], sig=(c == 7))
                    r_ = rl[f % 2]
                    K.op(ACT, lambda: a.activation(out=r_[:, :], in_=mp[:, :], func=AF.Relu), r=[mp], w=[r_])
                    K.op(DVE, lambda f=f: v.tensor_tensor(out=aT[:, f, :], in0=r_[:, :], in1=r_[:, :], op=ALU.mult), r=[r_], w=[aT])
                for t in range(4):
                    i = di[0]
                    di[0] += 1
                    ti = t0 + t
                    xr = x1in[i % 2]
                    K.dma(SP, xr[:, :], J["X1"].t[ti, :, :], r=[J["X1"]], w=[xr], sb=xr)
                    yp = yo_ps[i % 2]
                    for half in range(2):
                        for f in range(32):
                            K.op(PE, lambda f=f, half=half: nc.tensor.matmul(
                                yp[:, half * 512:(half + 1) * 512], lhsT=aT[:, f, t * 128:(t + 1) * 128],
                                rhs=wmo[:, f, half * 512:(half + 1) * 512], start=(f == 0), stop=(f == 31)),
                                r=[aT, wmo], w=[yp], sig=(f == 31 and half == 1))
                    y_ = yo[0]
                    K.op(DVE, lambda: v.tensor_tensor(out=y_[:, :], in0=yp[:, :], in1=xr[:, :], op=ALU.add), r=[yp, xr], w=[y_])
                    K.dma(POOL, J["y"].t[ti * 128:(ti + 1) * 128, :], y_[:, :], r=[y_], w=[J["y"]], sb=y_)

    K.finish(POOL)
    K.finish(SP)
    return nc


def _t5_bucket_np(rel):
    nb, max_exact = 16, 8
    rel = np.asarray(rel, dtype=np.int64)
    try:
        import jax
        import jax.numpy as jnp
        with jax.default_device(jax.devices("cpu")[0]):
            r = jnp.asarray(rel.astype(np.int32))
            ret = jnp.where(r > 0, nb, 0)
            n = jnp.abs(r)
            nf = jnp.maximum(n, 1).astype(jnp.float32)
            large = max_exact + (jnp.log(nf / max_exact) / math.log(128 / max_exact) * (nb - max_exact)).astype(jnp.int32)
            large = jnp.minimum(large, nb - 1)
            return np.asarray(ret + jnp.where(n < max_exact, n, large)).astype(np.int64)
    except Exception:
        ret = np.where(rel > 0, nb, 0)
        n = np.abs(rel)
        nf = np.maximum(n, 1).astype(np.float32)
        val = (np.log(nf / np.float32(max_exact)) / np.float32(math.log(128 / max_exact)) * np.float32(nb - max_exact)).astype(np.float32)
        large = np.minimum(max_exact + np.trunc(val).astype(np.int64), nb - 1)
        return ret + np.where(n < max_exact, n, large)


def _core_layout(x_seq, L, TO, part):
    nparts = L // TO
    NTO = TO // 128
    o0, o1 = part * TO, (part + 1) * TO
    segs = []
    before = list(range(0, part))
    after = list(range(nparts - 1, part, -1))
    for p_ in before:
        segs.append(("F", p_))
    for p_ in after:
        segs.append(("B", p_))
    rows = []
    for kind, p_ in segs:
        blk = x_seq[p_ * TO:(p_ + 1) * TO]
        if kind == "B":
            blk = blk.reshape(TO // 64, 64, -1)[::-1].reshape(TO, -1)
        rows.append(blk)
    zero = np.zeros((128, x_seq.shape[1]), np.float32)
    rows.append(x_seq[o0 - 128:o0] if o0 > 0 else zero)
    rows.append(x_seq[o1:o1 + 128] if o1 < L else zero)
    rows.append(x_seq[o0:o1])
    xj = np.ascontiguousarray(np.concatenate(rows, axis=0))
    ns = len(segs)
    dirF = [1.0 if k == "F" else 0.0 for k, _ in segs]
    cont = [1.0 if (i > 0 and segs[i][0] == segs[i - 1][0]) else 0.0 for i in range(ns)]
    endF = [1.0 if (segs[i][0] == "F" and (i == ns - 1 or segs[i + 1][0] != "F")) else 0.0 for i in range(ns)]
    endB = [1.0 if (segs[i][0] == "B" and i == ns - 1) else 0.0 for i in range(ns)]
    lastm = [NEG if (endF[i] or endB[i]) else 0.0 for i in range(ns)]
    prevm = NEG if o0 == 0 else 0.0
    nextm = NEG if o1 == L else 0.0
    return xj, dict(dirF=dirF, cont=cont, endF=endF, endB=endB, lastm=lastm, prevm=prevm, nextm=nextm)


def _prepare(inputs, TO):
    xp = np.asarray(inputs["x_prompt"], np.float32)
    xs = np.asarray(inputs["x_sample"], np.float32)
    LP, LS = xp.shape[1], xs.shape[1]
    rel = np.arange(128)[:, None] - np.arange(384)[None, :] + 128
    idxm = _t5_bucket_np(rel).astype(np.float32)
    common = {
        "w_in": np.ascontiguousarray(np.asarray(inputs["w_in"], np.float32)[0]),
        "w_out": np.ascontiguousarray(np.asarray(inputs["w_out"], np.float32)[0]),
        "w_mlp_in": np.ascontiguousarray(np.asarray(inputs["w_mlp_in"], np.float32)[0]),
        "w_mlp_out": np.ascontiguousarray(np.asarray(inputs["w_mlp_out"], np.float32)[0]),
        "attn_norm_w": np.asarray(inputs["attn_norm_w"], np.float32).reshape(1, D),
        "mlp_norm_w": np.asarray(inputs["mlp_norm_w"], np.float32).reshape(1, D),
        "qk_norm_w": np.asarray(inputs["qk_norm_w"], np.float32).reshape(2, 64),
        "diff_lambda": np.ascontiguousarray(np.asarray(inputs["diff_lambda"], np.float32).reshape(4, 64)),
        "diff_subln_w": np.asarray(inputs["diff_subln_w"], np.float32).reshape(1, 128),
        "rel_bias": np.ascontiguousarray(np.asarray(inputs["rel_bias"], np.float32).reshape(32, 4)),
        "hgrn_lb": np.ascontiguousarray(np.asarray(inputs["hgrn_lb"], np.float32).reshape(2, 2, 512)),
        "hgrn_norm_w": np.asarray(inputs["hgrn_norm_w"], np.float32).reshape(1, 128),
        "idxm": idxm,
    }
    in_maps = []
    for c in range(8):
        pP, pS = LP // TO, LS // TO
        xjP, fP = _core_layout(xp[c // pP], LP, TO, c % pP)
        xjS, fS = _core_layout(xs[c // pS], LS, TO, c % pS)
        fl = np.zeros((1, NFLAG), np.float32)
        for nm, base in (("dirF", FL_DIRF), ("cont", FL_CONT), ("endF", FL_ENDF), ("endB", FL_ENDB), ("lastm", FL_LASTM)):
            fl[0, base] = fP[nm][0]
            fl[0, base + 1:base + 4] = fS[nm]
        fl[0, FL_PREVM], fl[0, FL_PREVM + 1] = fP["prevm"], fS["prevm"]
        fl[0, FL_NEXTM], fl[0, FL_NEXTM + 1] = fP["nextm"], fS["nextm"]
        m = dict(common)
        m["x_P"], m["x_S"], m["flags"] = xjP, xjS, fl
        in_maps.append(m)
    return in_maps, LP, LS


_CACHE = {}


def run(inputs, TO=4096, dbg=False, serial=False):
    in_maps, LP, LS = _prepare(inputs, TO)
    assert LP == 2 * TO and LS == 4 * TO
    key = (TO, dbg, serial)
    if key not in _CACHE:
        cfg_ = Cfg(TO, dbg)
        cfg_.serial = serial
        _CACHE[key] = build(cfg_)
    nc = _CACHE[key]
    res = run_bass_kernel_spmd(nc, in_maps, core_ids=list(range(8)))
    B, DB = inputs["x_prompt"].shape[0], inputs["x_sample"].shape[0]
    yp = np.zeros((B, LP, D), np.float32)
    ys = np.zeros((DB, LS, D), np.float32)
    for c in range(8):
        r = res.results[c]
        yp[c // 2, (c % 2) * TO:(c % 2 + 1) * TO] = r["y_P"]
        ys[c // 4, (c % 4) * TO:(c % 4 + 1) * TO] = r["y_S"]
    return (yp, ys), res


def kernel(**inputs):
    out, _ = run(inputs, TO=4096)
    return out
```

```python
import bisect
import math
from contextlib import ExitStack

import numpy as np
import concourse.bass as bass
import concourse.mybir as mybir
from concourse.bass_utils import run_bass_kernel_spmd

F32 = mybir.dt.float32
BF16 = mybir.dt.bfloat16
AF = mybir.ActivationFunctionType
ALU = mybir.AluOpType
AX = mybir.AxisListType

D = 1024
NH = 4
PROJ = 4096
DFF = 4096
EPS = 1e-6
NEG = -30000.0
C_QA, C_KA, C_VA, C_QH, C_FF, C_FB, C_IH, C_GH = [512 * i for i in range(8)]
LAM_INIT = 0.8 - 0.6 * math.exp(-0.3 * 0)
NFLAG = 24
FL_DIRF, FL_CONT, FL_ENDF, FL_ENDB, FL_LASTM, FL_PREVM, FL_NEXTM = 0, 4, 8, 12, 16, 20, 22


class Eng:
    def __init__(self, K, name, eng):
        self.name, self.eng = name, eng
        self.sem = K.nc.alloc_semaphore("e_" + name)
        self.ninstr = 0
        self.sigpos = []
        self.waited = {}


class Buf:
    def __init__(self, name, t=None, dram=False):
        self.name, self.t, self.dram = name, t, dram
        self.w = {}
        self.r = {}
        self.dsem = None
        self.dcount = 0
        self.dwrites = {}

    def __getitem__(self, k):
        return self.t[k]


class Kern:
    def __init__(self, nc):
        self.nc = nc
        self.PE = Eng(self, "pe", nc.tensor)
        self.ACT = Eng(self, "act", nc.scalar)
        self.DVE = Eng(self, "dve", nc.vector)
        self.POOL = Eng(self, "pool", nc.gpsimd)
        self.SP = Eng(self, "sp", nc.sync)
        self.engs = [self.PE, self.ACT, self.DVE, self.POOL, self.SP]
        self.nsem = 5
        self.dma_bufs = []

    def _resolve(self, ev):
        if ev[0] == "d":
            return ev[1], ev[2]
        _, E, pos = ev
        i = bisect.bisect_left(E.sigpos, pos)
        assert i < len(E.sigpos), f"no covering signal on {E.name}"
        return E.sem, i + 1

    def _wait(self, E, deps, raw=()):
        need = {}
        rawset = set(id(e) for e in raw)
        for ev in list(deps) + list(raw):
            if ev is None:
                continue
            if ev[0] == "c" and ev[1] is E and E is not self.POOL and (E is self.PE or id(ev) not in rawset):
                continue
            sem, val = self._resolve(ev[:3])
            k = id(sem)
            if k not in need or need[k][1] < val:
                need[k] = (sem, val)
        for k, (sem, val) in need.items():
            if E.waited.get(k, 0) < val:
                E.eng.wait_ge(sem, val)
                E.waited[k] = val

    @staticmethod
    def _deps(r, w):
        raw, other = [], []
        for b in r:
            if b.dram:
                raw += list(b.dwrites.values())
            else:
                raw += list(b.w.values())
        for b in w:
            if b.dram:
                continue
            other += list(b.w.values())
            other += list(b.r.values())
        return raw, other

    def op(self, E, fn, r=(), w=(), sig=True, small=False):
        raw, other = self._deps(r, w)
        self._wait(E, other, raw)
        ins = fn()
        pos = E.ninstr
        E.ninstr += 1
        if sig:
            ins.then_inc(E.sem, 1)
            E.sigpos.append(pos)
        ev = ("c", E, pos, small)
        for b in r:
            if not b.dram:
                b.r[E.name] = ev
        for b in w:
            b.w[E.name] = ev
            b.r = {}
        return ins

    def dma(self, Q, out, in_, r, w, sb):
        raw, other = self._deps(r, w)
        self._wait(Q, other, raw)
        if sb.dsem is None:
            sb.dsem = self.nc.alloc_semaphore("d_" + sb.name)
            self.nsem += 1
            self.dma_bufs.append(sb)
        Q.eng.dma_start(out=out, in_=in_).then_inc(sb.dsem, 16)
        sb.dcount += 16
        ev = ("d", sb.dsem, sb.dcount)
        for b in r:
            if not b.dram:
                b.r["dma" + str(id(sb.dsem))] = ev
        for b in w:
            if b.dram:
                b.dwrites[id(sb.dsem)] = ev
            else:
                b.w["dma" + str(id(sb.dsem))] = ev
                b.r = {}

    def barrier(self):
        for E in self.engs:
            self.finish(E)
        for E in self.engs:
            for E2 in self.engs:
                if E2 is E or not E2.sigpos:
                    continue
                assert E2.sigpos[-1] == E2.ninstr - 1, f"last instr on {E2.name} not signalled"
                val = len(E2.sigpos)
                if E.waited.get(id(E2.sem), 0) < val:
                    E.eng.wait_ge(E2.sem, val)
                    E.waited[id(E2.sem)] = val

    def finish(self, Q):
        for sb in self.dma_bufs:
            k = id(sb.dsem)
            if sb.dcount and Q.waited.get(k, 0) < sb.dcount:
                Q.eng.wait_ge(sb.dsem, sb.dcount)
                Q.waited[k] = sb.dcount


class Cfg:
    def __init__(self, TO=4096, dbg=False):
        self.TO = TO
        self.NTO = TO // 128
        self.jobs = [dict(name="P", nseg=1, seg0=0, j=0), dict(name="S", nseg=3, seg0=1, j=1)]
        for J in self.jobs:
            J["NF"] = J["nseg"] * self.NTO
            J["NK"] = J["NF"] + 2 + self.NTO
        self.dbg = dbg
        self.serial = False


def build(cfg):
    nc = bass.Bass("TRN2", target_bir_lowering=False)
    K = Kern(nc)
    PE, ACT, DVE, POOL, SP = K.PE, K.ACT, K.DVE, K.POOL, K.SP
    NTO, TO = cfg.NTO, cfg.TO
    ctr = [0]

    def sb(name, shape, dt, stack=None):
        ctr[0] += 1
        nm = f"{name}_{ctr[0]}"
        if stack is None:
            t = nc.alloc_sbuf_tensor(nm, list(shape), dt)
        else:
            t = stack.enter_context(nc.sbuf_tensor(nm, list(shape), dt))
        return Buf(nm, t)

    def ps(name, shape, dt, stack):
        ctr[0] += 1
        nm = f"{name}_{ctr[0]}"
        t = stack.enter_context(nc.psum_tensor(nm, list(shape), dt))
        return Buf(nm, t)

    def dram_in(name, shape, dt=F32):
        return Buf(name, nc.dram_tensor(name, list(shape), dt, kind="ExternalInput").ap(), dram=True)

    def dram_out(name, shape, dt=F32):
        return Buf(name, nc.dram_tensor(name, list(shape), dt, kind="ExternalOutput").ap(), dram=True)

    def dram_scr(name, shape, dt):
        kind = "ExternalOutput" if cfg.dbg else "Internal"
        return Buf(name, nc.dram_tensor(name, list(shape), dt, kind=kind).ap(), dram=True)

    for J in cfg.jobs:
        n = J["name"]
        J["x"] = dram_in("x_" + n, [J["NK"] * 128, D])
        J["y"] = dram_out("y_" + n, [TO, D])
        J["QT"] = dram_scr("QT_" + n, [NH, 128, TO], BF16)
        J["KT"] = dram_scr("KT_" + n, [NH, 128, J["NK"] * 128], BF16)
        J["V"] = dram_scr("V_" + n, [NH, 128, J["NK"], 129], BF16)
        J["HG"] = dram_scr("HG_" + n, [NTO, 128, 7, 512], F32)
        J["HGo"] = dram_scr("HGo_" + n, [J["NF"], 128, 3, 512], F32)
        J["OF"] = dram_scr("OF_" + n, [NTO, 128, 512], F32)
        J["OB"] = dram_scr("OB_" + n, [NTO, 128, 512], F32)
        J["O"] = dram_scr("O_" + n, [NTO, 128, D], BF16)
        if cfg.dbg:
            J["dSF"] = dram_scr("dSF_" + n, [128, 512], F32)
            J["dSB"] = dram_scr("dSB_" + n, [128, 512], F32)
        J["X1"] = dram_scr("X1_" + n, [NTO, 128, D], F32)
        J["H2T"] = dram_scr("H2T_" + n, [8, 128, TO], BF16)
    w_in_d = dram_in("w_in", [D, PROJ])
    w_out_d = dram_in("w_out", [D, D])
    w_mi_d = dram_in("w_mlp_in", [D, DFF])
    w_mo_d = dram_in("w_mlp_out", [DFF, D])
    attn_nw_d = dram_in("attn_norm_w", [1, D])
    mlp_nw_d = dram_in("mlp_norm_w", [1, D])
    qk_nw_d = dram_in("qk_norm_w", [2, 64])
    dlam_d = dram_in("diff_lambda", [4, 64])
    subln_d = dram_in("diff_subln_w", [1, 128])
    relb_d = dram_in("rel_bias", [32, 4])
    hlb_d = dram_in("hgrn_lb", [2, 2, 512])
    hnw_d = dram_in("hgrn_norm_w", [1, 128])
    flags_d = dram_in("flags", [1, NFLAG])
    idxm_d = dram_in("idxm", [128, 384])

    ident_f = sb("identf", [128, 128], F32)
    ident = sb("ident", [128, 128], BF16)
    blk1_f = sb("blk1f", [128, 128], F32)
    blk1 = sb("blk1", [128, 128], BF16)
    tri = {k: sb("tri" + k, [128, 128], F32) for k in ("fi", "fx", "bi", "bx")}
    ones_c = sb("ones", [128, 1], F32)
    eps_c = sb("eps", [128, 1], F32)
    eps64_c = sb("eps64", [128, 1], F32)
    flg = sb("flg", [128, NFLAG], F32)
    nflg = sb("nflg", [128, NFLAG], F32)
    relb = sb("relb", [128, 128], F32)
    kscale = sb("kscale", [128, 1], F32)
    hbias = sb("hbias", [128, 16], F32)
    lam_t = sb("lam", [128, 4], F32)
    subln = sb("subln", [128, 128], F32)
    hnw = sb("hnw", [128, 512], F32)
    oml = [sb(f"oml{d}", [128, 512], F32) for d in range(2)]
    bfar = sb("bfar", [128, 32], F32)
    anw = sb("anw", [128, 8], F32)
    mnw = sb("mnw", [128, 8], F32)

    st_setup = ExitStack()
    dl = sb("dl", [128, 256], F32, st_setup)
    lbt = sb("lbt", [128, 2048], F32, st_setup)
    qkb = sb("qkb", [128, 128], F32, st_setup)
    wprod = sb("wprod", [128, 128], F32, st_setup)
    wbc = [sb("wbc", [128, D], F32, st_setup) for _ in range(2)]
    junk_s = sb("junk_s", [128, 64], F32, st_setup)

    def pool_op(fn, w, r=()):
        K.op(POOL, fn, r=r, w=w)

    g = nc.gpsimd
    pool_op(lambda: g.memset(ident_f[:], 1.0), [ident_f])
    pool_op(lambda: g.affine_select(out=ident_f[:], in_=ident_f[:], pattern=[[-1, 128]], compare_op=ALU.is_equal,
                                    fill=0.0, base=0, channel_multiplier=1), [ident_f], [ident_f])
    pool_op(lambda: g.memset(blk1_f[:], 0.0), [blk1_f])
    pool_op(lambda: g.memset(blk1_f[0:64, 0:64], 1.0), [blk1_f])
    pool_op(lambda: g.memset(blk1_f[64:128, 64:128], 1.0), [blk1_f])
    for k, (cm, pat, base, cmp) in dict(fi=(-1, 1, 0, ALU.is_ge), fx=(1, -1, 0, ALU.is_gt),
                                         bi=(1, -1, 0, ALU.is_ge), bx=(-1, 1, 0, ALU.is_gt)).items():
        t = tri[k]
        pool_op(lambda t=t: g.memset(t[:], 0.0), [t])
        pool_op(lambda t=t: g.memset(t[0:64, 0:64], 1.0), [t])
        pool_op(lambda t=t: g.memset(t[64:128, 64:128], 1.0), [t])
        pool_op(lambda t=t, cm=cm, pat=pat, cmp=cmp: g.affine_select(
            out=t[:], in_=t[:], pattern=[[pat, 128]], compare_op=cmp, fill=0.0, base=0, channel_multiplier=cm), [t], [t])
    pool_op(lambda: g.memset(ones_c[:], 1.0), [ones_c])
    pool_op(lambda: g.memset(eps_c[:], EPS), [eps_c])
    pool_op(lambda: g.memset(eps64_c[:], 64.0 * EPS), [eps64_c])

    def bload(dst, src_ap, n):
        K.dma(SP, dst[:, 0:n], src_ap.to_broadcast([128, n]), r=[], w=[dst], sb=dst)

    bload(flg, flags_d[0:1, :], NFLAG)
    bload(relb, relb_d.t.rearrange("(o b) h -> o (b h)", o=1), 128)
    bload(dl, dlam_d.t.rearrange("(o b) h -> o (b h)", o=1), 256)
    bload(subln, subln_d[0:1, :], 128)
    bload(lbt, hlb_d.t.rearrange("(o a) b c -> o (a b c)", o=1), 2048)
    for h in range(4):
        K.dma(SP, hnw[:, h * 128:(h + 1) * 128], hnw_d[0:1, :].to_broadcast([128, 128]), r=[], w=[hnw], sb=hnw)
    bload(qkb, qk_nw_d.t.rearrange("(o b) h -> o (b h)", o=1), 128)
    bload(wbc[0], attn_nw_d[0:1, :], D)
    bload(wbc[1], mlp_nw_d[0:1, :], D)

    v = nc.vector
    a = nc.scalar
    K.op(DVE, lambda: v.tensor_copy(out=ident[:], in_=ident_f[:]), r=[ident_f], w=[ident])
    K.op(DVE, lambda: v.tensor_copy(out=blk1[:], in_=blk1_f[:]), r=[blk1_f], w=[blk1])
    K.op(DVE, lambda: v.tensor_scalar(out=nflg[:], in0=flg[:], scalar1=-1.0, scalar2=1.0, op0=ALU.mult, op1=ALU.add),
         r=[flg], w=[nflg], small=True)
    for half in range(2):
        K.op(DVE, lambda half=half: v.scalar_tensor_tensor(out=wprod[:, half * 64:(half + 1) * 64], in0=qkb[:, 0:64], scalar=8.0,
                                                           in1=qkb[:, 64:128], op0=ALU.mult, op1=ALU.mult),
             r=[qkb], w=[wprod], small=True)
    K.op(DVE, lambda: v.tensor_tensor(out=wprod[:, :], in0=wprod[:, :], in1=ident_f[:, :], op=ALU.mult),
         r=[wprod, ident_f], w=[wprod], small=True)
    K.op(DVE, lambda: v.tensor_reduce(out=kscale[:, 0:1], in_=wprod[:, :], axis=AX.X, op=ALU.add), r=[wprod], w=[kscale], small=True)
    with ExitStack() as st0:
        wtmp0 = sb("wtmp0", [128, 8, 128], F32, st0)
        for wi, dst in ((0, anw), (1, mnw)):
            K.op(DVE, lambda wi=wi: v.tensor_tensor(out=wtmp0[:, :, :], in0=wbc[wi][:, :].rearrange("p (c j) -> p c j", c=8),
                                                    in1=ident_f[:, :].unsqueeze(1).to_broadcast([128, 8, 128]), op=ALU.mult),
                 r=[wbc[wi], ident_f], w=[wtmp0])
            K.op(DVE, lambda dst=dst: v.tensor_reduce(out=dst[:, :], in_=wtmp0[:, :, :], axis=AX.X, op=ALU.add),
                 r=[wtmp0], w=[dst], small=True)
    for jj_ in range(2):
        for which, (bk, fm_) in enumerate(((15, FL_PREVM), (31, FL_NEXTM))):
            c0_ = (jj_ * 2 + which) * 4
            K.op(DVE, lambda c0_=c0_, bk=bk, fm_=fm_, jj_=jj_: v.tensor_scalar(
                out=hbias[:, c0_:c0_ + 4], in0=relb[:, bk * 4:bk * 4 + 4], scalar1=flg[:, fm_ + jj_:fm_ + jj_ + 1], scalar2=None,
                op0=ALU.add), r=[relb, flg], w=[hbias], small=True)
    for i in range(2):
        K.op(DVE, lambda i=i: v.tensor_tensor(out=junk_s[:], in0=dl[:, 128 * i:128 * i + 64],
                                              in1=dl[:, 128 * i + 64:128 * i + 128], op=ALU.mult),
             r=[dl], w=[junk_s], small=True)
        K.op(DVE, lambda i=i: v.tensor_reduce(out=lam_t[:, i:i + 1], in_=junk_s[:], axis=AX.X, op=ALU.add),
             r=[junk_s], w=[lam_t], small=True)
    K.op(ACT, lambda: a.activation(out=lam_t[:, 2:4], in_=lam_t[:, 0:2], func=AF.Exp), r=[lam_t], w=[lam_t], small=True)
    K.op(DVE, lambda: v.tensor_tensor(out=lam_t[:, 0:1], in0=lam_t[:, 3:4], in1=lam_t[:, 2:3], op=ALU.subtract),
         r=[lam_t], w=[lam_t], small=True)
    K.op(DVE, lambda: v.tensor_scalar(out=lam_t[:, 0:1], in0=lam_t[:, 0:1], scalar1=-LAM_INIT, scalar2=None, op0=ALU.add),
         r=[lam_t], w=[lam_t], small=True)
    for d_ in range(2):
        K.op(DVE, lambda d_=d_: v.tensor_tensor(out=oml[d_][:], in0=lbt[:, d_ * 1024:d_ * 1024 + 512],
                                                in1=lbt[:, d_ * 1024 + 512:d_ * 1024 + 1024], op=ALU.subtract),
             r=[lbt], w=[oml[d_]])
        K.op(ACT, lambda d_=d_: a.activation(out=oml[d_][:], in_=oml[d_][:], func=AF.Exp), r=[oml[d_]], w=[oml[d_]], small=True)
        K.op(DVE, lambda d_=d_: v.tensor_scalar(out=oml[d_][:], in0=oml[d_][:], scalar1=1.0, scalar2=None, op0=ALU.add),
             r=[oml[d_]], w=[oml[d_]], small=True)
        K.op(DVE, lambda d_=d_: v.reciprocal(out=oml[d_][:], in_=oml[d_][:]), r=[oml[d_]], w=[oml[d_]], small=True)
    for s in range(4):
        K.op(DVE, lambda s=s: v.tensor_scalar(out=bfar[:, 4 * s:4 * s + 4], in0=relb[:, 60:64],
                                              scalar1=flg[:, FL_DIRF + s:FL_DIRF + s + 1], scalar2=None, op0=ALU.mult),
             r=[relb, flg], w=[bfar], small=True)
        K.op(DVE, lambda s=s: v.scalar_tensor_tensor(out=bfar[:, 4 * s:4 * s + 4], in0=relb[:, 124:128],
                                                     scalar=nflg[:, FL_DIRF + s:FL_DIRF + s + 1], in1=bfar[:, 4 * s:4 * s + 4],
                                                     op0=ALU.mult, op1=ALU.add), r=[relb, nflg, bfar], w=[bfar], small=True)
        K.op(DVE, lambda s=s: v.tensor_scalar(out=bfar[:, 16 + 4 * s:16 + 4 * s + 4], in0=bfar[:, 4 * s:4 * s + 4],
                                              scalar1=flg[:, FL_LASTM + s:FL_LASTM + s + 1], scalar2=None, op0=ALU.add),
             r=[bfar, flg], w=[bfar], small=True)

    K.barrier()
    st_setup.close()

    with ExitStack() as st:
        win = sb("win", [128, 8, PROJ], BF16, st)
        wsel = sb("wsel", [128, 8, 512], BF16, st)
        xt = [sb("xt", [128, D], F32, st) for _ in range(4)]
        hb = [sb("hb", [128, D], BF16, st) for _ in range(8)]
        sqj = sb("sqj", [128, D], BF16, st)
        ss = [sb("ss", [128, 8], F32, st) for _ in range(3)]
        hT = [sb("hT", [128, 8, 512], BF16, st) for _ in range(2)]
        sqb = [sb("sqb", [128, 512], BF16, st) for _ in range(2)]
        rs = [sb("rs", [128, 512], F32, st) for _ in range(2)]
        ktst = [sb("ktst", [128, NH, 512], BF16, st) for _ in range(2)]
        qtst = [sb("qtst", [128, NH, 512], BF16, st) for _ in range(2)]
        vst = [sb("vst", [128, NH, 4, 129], BF16, st) for _ in range(2)]
        hgst = [sb("hgst", [128, 512], F32, st) for _ in range(8)]
        omlselA = sb("omlselA", [128, 512], F32, st)
        sg = [sb("sg", [128, 512], F32, st) for _ in range(2)]
        tp_ps = [ps("tp", [128, 8, 128], BF16, st) for _ in range(2)]
        fm_ps = [ps("fm", [128, 512], F32, st) for _ in range(2)]
        ssb_ps = ps("ssb", [128, 512], F32, st)
        tm_ps = [ps("tm", [128, 512], F32, st) for _ in range(3)]

        for b in vst:
            K.op(POOL, lambda b=b: g.memset(b[:], 1.0), w=[b])

        w_view = w_in_d.t.rearrange("(c p) n -> p c n", p=128)
        cnt = 0
        for c in range(8):
            for q4 in range(4):
                col = q4 * 1024
                stg = xt[cnt % 3]
                K.dma(SP, stg[:, :], w_view[:, c, col:col + 1024], r=[w_in_d], w=[stg], sb=stg)
                if cnt % 2 == 0:
                    K.op(DVE, lambda c=c, col=col, stg=stg: v.tensor_scalar(
                        out=win[:, c, col:col + 1024], in0=stg[:, :], scalar1=anw[:, c:c + 1], scalar2=None,
                        op0=ALU.mult), r=[stg, anw], w=[win])
                else:
                    K.op(ACT, lambda c=c, col=col, stg=stg: a.activation(
                        out=win[:, c, col:col + 1024], in_=stg[:, :], func=AF.Copy, scale=anw[:, c:c + 1]),
                        r=[stg, anw], w=[win])
                cnt += 1

        gi = [0]
        sti = [0]
        hgi = [0]
        fmi = [0]
        tmi = [0]

        def norm_a(J, rows):
            nt = len(rows)
            i0 = gi[0]
            gi[0] += 4
            s_ = ss[(i0 // 4) % 3]
            for t, trow in enumerate(rows):
                x_ = xt[(i0 + t) % 4]
                K.dma(SP, x_[:, :], J["x"].t[trow * 128:(trow + 1) * 128, :], r=[J["x"]], w=[x_], sb=x_)
                K.op(ACT, lambda x_=x_, t=t: a.activation(out=sqj[:, :], in_=x_[:, :], func=AF.Square, accum_out=s_[:, t:t + 1]),
                     r=[x_], w=[sqj, s_])
            K.op(ACT, lambda: a.activation(out=s_[:, 4:4 + nt], in_=s_[:, 0:nt], func=AF.Ln, bias=eps_c[:], scale=1.0 / D),
                 r=[s_, eps_c], w=[s_])
            K.op(ACT, lambda: a.activation(out=s_[:, 4:4 + nt], in_=s_[:, 4:4 + nt], func=AF.Exp, scale=-0.5), r=[s_], w=[s_])
            for t in range(nt):
                x_, h_ = xt[(i0 + t) % 4], hb[(i0 + t) % 8]
                K.op(DVE, lambda x_=x_, h_=h_, t=t: v.tensor_scalar(out=h_[:, :], in0=x_[:, :], scalar1=s_[:, 4 + t:5 + t], scalar2=None,
                                                                    op0=ALU.mult), r=[x_, s_], w=[h_])
            return i0

        def norm_b(i0, nt, hslot):
            for t in range(nt):
                i = i0 + t
                h_ = hb[i % 8]
                tp = tp_ps[i % 2]
                for c in range(8):
                    K.op(PE, lambda c=c, h_=h_, tp=tp: nc.tensor.transpose(tp[:, c, :], h_[:, c * 128:(c + 1) * 128], ident[:]),
                         r=[h_, ident], w=[tp], sig=(c == 7))
                K.op(DVE, lambda tp=tp, t=t: v.tensor_copy(out=hT[hslot][:, :, t * 128:(t + 1) * 128], in_=tp[:, :, :]),
                     r=[tp], w=[hT[hslot]])

        def fm_block(hslot, N, col, is_k, dst, h):
            i = fmi[0]
            fmi[0] += 1
            p_ = fm_ps[i % 2]
            for c in range(8):
                K.op(PE, lambda c=c: nc.tensor.matmul(p_[:, 0:N], lhsT=win[:, c, col:col + 128], rhs=hT[hslot][:, c, 0:N],
                                                      start=(c == 0), stop=(c == 7)),
                     r=[win, hT[hslot]], w=[p_], sig=(c == 7))
            q_, r_ = sqb[i % 2], rs[i % 2]
            K.op(ACT, lambda: a.activation(out=q_[:, 0:N], in_=p_[:, 0:N], func=AF.Square), r=[p_], w=[q_])
            K.op(PE, lambda: nc.tensor.matmul(ssb_ps[:, 0:N], lhsT=blk1[:], rhs=q_[:, 0:N], start=True, stop=True),
                 r=[blk1, q_], w=[ssb_ps])
            K.op(ACT, lambda: a.activation(out=r_[:, 0:N], in_=ssb_ps[:, 0:N], func=AF.Ln, bias=eps64_c[:], scale=1.0),
                 r=[ssb_ps, eps64_c], w=[r_])
            K.op(ACT, lambda: a.activation(out=r_[:, 0:N], in_=r_[:, 0:N], func=AF.Exp, scale=-0.5), r=[r_], w=[r_],
                 small=(N < 256))
            if is_k:
                K.op(DVE, lambda: v.scalar_tensor_tensor(out=dst[:, h, 0:N], in0=p_[:, 0:N], scalar=kscale[:, 0:1],
                                                         in1=r_[:, 0:N], op0=ALU.mult, op1=ALU.mult),
                     r=[p_, kscale, r_], w=[dst])
            else:
                K.op(DVE, lambda: v.tensor_tensor(out=dst[:, h, 0:N], in0=p_[:, 0:N], in1=r_[:, 0:N], op=ALU.mult),
                     r=[p_, r_], w=[dst])

        def tm_matmul(hslot, t, wbuf, col):
            i = tmi[0]
            tmi[0] += 1
            p_ = tm_ps[i % 3]
            for c in range(8):
                K.op(PE, lambda c=c: nc.tensor.matmul(p_[:, :], lhsT=hT[hslot][:, c, t * 128:(t + 1) * 128],
                                                      rhs=wbuf[:, c, col:col + 512], start=(c == 0), stop=(c == 7)),
                     r=[hT[hslot], wbuf], w=[p_], sig=(c == 7))
            return p_

        def hg_store(src_ap_fn, eng, dst_ap, dstbuf, extra_r=()):
            i = hgi[0]
            hgi[0] += 1
            s_ = hgst[i % 8]
            src_ap_fn(s_)
            K.dma(POOL, dst_ap, s_[:, :], r=[s_], w=[dstbuf], sb=s_)

        def silu_block(p_, scale, dst_ap, dstbuf):
            i = hgi[0]
            s1 = sg[i % 2]
            K.op(ACT, lambda: a.activation(out=s1[:, :], in_=p_[:, :], func=AF.Exp, scale=-1.0), r=[p_], w=[s1])
            K.op(ACT, lambda: a.activation(out=s1[:, :], in_=s1[:, :], func=AF.Ln, bias=ones_c[:], scale=1.0),
                 r=[s1, ones_c], w=[s1])
            K.op(ACT, lambda: a.activation(out=s1[:, :], in_=s1[:, :], func=AF.Exp, scale=-1.0), r=[s1], w=[s1])

            def ev(s_):
                K.op(DVE, lambda: v.scalar_tensor_tensor(out=s_[:, :], in0=p_[:, :], scalar=scale, in1=s1[:, :],
                                                         op0=ALU.mult, op1=ALU.mult), r=[p_, s1], w=[s_])
            hg_store(ev, DVE, dst_ap, dstbuf)

        def gate_block(p_, omlb, dst_k, dst_lf, dstbuf):
            s1 = sg[hgi[0] % 2]
            K.op(ACT, lambda: a.activation(out=s1[:, :], in_=p_[:, :], func=AF.Exp), r=[p_], w=[s1])
            K.op(ACT, lambda: a.activation(out=s1[:, :], in_=s1[:, :], func=AF.Ln, bias=ones_c[:], scale=1.0),
                 r=[s1, ones_c], w=[s1])
            K.op(ACT, lambda: a.activation(out=s1[:, :], in_=s1[:, :], func=AF.Exp, scale=-1.0), r=[s1], w=[s1])
            ks = hgst[hgi[0] % 8]
            hgi[0] += 1
            K.op(DVE, lambda: v.tensor_tensor(out=ks[:, :], in0=s1[:, :], in1=omlb[:, :], op=ALU.mult), r=[s1, omlb], w=[ks])
            K.dma(POOL, dst_k, ks[:, :], r=[ks], w=[dstbuf], sb=ks)
            ls = hgst[hgi[0] % 8]
            hgi[0] += 1
            K.op(ACT, lambda: a.activation(out=ls[:, :], in_=ks[:, :], func=AF.Ln, bias=ones_c[:], scale=-1.0),
                 r=[ks, ones_c], w=[ls])
            K.dma(POOL, dst_lf, ls[:, :], r=[ls], w=[dstbuf], sb=ls)

        def copy_block(p_, eng, dst_ap, dstbuf):
            def ev(s_):
                if eng is DVE:
                    K.op(DVE, lambda: v.tensor_copy(out=s_[:, :], in_=p_[:, :]), r=[p_], w=[s_])
                else:
                    K.op(ACT, lambda: a.copy(out=s_[:, :], in_=p_[:, :]), r=[p_], w=[s_])
            hg_store(ev, eng, dst_ap, dstbuf)

        def super_tile(J, kind, rows, kt0, own0=None, seg=None, far0=None, i0=None, nxt=None):
            si = sti[0]
            sti[0] += 1
            hslot = si % 2
            nt = len(rows)
            N = nt * 128
            norm_b(i0, nt, hslot)
            nxt_i0 = norm_a(*nxt) if nxt is not None else None
            kst, qst, vs_ = ktst[si % 2], qtst[si % 2], vst[si % 2]
            for h in range(NH):
                fm_block(hslot, N, C_KA + h * 128, True, kst, h)
            for h in range(NH):
                K.dma(POOL, J["KT"].t[h, :, kt0 * 128:kt0 * 128 + N], kst[:, h, 0:N], r=[kst], w=[J["KT"]], sb=kst)
            if kind == "own":
                for h in range(NH):
                    fm_block(hslot, N, C_QA + h * 128, False, qst, h)
                for h in range(NH):
                    K.dma(POOL, J["QT"].t[h, :, own0 * 128:own0 * 128 + N], qst[:, h, 0:N], r=[qst], w=[J["QT"]], sb=qst)
            for t in range(nt):
                p_ = tm_matmul(hslot, t, win, C_VA)
                K.op(DVE, lambda p_=p_, t=t: v.tensor_copy(out=vs_[:, :, t, 0:128], in_=p_[:, :].rearrange("p (h d) -> p h d", h=NH)),
                     r=[p_], w=[vs_])
                if kind == "own":
                    hgd = J["HG"]
                    ti = own0 + t
                    p_ = tm_matmul(hslot, t, win, C_QH)
                    silu_block(p_, 128.0 ** -0.5, hgd.t[ti, :, 0, :], hgd)
                    p_ = tm_matmul(hslot, t, win, C_FF)
                    gate_block(p_, oml[0], hgd.t[ti, :, 1, :], hgd.t[ti, :, 2, :], hgd)
                    p_ = tm_matmul(hslot, t, win, C_FB)
                    gate_block(p_, oml[1], hgd.t[ti, :, 3, :], hgd.t[ti, :, 4, :], hgd)
                    p_ = tm_matmul(hslot, t, win, C_IH)
                    copy_block(p_, DVE, hgd.t[ti, :, 5, :], hgd)
                    p_ = tm_matmul(hslot, t, win, C_GH)
                    silu_block(p_, 1.0, hgd.t[ti, :, 6, :], hgd)
                elif kind == "far":
                    hgd = J["HGo"]
                    ti = far0 + t
                    p_ = tm_matmul(hslot, t, wsel, 0)
                    gate_block(p_, omlselA, hgd.t[ti, :, 0, :], hgd.t[ti, :, 1, :], hgd)
                    p_ = tm_matmul(hslot, t, win, C_IH)
                    copy_block(p_, DVE, hgd.t[ti, :, 2, :], hgd)
            for h in range(NH):
                K.dma(POOL, J["V"].t[h, :, kt0:kt0 + nt, :], vs_[:, h, 0:nt, :], r=[vs_], w=[J["V"]], sb=vs_)
            return nxt_i0

        stl = []
        for J in cfg.jobs:
            NF = J["NF"]
            for s_ in range(J["nseg"]):
                for t0 in range(0, NTO, 4):
                    f0 = s_ * NTO + t0
                    stl.append(dict(J=J, kind="far", rows=list(range(f0, f0 + 4)), kt0=f0, own0=None, far0=f0, seg=s_,
                                    first=(t0 == 0)))
            stl.append(dict(J=J, kind="halo", rows=[NF, NF + 1], kt0=NF, own0=None, far0=None, seg=None, first=False))
            for t0 in range(0, NTO, 4):
                stl.append(dict(J=J, kind="own", rows=list(range(NF + 2 + t0, NF + 2 + t0 + 4)), kt0=NF + 2 + t0, own0=t0,
                                far0=None, seg=None, first=False))
        i0 = norm_a(stl[0]["J"], stl[0]["rows"])
        for k_, T in enumerate(stl):
            if T["first"]:
                fs = FL_DIRF + T["J"]["seg0"] + T["seg"]
                K.op(DVE, lambda fs=fs: v.tensor_scalar(out=wsel[:, :, :], in0=win[:, :, C_FF:C_FF + 512],
                                                        scalar1=flg[:, fs:fs + 1], scalar2=None, op0=ALU.mult),
                     r=[win, flg], w=[wsel])
                K.op(DVE, lambda fs=fs: v.scalar_tensor_tensor(out=wsel[:, :, :], in0=win[:, :, C_FB:C_FB + 512],
                                                               scalar=nflg[:, fs:fs + 1], in1=wsel[:, :, :],
                                                               op0=ALU.mult, op1=ALU.add), r=[win, nflg, wsel], w=[wsel])
                K.op(DVE, lambda fs=fs: v.tensor_scalar(out=omlselA[:, :], in0=oml[0][:, :], scalar1=flg[:, fs:fs + 1],
                                                        scalar2=None, op0=ALU.mult), r=[oml[0], flg], w=[omlselA])
                K.op(DVE, lambda fs=fs: v.scalar_tensor_tensor(out=omlselA[:, :], in0=oml[1][:, :], scalar=nflg[:, fs:fs + 1],
                                                               in1=omlselA[:, :], op0=ALU.mult, op1=ALU.add),
                     r=[oml[1], nflg, omlselA], w=[omlselA])
            nxt = (stl[k_ + 1]["J"], stl[k_ + 1]["rows"]) if k_ + 1 < len(stl) else None
            i0 = super_tile(T["J"], T["kind"], T["rows"], T["kt0"], own0=T["own0"], seg=T["seg"], far0=T["far0"], i0=i0, nxt=nxt)

    K.barrier()

    with ExitStack() as st:
        def make_bufs():
            NB = 2
            S = sb("S", [128, 512], F32, st)
            Sbf = [sb("Sbf", [128, 512], BF16, st) for _ in range(2)]
            SF = sb("SF", [128, 512], F32, st)
            SB = sb("SB", [128, 512], F32, st)
            omlsel = sb("omlsel", [128, 512], F32, st)
            trixsel = sb("trixsel", [128, 128], F32, st)
            vin = [sb("vin", [128, 512], F32, st) for _ in range(NB)]
            qin = [sb("qin", [128, 512], F32, st) for _ in range(NB)]
            kf = [sb("kf", [128, 512], F32, st) for _ in range(NB)]
            lf = [sb("lf", [128, 512], F32, st) for _ in range(NB)]
            eb = [sb("eb", [128, 512], F32, st) for _ in range(NB)]
            enb = [sb("enb", [128, 512], F32, st) for _ in range(NB)]
            ebl = [sb("ebl", [128, 512], F32, st) for _ in range(NB)]
            qt = [sb("qt", [128, 512], BF16, st) for _ in range(NB)]
            kt_ = [sb("ktt", [128, 512], BF16, st) for _ in range(NB)]
            kk = [sb("kk", [128, 512], BF16, st) for _ in range(NB)]
            vbf = [sb("vbf", [128, 512], BF16, st) for _ in range(NB)]
            dcy = [sb("dcy", [128, 8], F32, st) for _ in range(NB)]
            qkT = [sb("qkT", [128, 8, 128], BF16, st) for _ in range(NB)]
            atsb = [sb("atsb", [128, NH, 128], BF16, st) for _ in range(NB)]
            ost = [sb("ost", [128, 512], F32, st) for _ in range(NB)]
            pp_ps = ps("ppps", [128, 512], F32, st)
            tpb_ps = ps("tpb", [128, 8, 128], BF16, st)
            ao_ps = ps("aops", [128, 512], F32, st)
            ds_ps = ps("dsps", [128, 512], F32, st)
            bi = [0]
            sbi = [0]

            def hg_prep(J, own, src, tix, trii, trix, omlb, d_idx):
                i = bi[0] % NB
                bi[0] += 1
                v_, q_ = vin[i], qin[i]
                k_, l_ = kf[i], lf[i]
                if own:
                    K.dma(SP, k_[:, :], src.t[tix, :, 1 + 2 * d_idx, :], r=[src], w=[k_], sb=k_)
                    K.dma(SP, l_[:, :], src.t[tix, :, 2 + 2 * d_idx, :], r=[src], w=[l_], sb=l_)
                    K.dma(SP, v_[:, :], src.t[tix, :, 5, :], r=[src], w=[v_], sb=v_)
                    K.dma(SP, q_[:, :], src.t[tix, :, 0, :], r=[src], w=[q_], sb=q_)
                else:
                    K.dma(SP, k_[:, :], src.t[tix, :, 0, :], r=[src], w=[k_], sb=k_)
                    K.dma(SP, l_[:, :], src.t[tix, :, 1, :], r=[src], w=[l_], sb=l_)
                    K.dma(SP, v_[:, :], src.t[tix, :, 2, :], r=[src], w=[v_], sb=v_)
                yield
                K.op(DVE, lambda: v.tensor_copy(out=vbf[i][:, :], in_=v_[:, :]), r=[v_], w=[vbf[i]])
                K.op(PE, lambda: nc.tensor.matmul(pp_ps[:, :], lhsT=trix[:], rhs=l_[:, :], start=True, stop=True),
                     r=[trix, l_], w=[pp_ps])
                yield
                K.op(ACT, lambda: a.activation(out=ebl[i][:, :], in_=pp_ps[:, :], func=AF.Exp), r=[pp_ps], w=[ebl[i]])
                yield
                K.op(DVE, lambda: v.tensor_tensor(out=kk[i][:, :], in0=k_[:, :], in1=ebl[i][:, :], op=ALU.mult),
                     r=[k_, ebl[i]], w=[kk[i]])
                if own:
                    K.op(PE, lambda: nc.tensor.matmul(pp_ps[:, :], lhsT=trii[:], rhs=l_[:, :], start=True, stop=True),
                         r=[trii, l_], w=[pp_ps])
                    yield
                    K.op(ACT, lambda: a.activation(out=eb[i][:, :], in_=pp_ps[:, :], func=AF.Exp), r=[pp_ps], w=[eb[i]])
                    K.op(ACT, lambda: a.activation(out=enb[i][:, :], in_=pp_ps[:, :], func=AF.Exp, scale=-1.0),
                         r=[pp_ps], w=[enb[i]])
                    yield
                    K.op(DVE, lambda: v.tensor_tensor(out=qt[i][:, :], in0=q_[:, :], in1=eb[i][:, :], op=ALU.mult),
                         r=[q_, eb[i]], w=[qt[i]])
                    K.op(DVE, lambda: v.tensor_tensor(out=kt_[i][:, :], in0=k_[:, :], in1=enb[i][:, :], op=ALU.mult),
                         r=[k_, enb[i]], w=[kt_[i]])
                    yield
                    for h in range(NH):
                        K.op(PE, lambda h=h: nc.tensor.transpose(tpb_ps[:, h, :], qt[i][:, h * 128:(h + 1) * 128], ident[:]),
                             r=[qt[i], ident], w=[tpb_ps], sig=False)
                    for h in range(NH):
                        K.op(PE, lambda h=h: nc.tensor.transpose(tpb_ps[:, 4 + h, :], kt_[i][:, h * 128:(h + 1) * 128], ident[:]),
                             r=[kt_[i], ident], w=[tpb_ps], sig=(h == NH - 1))
                    yield
                    K.op(DVE, lambda: v.tensor_copy(out=qkT[i][:, :, :], in_=tpb_ps[:, :, :]), r=[tpb_ps], w=[qkT[i]])
                    yield
                    for h in range(NH):
                        K.op(PE, lambda h=h: nc.tensor.matmul(pp_ps[:, h * 128:(h + 1) * 128], lhsT=qkT[i][:, 4 + h, :],
                                                              rhs=qkT[i][:, h, :], start=True, stop=True),
                             r=[qkT[i]], w=[pp_ps], sig=(h == NH - 1))
                    yield
                    K.op(DVE, lambda: v.tensor_tensor(out=atsb[i][:, :, :], in0=pp_ps[:, :].rearrange("p (h c) -> p h c", h=NH),
                                                      in1=trii[:, :].unsqueeze(1).to_broadcast([128, NH, 128]), op=ALU.mult),
                         r=[pp_ps, trii], w=[atsb[i]])
                yield
                for j in range(2):
                    for h in range(NH):
                        K.op(PE, lambda j=j, h=h: nc.tensor.matmul(
                            pp_ps[:, j * 4 + h:j * 4 + h + 1], lhsT=l_[64 * j:64 * j + 64, h * 128:(h + 1) * 128],
                            rhs=ones_c[64 * j:64 * j + 64, 0:1], start=True, stop=True),
                            r=[l_, ones_c], w=[pp_ps], sig=(j == 1 and h == NH - 1))
                yield
                K.op(ACT, lambda: a.activation(out=dcy[i][:, :], in_=pp_ps[:, 0:8], func=AF.Exp), r=[pp_ps], w=[dcy[i]])
                yield
                return i

            def hg_recur(J, own, i, tix, chunk_order, trii, d_idx):
                if own:
                    for h in range(NH):
                        hs = slice(h * 128, (h + 1) * 128)
                        K.op(PE, lambda h=h, hs=hs: nc.tensor.matmul(ao_ps[:, hs], lhsT=atsb[i][:, h, :], rhs=vbf[i][:, hs],
                                                                     start=(h == 0), stop=False, skip_group_check=True),
                             r=[atsb[i], vbf[i]], w=[ao_ps], sig=(h == NH - 1))
                    yield
                for j in chunk_order:
                    pr = slice(64 * j, 64 * j + 64)
                    if own:
                        sb_cur = Sbf[sbi[0] % 2]
                        for h in range(NH):
                            hs = slice(h * 128, (h + 1) * 128)
                            K.op(PE, lambda h=h, hs=hs: nc.tensor.matmul(ao_ps[pr, hs], lhsT=qkT[i][:, h, pr], rhs=sb_cur[:, hs],
                                                                         start=False, stop=True, skip_group_check=True),
                                 r=[qkT[i], sb_cur], w=[ao_ps], sig=False)
                    for h in range(NH):
                        hs = slice(h * 128, (h + 1) * 128)
                        K.op(PE, lambda hs=hs: nc.tensor.matmul(ds_ps[:, hs], lhsT=kk[i][pr, hs], rhs=vbf[i][pr, hs],
                                                                start=True, stop=True), r=[kk[i], vbf[i]], w=[ds_ps],
                             sig=(h == NH - 1))
                    yield
                    for h in range(NH):
                        hs = slice(h * 128, (h + 1) * 128)
                        K.op(DVE, lambda h=h, hs=hs: v.scalar_tensor_tensor(
                            out=S[:, hs], in0=S[:, hs], scalar=dcy[i][:, j * 4 + h:j * 4 + h + 1], in1=ds_ps[:, hs],
                            op0=ALU.mult, op1=ALU.add), r=[S, dcy[i], ds_ps], w=[S])
                    if own:
                        sbi[0] += 1
                        nxt = Sbf[sbi[0] % 2]
                        K.op(DVE, lambda nxt=nxt: v.tensor_copy(out=nxt[:, :], in_=S[:, :]), r=[S], w=[nxt])
                    yield
                if own:
                    K.op(ACT, lambda: a.copy(out=ost[i][:, :], in_=ao_ps[:, :]), r=[ao_ps], w=[ost[i]])
                    dst = J["OF"] if d_idx == 0 else J["OB"]
                    K.dma(POOL, dst.t[tix, :, :], ost[i][:, :], r=[ost[i]], w=[dst], sb=ost[i])
                    yield

            def run_items(J, items):
                pend = None
                for it in items + [None]:
                    g_prep = hg_prep(J, it["own"], it["src"], it["tix"], it["trii"], it["trix"], it["oml"], it["d_idx"]) \
                        if it is not None else None
                    g_rec = None
                    if pend is not None:
                        pit, pi = pend
                        if pit.get("pre"):
                            pit["pre"]()
                        g_rec = hg_recur(J, pit["own"], pi, pit["tix"], pit["chunk_order"], pit["trii"], pit["d_idx"])
                    slot = None
                    while g_prep is not None or g_rec is not None:
                        if g_prep is not None:
                            try:
                                next(g_prep)
                            except StopIteration as e_:
                                slot = e_.value
                                g_prep = None
                        if g_rec is not None:
                            try:
                                next(g_rec)
                            except StopIteration:
                                g_rec = None
                        yield
                    if pend is not None and pend[0].get("post"):
                        pend[0]["post"]()
                    pend = (it, slot) if it is not None else None

            def pass_others(J, done):
                K.op(POOL, lambda: g.memset(S[:, :], 0.0), w=[S])
                K.op(POOL, lambda: g.memset(SF[:, :], 0.0), w=[SF])
                K.op(POOL, lambda: g.memset(SB[:, :], 0.0), w=[SB])
                yield
                for s in range(J["nseg"]):
                    sg_ = J["seg0"] + s
                    fF, fC = FL_DIRF + sg_, FL_CONT + sg_
                    K.op(DVE, lambda fF=fF: v.tensor_scalar(out=omlsel[:, :], in0=oml[0][:, :], scalar1=flg[:, fF:fF + 1],
                                                            scalar2=None, op0=ALU.mult), r=[oml[0], flg], w=[omlsel])
                    K.op(DVE, lambda fF=fF: v.scalar_tensor_tensor(out=omlsel[:, :], in0=oml[1][:, :], scalar=nflg[:, fF:fF + 1],
                                                                   in1=omlsel[:, :], op0=ALU.mult, op1=ALU.add),
                         r=[oml[1], nflg, omlsel], w=[omlsel])
                    K.op(DVE, lambda fF=fF: v.tensor_scalar(out=trixsel[:, :], in0=tri["fx"][:, :], scalar1=flg[:, fF:fF + 1],
                                                            scalar2=None, op0=ALU.mult), r=[tri["fx"], flg], w=[trixsel])
                    K.op(DVE, lambda fF=fF: v.scalar_tensor_tensor(out=trixsel[:, :], in0=tri["bx"][:, :], scalar=nflg[:, fF:fF + 1],
                                                                   in1=trixsel[:, :], op0=ALU.mult, op1=ALU.add),
                         r=[tri["bx"], nflg, trixsel], w=[trixsel])
                    items = [dict(own=False, src=J["HGo"], tix=s * NTO + t, chunk_order=(0, 1), trii=None, trix=trixsel,
                                  oml=omlsel, d_idx=0) for t in range(NTO)]

                    def pre(fC=fC):
                        K.op(DVE, lambda: v.tensor_scalar(out=S[:, :], in0=S[:, :], scalar1=flg[:, fC:fC + 1], scalar2=None,
                                                          op0=ALU.mult), r=[S, flg], w=[S])

                    def post(sg_=sg_):
                        fEF, fEB = FL_ENDF + sg_, FL_ENDB + sg_
                        K.op(DVE, lambda: v.scalar_tensor_tensor(out=SF[:, :], in0=S[:, :], scalar=flg[:, fEF:fEF + 1],
                                                                 in1=SF[:, :], op0=ALU.mult, op1=ALU.add), r=[S, flg, SF], w=[SF])
                        K.op(DVE, lambda: v.scalar_tensor_tensor(out=SB[:, :], in0=S[:, :], scalar=flg[:, fEB:fEB + 1],
                                                                 in1=SB[:, :], op0=ALU.mult, op1=ALU.add), r=[S, flg, SB], w=[SB])
                    items[0]["pre"] = pre
                    items[-1]["post"] = post
                    yield from run_items(J, items)
                done[J["name"]] = (SF, SB)

            def pass_own(J, d_idx, done):
                while J["name"] not in done:
                    yield
                S0 = done[J["name"]][d_idx]
                order, co, ti_, tx_ = [(range(NTO), (0, 1), "fi", "fx"), (range(NTO - 1, -1, -1), (1, 0), "bi", "bx")][d_idx]
                items = [dict(own=True, src=J["HG"], tix=t, chunk_order=co, trii=tri[ti_], trix=tri[tx_], oml=oml[d_idx],
                              d_idx=d_idx) for t in order]

                def pre():
                    K.op(DVE, lambda: v.tensor_copy(out=S[:, :], in_=S0[:, :]), r=[S0], w=[S])
                    K.op(DVE, lambda: v.tensor_copy(out=Sbf[sbi[0] % 2][:, :], in_=S[:, :]), r=[S], w=[Sbf[sbi[0] % 2]])
                items[0]["pre"] = pre
                yield from run_items(J, items)

            return pass_others, pass_own

        def chain(*gens):
            for g_ in gens:
                yield from g_

        JP, JS = cfg.jobs
        done = {}
        po1, pw1 = make_bufs()
        po2, pw2 = make_bufs()
        queue = [(JP, 0), (JS, 0), (JP, 1), (JS, 1)]

        def worker(po, pw, J0):
            yield from po(J0, done)
            while queue:
                Jn, d_ = queue.pop(0)
                yield from pw(Jn, d_, done)

        active = [worker(po1, pw1, JP), worker(po2, pw2, JS)]
        while active:
            for gth in list(active):
                try:
                    next(gth)
                except StopIteration:
                    active.remove(gth)

    K.barrier()

    with ExitStack() as st:
        LKmax = max(J["NK"] for J in cfg.jobs)
        wt = sb("wt", [128, NH, 1152], F32, st)
        idxm = sb("idxm", [128, 384], F32, st)
        wtmp = sb("wtmp", [128, 384], F32, st)
        ktb2 = [sb("ktb", [128, LKmax * 128], BF16, st) for _ in range(2)]
        vab = sb("vab", [128, LKmax, 129], BF16, st)
        qtb2 = [sb("qtb", [128, TO], BF16, st) for _ in range(2)]
        accsb = [sb("accsb", [128, 8, 129], F32, st) for _ in range(2)]
        epi = [0]
        NPB = 3
        pT = [sb("pT", [128, 1024], BF16, st) for _ in range(NPB)]
        stmp = [sb("stmp", [128, 2, 512], F32, st) for _ in range(2)]
        nrm = [sb("nrm", [128, 16], F32, st) for _ in range(2)]
        otmp = [sb("otmp", [128, 512], F32, st) for _ in range(2)]
        osqa = [sb("osqa", [128, 512], F32, st) for _ in range(2)]
        oab = [sb("oab", [128, 4, 128], BF16, st) for _ in range(2)]
        s_ps = [ps("sps", [128, 1024], F32, st) for _ in range(2)]
        acc_ps = [ps("acc", [128, 3, 129], F32, st) for _ in range(3)]

        K.dma(SP, idxm[:, :], idxm_d.t[:, :], r=[idxm_d], w=[idxm], sb=idxm)
        for h in range(NH):
            K.op(DVE, lambda h=h: v.tensor_scalar(out=wt[:, h, 0:384], in0=idxm[:, :], scalar1=0.0,
                                                  scalar2=relb[:, 31 * 4 + h:31 * 4 + h + 1], op0=ALU.mult, op1=ALU.add),
                 r=[idxm, relb], w=[wt])
            K.op(DVE, lambda h=h: v.tensor_scalar(out=wt[:, h, 768:1152], in0=idxm[:, :], scalar1=0.0,
                                                  scalar2=relb[:, 15 * 4 + h:15 * 4 + h + 1], op0=ALU.mult, op1=ALU.add),
                 r=[idxm, relb], w=[wt])
            for b in range(32):
                dst = wt[:, h, 384:768] if b == 0 else wtmp[:, :]
                K.op(DVE, lambda h=h, b=b, dst=dst: v.tensor_scalar(out=dst, in0=idxm[:, :], scalar1=float(b),
                                                                    scalar2=relb[:, b * 4 + h:b * 4 + h + 1],
                                                                    op0=ALU.is_equal, op1=ALU.mult),
                     r=[idxm, relb], w=[wt if b == 0 else wtmp])
                if b > 0:
                    K.op(DVE, lambda h=h: v.tensor_tensor(out=wt[:, h, 384:768], in0=wt[:, h, 384:768], in1=wtmp[:, :], op=ALU.add),
                         r=[wt, wtmp], w=[wt])

        it = [0]
        NG = TO // 512
        units = [(J, h) for J in cfg.jobs for h in range(NH)]

        def load_kq(u):
            J, h = units[u]
            kb_, qb_ = ktb2[u % 2], qtb2[u % 2]
            K.dma(SP, kb_[:, 0:J["NK"] * 128], J["KT"].t[h, :, :], r=[J["KT"]], w=[kb_], sb=kb_)
            K.dma(SP, qb_[:, :], J["QT"].t[h, :, :], r=[J["QT"]], w=[qb_], sb=qb_)

        def epilogue(J, h, gq, asb):
            e = epi[0] % 2
            epi[0] += 1
            n_, o_, q_, ob = nrm[e], otmp[e], osqa[e], oab[e]
            o4 = o_[:, :].rearrange("p (a d) -> p a d", a=4)
            q4 = q_[:, :].rearrange("p (a d) -> p a d", a=4)
            K.op(DVE, lambda: v.reciprocal(out=n_[:, 0:8], in_=asb[:, :, 128]), r=[asb], w=[n_])
            K.op(DVE, lambda: v.tensor_scalar(out=n_[:, 4:8], in0=n_[:, 4:8], scalar1=lam_t[:, 0:1], scalar2=None, op0=ALU.mult),
                 r=[n_, lam_t], w=[n_])
            yield
            K.op(DVE, lambda: v.tensor_tensor(out=o4, in0=asb[:, 0:4, 0:128], in1=n_[:, 0:4].unsqueeze(2).to_broadcast([128, 4, 128]),
                                              op=ALU.mult), r=[asb, n_], w=[o_])
            K.op(DVE, lambda: v.tensor_tensor(out=q4, in0=asb[:, 4:8, 0:128], in1=n_[:, 4:8].unsqueeze(2).to_broadcast([128, 4, 128]),
                                              op=ALU.mult), r=[asb, n_], w=[q_])
            yield
            K.op(DVE, lambda: v.tensor_tensor(out=o_[:, :], in0=o_[:, :], in1=q_[:, :], op=ALU.add), r=[o_, q_], w=[o_])
            yield
            K.op(DVE, lambda: v.tensor_tensor(out=q_[:, :], in0=o_[:, :], in1=o_[:, :], op=ALU.mult), r=[o_], w=[q_])
            yield
            K.op(DVE, lambda: v.tensor_reduce(out=n_[:, 8:12], in_=q4, axis=AX.X, op=ALU.add), r=[q_], w=[n_])
            yield
            K.op(ACT, lambda: a.activation(out=n_[:, 12:16], in_=n_[:, 8:12], func=AF.Ln, bias=eps_c[:], scale=1.0 / 128),
                 r=[n_, eps_c], w=[n_])
            yield
            K.op(ACT, lambda: a.activation(out=n_[:, 12:16], in_=n_[:, 12:16], func=AF.Exp, scale=-0.5), r=[n_], w=[n_])
            yield
            K.op(DVE, lambda: v.tensor_tensor(out=o4, in0=o4, in1=n_[:, 12:16].unsqueeze(2).to_broadcast([128, 4, 128]), op=ALU.mult),
                 r=[o_, n_], w=[o_])
            yield
            K.op(DVE, lambda: v.scalar_tensor_tensor(out=ob[:, :, :], in0=o4, scalar=1.0 - LAM_INIT,
                                                     in1=subln[:, :].unsqueeze(1).to_broadcast([128, 4, 128]),
                                                     op0=ALU.mult, op1=ALU.mult), r=[o_, subln], w=[ob])
            yield
            K.dma(POOL, J["O"].t[gq * 4:(gq + 1) * 4, :, h * 128:(h + 1) * 128].rearrange("t p c -> p t c"), ob[:, :, :],
                  r=[ob], w=[J["O"]], sb=ob)

        pending = []
        load_kq(0)
        for u, (J, h) in enumerate(units):
            NK, NF = J["NK"], J["NF"]
            jj = J["j"]
            ktb, qtb = ktb2[u % 2], qtb2[u % 2]
            K.dma(SP, vab[:, 0:NK, :], J["V"].t[h, :, :, :], r=[J["V"]], w=[vab], sb=vab)
            if u + 1 < len(units):
                load_kq(u + 1)
            if True:
                for gq in range(NG):
                    sched = []
                    for kt in range(NK):
                        if kt < NF:
                            s = kt // NTO
                            last = (kt % NTO) == NTO - 1
                            col = (16 if last else 0) + 4 * (J["seg0"] + s) + h
                            sched.append((kt, "const", bfar[:, col:col + 1]))
                        else:
                            if kt == NF:
                                nb, mask = -1, flg[:, FL_PREVM + jj:FL_PREVM + jj + 1]
                            elif kt == NF + 1:
                                nb, mask = NTO, flg[:, FL_NEXTM + jj:FL_NEXTM + jj + 1]
                            else:
                                nb, mask = kt - NF - 2, None
                            dlt = nb - 4 * gq
                            if -1 <= dlt <= 4:
                                sched.append((kt, "tile", (128 * (4 - dlt), mask)))
                            else:
                                side = relb[:, 15 * 4 + h:15 * 4 + h + 1] if dlt < 0 else relb[:, 31 * 4 + h:31 * 4 + h + 1]
                                if mask is not None:
                                    which = 0 if kt == NF else 1
                                    c0_ = (jj * 2 + which) * 4 + h
                                    sched.append((kt, "const", hbias[:, c0_:c0_ + 1]))
                                else:
                                    sched.append((kt, "const", side))
                    nsch = len(sched)

                    def qk(n):
                        kt = sched[n][0]
                        i = it[0] + n
                        sp_ = s_ps[i % 2]
                        K.op(PE, lambda: nc.tensor.matmul(sp_[:, 0:512], lhsT=ktb[0:64, kt * 128:(kt + 1) * 128],
                                                          rhs=qtb[0:64, gq * 512:(gq + 1) * 512], start=True, stop=True),
                             r=[ktb, qtb], w=[sp_], sig=False)
                        K.op(PE, lambda: nc.tensor.matmul(sp_[:, 512:1024], lhsT=ktb[64:128, kt * 128:(kt + 1) * 128],
                                                          rhs=qtb[64:128, gq * 512:(gq + 1) * 512], start=True, stop=True),
                             r=[ktb, qtb], w=[sp_])

                    def expo(n):
                        kt, mode, arg = sched[n]
                        i = it[0] + n
                        sp_, p_ = s_ps[i % 2], pT[i % NPB]
                        if mode == "const":
                            K.op(ACT, lambda: a.activation(out=p_[:, :], in_=sp_[:, :], func=AF.Exp, bias=arg, scale=1.0),
                                 r=[sp_, bfar, relb, hbias], w=[p_])
                        else:
                            c0, mask = arg
                            tb = stmp[i % 2]
                            K.op(DVE, lambda: v.tensor_tensor(
                                out=tb[:, :, :], in0=sp_[:, :].rearrange("p (c q) -> p c q", c=2),
                                in1=wt[:, h, c0:c0 + 512].unsqueeze(1).to_broadcast([128, 2, 512]), op=ALU.add),
                                r=[sp_, wt], w=[tb])
                            if mask is None:
                                K.op(ACT, lambda: a.activation(out=p_[:, :], in_=tb[:, :, :].rearrange("p c q -> p (c q)"),
                                                               func=AF.Exp), r=[tb], w=[p_])
                            else:
                                K.op(ACT, lambda: a.activation(out=p_[:, :], in_=tb[:, :, :].rearrange("p c q -> p (c q)"),
                                                               func=AF.Exp, bias=mask, scale=1.0), r=[tb, flg], w=[p_])

                    def pv(n):
                        kt = sched[n][0]
                        i = it[0] + n
                        p_ = pT[i % NPB]
                        for c in range(2):
                            for qs in range(4):
                                ai = c * 4 + qs
                                acc = acc_ps[ai // 3]
                                K.op(PE, lambda c=c, qs=qs, ai=ai, acc=acc: nc.tensor.matmul(
                                    acc[:, ai % 3, :], lhsT=p_[:, c * 512 + qs * 128:c * 512 + (qs + 1) * 128], rhs=vab[:, kt, :],
                                    start=(n == 0 and ai % 3 == 0), stop=(n == nsch - 1), skip_group_check=True),
                                    r=[p_, vab], w=[acc], sig=(ai == 7))

                    qk(0)
                    for n in range(nsch):
                        if n + 1 < nsch:
                            qk(n + 1)
                        expo(n)
                        pv(n)
                        if n >= 3 and pending:
                            try:
                                next(pending[0])
                            except StopIteration:
                                pending.pop(0)
                    it[0] += nsch
                    while pending:
                        for _ in pending[0]:
                            pass
                        pending.pop(0)
                    asb = accsb[(u * NG + gq) % 2]
                    for b_ in range(3):
                        nacc = 3 if b_ < 2 else 2
                        K.op(DVE, lambda b_=b_, nacc=nacc: v.tensor_copy(out=asb[:, 3 * b_:3 * b_ + nacc, :], in_=acc_ps[b_][:, 0:nacc, :]),
                             r=[acc_ps[b_]], w=[asb])
                    pending.append(epilogue(J, h, gq, asb))
        while pending:
            for _ in pending[0]:
                pass
            pending.pop(0)

    K.barrier()

    with ExitStack() as st:
        wo = sb("wo", [128, 8, D], BF16, st)
        xin = [sb("xin", [128, D], F32, st) for _ in range(4)]
        oin = [sb("oin", [128, D], BF16, st) for _ in range(4)]
        oT = [sb("oT", [128, 8, 128], BF16, st) for _ in range(4)]
        ofd = [sb("ofd", [128, 512], F32, st) for _ in range(4)]
        obd = [sb("obd", [128, 512], F32, st) for _ in range(4)]
        gsd = [sb("gsd", [128, 512], F32, st) for _ in range(4)]
        ss4d = [sb("ss4d", [128, 8], F32, st) for _ in range(4)]
        x1 = [sb("x1", [128, D], F32, st) for _ in range(4)]
        ssd = [sb("ssd", [128, 2], F32, st) for _ in range(4)]
        sqd = sb("sqd", [128, D], BF16, st)
        h2 = [sb("h2", [128, D], BF16, st) for _ in range(4)]
        h2Tst = [sb("h2Tst", [128, 8, 512], BF16, st) for _ in range(2)]
        tpx_ps = [ps("tpx", [128, 8, 128], BF16, st) for _ in range(4)]
        prh_ps = [ps("prh", [128, 512], F32, st) for _ in range(4)]

        wo_v = w_out_d.t.rearrange("(c p) n -> p c n", p=128)
        for c in range(8):
            s_ = xin[c % 4]
            K.dma(SP, s_[:, :], wo_v[:, c, :], r=[w_out_d], w=[s_], sb=s_)
            if c % 2 == 0:
                K.op(DVE, lambda c=c, s_=s_: v.tensor_copy(out=wo[:, c, :], in_=s_[:, :]), r=[s_], w=[wo])
            else:
                K.op(ACT, lambda c=c, s_=s_: a.copy(out=wo[:, c, :], in_=s_[:, :]), r=[s_], w=[wo])

        def d1_tile(J, i, ti, hst, t):
            NF = J["NF"]
            xi, oi, oT_, x1_ = xin[i % 4], oin[i % 4], oT[i % 4], x1[i % 4]
            K.dma(SP, xi[:, :], J["x"].t[(NF + 2 + ti) * 128:(NF + 3 + ti) * 128, :], r=[J["x"]], w=[xi], sb=xi)
            K.dma(SP, oi[:, 0:512], J["O"].t[ti, :, 0:512], r=[J["O"]], w=[oi], sb=oi)
            of_, ob_, gs_, s4 = ofd[i % 4], obd[i % 4], gsd[i % 4], ss4d[i % 4]
            K.dma(SP, of_[:, :], J["OF"].t[ti, :, :], r=[J["OF"]], w=[of_], sb=of_)
            K.dma(SP, ob_[:, :], J["OB"].t[ti, :, :], r=[J["OB"]], w=[ob_], sb=ob_)
            K.dma(SP, gs_[:, :], J["HG"].t[ti, :, 6, :], r=[J["HG"]], w=[gs_], sb=gs_)
            yield
            K.op(DVE, lambda: v.tensor_tensor(out=of_[:, :], in0=of_[:, :], in1=ob_[:, :], op=ALU.add), r=[of_, ob_], w=[of_])
            K.op(DVE, lambda: v.tensor_tensor(out=ob_[:, :], in0=of_[:, :], in1=of_[:, :], op=ALU.mult), r=[of_], w=[ob_])
            yield
            K.op(DVE, lambda: v.tensor_reduce(out=s4[:, 0:4], in_=ob_[:, :].rearrange("p (h d) -> p h d", h=NH),
                                              axis=AX.X, op=ALU.add), r=[ob_], w=[s4])
            yield
            K.op(ACT, lambda: a.activation(out=s4[:, 4:8], in_=s4[:, 0:4], func=AF.Ln, bias=eps_c[:], scale=1.0 / 128),
                 r=[s4, eps_c], w=[s4])
            yield
            K.op(ACT, lambda: a.activation(out=s4[:, 4:8], in_=s4[:, 4:8], func=AF.Exp, scale=-0.5), r=[s4], w=[s4])
            yield
            o3 = of_[:, :].rearrange("p (h d) -> p h d", h=NH)
            K.op(DVE, lambda: v.tensor_tensor(out=o3, in0=o3, in1=s4[:, 4:8].unsqueeze(2).to_broadcast([128, NH, 128]),
                                              op=ALU.mult), r=[of_, s4], w=[of_])
            K.op(DVE, lambda: v.tensor_tensor(out=of_[:, :], in0=of_[:, :], in1=hnw[:, :], op=ALU.mult), r=[of_, hnw], w=[of_])
            yield
            K.op(DVE, lambda: v.tensor_tensor(out=oi[:, 512:1024], in0=of_[:, :], in1=gs_[:, :], op=ALU.mult),
                 r=[of_, gs_], w=[oi])
            yield
            tp = tpx_ps[i % 4]
            for c in range(8):
                K.op(PE, lambda c=c: nc.tensor.transpose(tp[:, c, :], oi[:, c * 128:(c + 1) * 128], ident[:]),
                     r=[oi, ident], w=[tp], sig=(c == 7))
            yield
            K.op(DVE, lambda: v.tensor_copy(out=oT_[:, :, :], in_=tp[:, :, :]), r=[tp], w=[oT_])
            yield
            pp = prh_ps[i % 4]
            for half in range(2):
                hc = slice(half * 512, (half + 1) * 512)
                for c in range(8):
                    K.op(PE, lambda c=c, hc=hc: nc.tensor.matmul(pp[:, :], lhsT=oT_[:, c, :], rhs=wo[:, c, hc],
                                                                 start=(c == 0), stop=(c == 7)), r=[oT_, wo], w=[pp], sig=(c == 7))
                yield
                K.op(DVE, lambda hc=hc: v.tensor_tensor(out=x1_[:, hc], in0=pp[:, :], in1=xi[:, hc], op=ALU.add),
                     r=[pp, xi], w=[x1_])
                yield
            K.dma(POOL, J["X1"].t[ti, :, :], x1_[:, :], r=[x1_], w=[J["X1"]], sb=x1_)
            yield
            s_ = ssd[i % 4]
            K.op(ACT, lambda: a.activation(out=sqd[:, :], in_=x1_[:, :], func=AF.Square, accum_out=s_[:, 0:1]),
                 r=[x1_], w=[sqd, s_])
            yield
            K.op(ACT, lambda: a.activation(out=s_[:, 1:2], in_=s_[:, 0:1], func=AF.Ln, bias=eps_c[:], scale=1.0 / D),
                 r=[s_, eps_c], w=[s_])
            yield
            K.op(ACT, lambda: a.activation(out=s_[:, 1:2], in_=s_[:, 1:2], func=AF.Exp, scale=-0.5), r=[s_], w=[s_])
            yield
            h2_ = h2[i % 4]
            K.op(ACT, lambda: a.activation(out=h2_[:, :], in_=x1_[:, :], func=AF.Copy, scale=s_[:, 1:2]), r=[x1_, s_], w=[h2_])
            yield
            tp2 = tpx_ps[i % 4]
            for c in range(8):
                K.op(PE, lambda c=c: nc.tensor.transpose(tp2[:, c, :], h2_[:, c * 128:(c + 1) * 128], ident[:]),
                     r=[h2_, ident], w=[tp2], sig=(c == 7))
            yield
            K.op(DVE, lambda: v.tensor_copy(out=hst[:, :, t * 128:(t + 1) * 128], in_=tp2[:, :, :]), r=[tp2], w=[hst])
            if t == 3:
                t0_ = ti - 3
                for c in range(8):
                    K.dma(POOL, J["H2T"].t[c, :, t0_ * 128:(t0_ + 4) * 128], hst[:, c, :], r=[hst], w=[J["H2T"]], sb=hst)

        tiles = []
        for J in cfg.jobs:
            for ti in range(NTO):
                tiles.append((J, ti))
        gens = [d1_tile(J, i, ti, h2Tst[(i // 4) % 2], ti % 4) for i, (J, ti) in enumerate(tiles)]
        live = []
        nxt_i = 0
        while live or nxt_i < len(gens):
            while len(live) < 4 and nxt_i < len(gens):
                live.append(gens[nxt_i])
                nxt_i += 1
            for gth in list(live):
                try:
                    next(gth)
                except StopIteration:
                    live.remove(gth)

    K.barrier()

    with ExitStack() as st:
        wmi = sb("wmi", [128, 8, DFF], BF16, st)
        wmo = sb("wmo", [128, 32, D], BF16, st)
        h2T = [sb("h2T", [128, 8, 512], BF16, st) for _ in range(2)]
        aT = sb("aT", [128, 32, 512], BF16, st)
        rl = [sb("rl", [128, 512], F32, st) for _ in range(2)]
        x1in = [sb("x1in", [128, D], F32, st) for _ in range(2)]
        yo = [sb("yo", [128, D], F32, st) for _ in range(1)]
        mi_ps = [ps("mid", [128, 512], F32, st) for _ in range(2)]
        yo_ps = [ps("yod", [128, 1024], F32, st) for _ in range(2)]

        stgs = x1in + yo
        cnt = 0

        def wload(dst, dbuf, src_ap, srcbuf, scale_ap, scale_buf):
            nonlocal cnt
            s_ = stgs[cnt % 3]
            K.dma(SP, s_[:, :], src_ap, r=[srcbuf], w=[s_], sb=s_)
            if cnt % 2 == 0:
                if scale_ap is None:
                    K.op(DVE, lambda: v.tensor_copy(out=dst, in_=s_[:, :]), r=[s_], w=[dbuf])
                else:
                    K.op(DVE, lambda: v.tensor_scalar(out=dst, in0=s_[:, :], scalar1=scale_ap, scalar2=None, op0=ALU.mult),
                         r=[s_, scale_buf], w=[dbuf])
            else:
                if scale_ap is None:
                    K.op(ACT, lambda: a.copy(out=dst, in_=s_[:, :]), r=[s_], w=[dbuf])
                else:
                    K.op(ACT, lambda: a.activation(out=dst, in_=s_[:, :], func=AF.Copy, scale=scale_ap), r=[s_, scale_buf], w=[dbuf])
            cnt += 1

        wmi_v = w_mi_d.t.rearrange("(c p) n -> p c n", p=128)
        wmo_v = w_mo_d.t.rearrange("(c p) n -> p c n", p=128)
        for c in range(8):
            for q4 in range(4):
                wload(wmi[:, c, q4 * 1024:(q4 + 1) * 1024], wmi, wmi_v[:, c, q4 * 1024:(q4 + 1) * 1024], w_mi_d, mnw[:, c:c + 1], mnw)
        for c in range(32):
            wload(wmo[:, c, :], wmo, wmo_v[:, c, :], w_mo_d, None, None)

        di = [0]
        sti2 = [0]
        for J in cfg.jobs:
            for t0 in range(0, NTO, 4):
                hT_ = h2T[sti2[0] % 2]
                sti2[0] += 1
                for c in range(8):
                    K.dma(SP, hT_[:, c, :], J["H2T"].t[c, :, t0 * 128:(t0 + 4) * 128], r=[J["H2T"]], w=[hT_], sb=hT_)
                for f in range(32):
                    mp = mi_ps[f % 2]
                    for c in range(8):
                        K.op(PE, lambda c=c, f=f: nc.tensor.matmul(mp[:, :], lhsT=wmi[:, c, f * 128:(f + 1) * 128], rhs=hT_[:, c, :],
                                                                   start=(c == 0), stop=(c == 7)), r=[wmi, hT_], w=[mp], sig=(c == 7))
                    r_ = rl[f % 2]
                    K.op(ACT, lambda: a.activation(out=r_[:, :], in_=mp[:, :], func=AF.Relu), r=[mp], w=[r_])
                    K.op(DVE, lambda f=f: v.tensor_tensor(out=aT[:, f, :], in0=r_[:, :], in1=r_[:, :], op=ALU.mult), r=[r_], w=[aT])
                for t in range(4):
                    i = di[0]
                    di[0] += 1
                    ti = t0 + t
                    xr = x1in[i % 2]
                    K.dma(SP, xr[:, :], J["X1"].t[ti, :, :], r=[J["X1"]], w=[xr], sb=xr)
                    yp = yo_ps[i % 2]
                    for half in range(2):
                        for f in range(32):
                            K.op(PE, lambda f=f, half=half: nc.tensor.matmul(
                                yp[:, half * 512:(half + 1) * 512], lhsT=aT[:, f, t * 128:(t + 1) * 128],
                                rhs=wmo[:, f, half * 512:(half + 1) * 512], start=(f == 0), stop=(f == 31)),
                                r=[aT, wmo], w=[yp], sig=(f == 31 and half == 1))
                    y_ = yo[0]
                    K.op(DVE, lambda: v.tensor_tensor(out=y_[:, :], in0=yp[:, :], in1=xr[:, :], op=ALU.add), r=[yp, xr], w=[y_])
                    K.dma(POOL, J["y"].t[ti * 128:(ti + 1) * 128, :], y_[:, :], r=[y_], w=[J["y"]], sb=y_)

    K.finish(POOL)
    K.finish(SP)
    return nc


def _t5_bucket_np(rel):
    nb, max_exact = 16, 8
    rel = np.asarray(rel, dtype=np.int64)
    try:
        import jax
        import jax.numpy as jnp
        with jax.default_device(jax.devices("cpu")[0]):
            r = jnp.asarray(rel.astype(np.int32))
            ret = jnp.where(r > 0, nb, 0)
            n = jnp.abs(r)
            nf = jnp.maximum(n, 1).astype(jnp.float32)
            large = max_exact + (jnp.log(nf / max_exact) / math.log(128 / max_exact) * (nb - max_exact)).astype(jnp.int32)
            large = jnp.minimum(large, nb - 1)
            return np.asarray(ret + jnp.where(n < max_exact, n, large)).astype(np.int64)
    except Exception:
        ret = np.where(rel > 0, nb, 0)
        n = np.abs(rel)
        nf = np.maximum(n, 1).astype(np.float32)
        val = (np.log(nf / np.float32(max_exact)) / np.float32(math.log(128 / max_exact)) * np.float32(nb - max_exact)).astype(np.float32)
        large = np.minimum(max_exact + np.trunc(val).astype(np.int64), nb - 1)
        return ret + np.where(n < max_exact, n, large)


def _core_layout(x_seq, L, TO, part):
    nparts = L // TO
    NTO = TO // 128
    o0, o1 = part * TO, (part + 1) * TO
    segs = []
    before = list(range(0, part))
    after = list(range(nparts - 1, part, -1))
    for p_ in before:
        segs.append(("F", p_))
    for p_ in after:
        segs.append(("B", p_))
    rows = []
    for kind, p_ in segs:
        blk = x_seq[p_ * TO:(p_ + 1) * TO]
        if kind == "B":
            blk = blk.reshape(TO // 64, 64, -1)[::-1].reshape(TO, -1)
        rows.append(blk)
    zero = np.zeros((128, x_seq.shape[1]), np.float32)
    rows.append(x_seq[o0 - 128:o0] if o0 > 0 else zero)
    rows.append(x_seq[o1:o1 + 128] if o1 < L else zero)
    rows.append(x_seq[o0:o1])
    xj = np.ascontiguousarray(np.concatenate(rows, axis=0))
    ns = len(segs)
    dirF = [1.0 if k == "F" else 0.0 for k, _ in segs]
    cont = [1.0 if (i > 0 and segs[i][0] == segs[i - 1][0]) else 0.0 for i in range(ns)]
    endF = [1.0 if (segs[i][0] == "F" and (i == ns - 1 or segs[i + 1][0] != "F")) else 0.0 for i in range(ns)]
    endB = [1.0 if (segs[i][0] == "B" and i == ns - 1) else 0.0 for i in range(ns)]
    lastm = [NEG if (endF[i] or endB[i]) else 0.0 for i in range(ns)]
    prevm = NEG if o0 == 0 else 0.0
    nextm = NEG if o1 == L else 0.0
    return xj, dict(dirF=dirF, cont=cont, endF=endF, endB=endB, lastm=lastm, prevm=prevm, nextm=nextm)


def _prepare(inputs, TO):
    xp = np.asarray(inputs["x_prompt"], np.float32)
    xs = np.asarray(inputs["x_sample"], np.float32)
    LP, LS = xp.shape[1], xs.shape[1]
    rel = np.arange(128)[:, None] - np.arange(384)[None, :] + 128
    idxm = _t5_bucket_np(rel).astype(np.float32)
    common = {
        "w_in": np.ascontiguousarray(np.asarray(inputs["w_in"], np.float32)[0]),
        "w_out": np.ascontiguousarray(np.asarray(inputs["w_out"], np.float32)[0]),
        "w_mlp_in": np.ascontiguousarray(np.asarray(inputs["w_mlp_in"], np.float32)[0]),
        "w_mlp_out": np.ascontiguousarray(np.asarray(inputs["w_mlp_out"], np.float32)[0]),
        "attn_norm_w": np.asarray(inputs["attn_norm_w"], np.float32).reshape(1, D),
        "mlp_norm_w": np.asarray(inputs["mlp_norm_w"], np.float32).reshape(1, D),
        "qk_norm_w": np.asarray(inputs["qk_norm_w"], np.float32).reshape(2, 64),
        "diff_lambda": np.ascontiguousarray(np.asarray(inputs["diff_lambda"], np.float32).reshape(4, 64)),
        "diff_subln_w": np.asarray(inputs["diff_subln_w"], np.float32).reshape(1, 128),
        "rel_bias": np.ascontiguousarray(np.asarray(inputs["rel_bias"], np.float32).reshape(32, 4)),
        "hgrn_lb": np.ascontiguousarray(np.asarray(inputs["hgrn_lb"], np.float32).reshape(2, 2, 512)),
        "hgrn_norm_w": np.asarray(inputs["hgrn_norm_w"], np.float32).reshape(1, 128),
        "idxm": idxm,
    }
    in_maps = []
    for c in range(8):
        pP, pS = LP // TO, LS // TO
        xjP, fP = _core_layout(xp[c // pP], LP, TO, c % pP)
        xjS, fS = _core_layout(xs[c // pS], LS, TO, c % pS)
        fl = np.zeros((1, NFLAG), np.float32)
        for nm, base in (("dirF", FL_DIRF), ("cont", FL_CONT), ("endF", FL_ENDF), ("endB", FL_ENDB), ("lastm", FL_LASTM)):
            fl[0, base] = fP[nm][0]
            fl[0, base + 1:base + 4] = fS[nm]
        fl[0, FL_PREVM], fl[0, FL_PREVM + 1] = fP["prevm"], fS["prevm"]
        fl[0, FL_NEXTM], fl[0, FL_NEXTM + 1] = fP["nextm"], fS["nextm"]
        m = dict(common)
        m["x_P"], m["x_S"], m["flags"] = xjP, xjS, fl
        in_maps.append(m)
    return in_maps, LP, LS


_CACHE = {}


def run(inputs, TO=4096, dbg=False, serial=False):
    in_maps, LP, LS = _prepare(inputs, TO)
    assert LP == 2 * TO and LS == 4 * TO
    key = (TO, dbg, serial)
    if key not in _CACHE:
        cfg_ = Cfg(TO, dbg)
        cfg_.serial = serial
        _CACHE[key] = build(cfg_)
    nc = _CACHE[key]
    res = run_bass_kernel_spmd(nc, in_maps, core_ids=list(range(8)))
    B, DB = inputs["x_prompt"].shape[0], inputs["x_sample"].shape[0]
    yp = np.zeros((B, LP, D), np.float32)
    ys = np.zeros((DB, LS, D), np.float32)
    for c in range(8):
        r = res.results[c]
        yp[c // 2, (c % 2) * TO:(c % 2 + 1) * TO] = r["y_P"]
        ys[c // 4, (c % 4) * TO:(c % 4 + 1) * TO] = r["y_S"]
    return (yp, ys), res


def kernel(**inputs):
    out, _ = run(inputs, TO=4096)
    return out
```

```python
import bisect
import math
from contextlib import ExitStack

import numpy as np
import concourse.bass as bass
import concourse.mybir as mybir
from concourse.bass_utils import run_bass_kernel_spmd

F32 = mybir.dt.float32
BF16 = mybir.dt.bfloat16
AF = mybir.ActivationFunctionType
ALU = mybir.AluOpType
AX = mybir.AxisListType

D = 1024
NH = 4
PROJ = 4096
DFF = 4096
EPS = 1e-6
NEG = -30000.0
C_QA, C_KA, C_VA, C_QH, C_FF, C_FB, C_IH, C_GH = [512 * i for i in range(8)]
LAM_INIT = 0.8 - 0.6 * math.exp(-0.3 * 0)
STRICT_SYNC = True
NFLAG = 24
FL_DIRF, FL_CONT, FL_ENDF, FL_ENDB, FL_LASTM, FL_PREVM, FL_NEXTM = 0, 4, 8, 12, 16, 20, 22


class Eng:
    def __init__(self, K, name, eng):
        self.name, self.eng = name, eng
        self.sem = K.nc.alloc_semaphore("e_" + name)
        self.ninstr = 0
        self.sigpos = []
        self.waited = {}


class Buf:
    def __init__(self, name, t=None, dram=False):
        self.name, self.t, self.dram = name, t, dram
        self.w = {}
        self.r = {}
        self.dsem = None
        self.dcount = 0
        self.dwrites = {}

    def __getitem__(self, k):
        return self.t[k]


class Kern:
    def __init__(self, nc):
        self.nc = nc
        self.PE = Eng(self, "pe", nc.tensor)
        self.ACT = Eng(self, "act", nc.scalar)
        self.DVE = Eng(self, "dve", nc.vector)
        self.POOL = Eng(self, "pool", nc.gpsimd)
        self.SP = Eng(self, "sp", nc.sync)
        self.engs = [self.PE, self.ACT, self.DVE, self.POOL, self.SP]
        self.nsem = 5
        self.dma_bufs = []

    def _resolve(self, ev):
        if ev[0] == "d":
            return ev[1], ev[2]
        _, E, pos = ev
        i = bisect.bisect_left(E.sigpos, pos)
        assert i < len(E.sigpos), f"no covering signal on {E.name}"
        return E.sem, i + 1

    def _wait(self, E, deps, raw=()):
        need = {}
        rawset = set(id(e) for e in raw)
        for ev in list(deps) + list(raw):
            if ev is None:
                continue
            if ev[0] == "c" and ev[1] is E and (E is self.PE or (not STRICT_SYNC and E is not self.POOL
                                                                 and id(ev) not in rawset)):
                continue
            sem, val = self._resolve(ev[:3])
            k = id(sem)
            if k not in need or need[k][1] < val:
                need[k] = (sem, val)
        for k, (sem, val) in need.items():
            if E.waited.get(k, 0) < val:
                E.eng.wait_ge(sem, val)
                E.waited[k] = val

    @staticmethod
    def _deps(r, w):
        raw, other = [], []
        for b in r:
            if b.dram:
                raw += list(b.dwrites.values())
            else:
                raw += list(b.w.values())
        for b in w:
            if b.dram:
                continue
            other += list(b.w.values())
            other += list(b.r.values())
        return raw, other

    def op(self, E, fn, r=(), w=(), sig=True, small=False):
        raw, other = self._deps(r, w)
        self._wait(E, other, raw)
        ins = fn()
        pos = E.ninstr
        E.ninstr += 1
        if sig:
            ins.then_inc(E.sem, 1)
            E.sigpos.append(pos)
        ev = ("c", E, pos, small)
        for b in r:
            if not b.dram:
                b.r[E.name] = ev
        for b in w:
            b.w[E.name] = ev
            b.r = {}
        return ins

    def dma(self, Q, out, in_, r, w, sb):
        raw, other = self._deps(r, w)
        self._wait(Q, other, raw)
        if sb.dsem is None:
            sb.dsem = self.nc.alloc_semaphore("d_" + sb.name)
            self.nsem += 1
            self.dma_bufs.append(sb)
        Q.eng.dma_start(out=out, in_=in_).then_inc(sb.dsem, 16)
        sb.dcount += 16
        ev = ("d", sb.dsem, sb.dcount)
        for b in r:
            if not b.dram:
                b.r["dma" + str(id(sb.dsem))] = ev
        for b in w:
            if b.dram:
                b.dwrites[id(sb.dsem)] = ev
            else:
                b.w["dma" + str(id(sb.dsem))] = ev
                b.r = {}

    def barrier(self):
        for E in self.engs:
            self.finish(E)
        for E in self.engs:
            for E2 in self.engs:
                if E2 is E or not E2.sigpos:
                    continue
                assert E2.sigpos[-1] == E2.ninstr - 1, f"last instr on {E2.name} not signalled"
                val = len(E2.sigpos)
                if E.waited.get(id(E2.sem), 0) < val:
                    E.eng.wait_ge(E2.sem, val)
                    E.waited[id(E2.sem)] = val

    def finish(self, Q):
        for sb in self.dma_bufs:
            k = id(sb.dsem)
            if sb.dcount and Q.waited.get(k, 0) < sb.dcount:
                Q.eng.wait_ge(sb.dsem, sb.dcount)
                Q.waited[k] = sb.dcount


class Cfg:
    def __init__(self, TO=4096, dbg=False):
        self.TO = TO
        self.NTO = TO // 128
        self.jobs = [dict(name="P", nseg=1, seg0=0, j=0), dict(name="S", nseg=3, seg0=1, j=1)]
        for J in self.jobs:
            J["NF"] = J["nseg"] * self.NTO
            J["NK"] = J["NF"] + 2 + self.NTO
        self.dbg = dbg
        self.serial = False


def build(cfg):
    nc = bass.Bass("TRN2", target_bir_lowering=False)
    K = Kern(nc)
    PE, ACT, DVE, POOL, SP = K.PE, K.ACT, K.DVE, K.POOL, K.SP
    NTO, TO = cfg.NTO, cfg.TO
    ctr = [0]

    def sb(name, shape, dt, stack=None):
        ctr[0] += 1
        nm = f"{name}_{ctr[0]}"
        if stack is None:
            t = nc.alloc_sbuf_tensor(nm, list(shape), dt)
        else:
            t = stack.enter_context(nc.sbuf_tensor(nm, list(shape), dt))
        return Buf(nm, t)

    def ps(name, shape, dt, stack):
        ctr[0] += 1
        nm = f"{name}_{ctr[0]}"
        t = stack.enter_context(nc.psum_tensor(nm, list(shape), dt))
        return Buf(nm, t)

    def dram_in(name, shape, dt=F32):
        return Buf(name, nc.dram_tensor(name, list(shape), dt, kind="ExternalInput").ap(), dram=True)

    def dram_out(name, shape, dt=F32):
        return Buf(name, nc.dram_tensor(name, list(shape), dt, kind="ExternalOutput").ap(), dram=True)

    def dram_scr(name, shape, dt):
        kind = "ExternalOutput" if cfg.dbg else "Internal"
        return Buf(name, nc.dram_tensor(name, list(shape), dt, kind=kind).ap(), dram=True)

    for J in cfg.jobs:
        n = J["name"]
        J["x"] = dram_in("x_" + n, [J["NK"] * 128, D])
        J["y"] = dram_out("y_" + n, [TO, D])
        J["QT"] = dram_scr("QT_" + n, [NH, 128, TO], BF16)
        J["KT"] = dram_scr("KT_" + n, [NH, 128, J["NK"] * 128], BF16)
        J["V"] = dram_scr("V_" + n, [NH, 128, J["NK"], 129], BF16)
        J["HG"] = dram_scr("HG_" + n, [NTO, 128, 7, 512], F32)
        J["HGo"] = dram_scr("HGo_" + n, [J["NF"], 128, 3, 512], F32)
        J["OF"] = dram_scr("OF_" + n, [NTO, 128, 512], F32)
        J["OB"] = dram_scr("OB_" + n, [NTO, 128, 512], F32)
        J["O"] = dram_scr("O_" + n, [NTO, 128, D], BF16)
        if cfg.dbg:
            J["dSF"] = dram_scr("dSF_" + n, [128, 512], F32)
            J["dSB"] = dram_scr("dSB_" + n, [128, 512], F32)
        J["X1"] = dram_scr("X1_" + n, [NTO, 128, D], F32)
        J["H2T"] = dram_scr("H2T_" + n, [8, 128, TO], BF16)
    w_in_d = dram_in("w_in", [D, PROJ])
    w_out_d = dram_in("w_out", [D, D])
    w_mi_d = dram_in("w_mlp_in", [D, DFF])
    w_mo_d = dram_in("w_mlp_out", [DFF, D])
    attn_nw_d = dram_in("attn_norm_w", [1, D])
    mlp_nw_d = dram_in("mlp_norm_w", [1, D])
    qk_nw_d = dram_in("qk_norm_w", [2, 64])
    dlam_d = dram_in("diff_lambda", [4, 64])
    subln_d = dram_in("diff_subln_w", [1, 128])
    relb_d = dram_in("rel_bias", [32, 4])
    hlb_d = dram_in("hgrn_lb", [2, 2, 512])
    hnw_d = dram_in("hgrn_norm_w", [1, 128])
    flags_d = dram_in("flags", [1, NFLAG])
    idxm_d = dram_in("idxm", [128, 384])

    ident_f = sb("identf", [128, 128], F32)
    ident = sb("ident", [128, 128], BF16)
    blk1_f = sb("blk1f", [128, 128], F32)
    blk1 = sb("blk1", [128, 128], BF16)
    tri = {k: sb("tri" + k, [128, 128], F32) for k in ("fi", "fx", "bi", "bx")}
    ones_c = sb("ones", [128, 1], F32)
    eps_c = sb("eps", [128, 1], F32)
    eps64_c = sb("eps64", [128, 1], F32)
    flg = sb("flg", [128, NFLAG], F32)
    nflg = sb("nflg", [128, NFLAG], F32)
    relb = sb("relb", [128, 128], F32)
    kscale = sb("kscale", [128, 1], F32)
    hbias = sb("hbias", [128, 16], F32)
    lam_t = sb("lam", [128, 4], F32)
    subln = sb("subln", [128, 128], F32)
    hnw = sb("hnw", [128, 512], F32)
    oml = [sb(f"oml{d}", [128, 512], F32) for d in range(2)]
    bfar = sb("bfar", [128, 32], F32)
    anw = sb("anw", [128, 8], F32)
    mnw = sb("mnw", [128, 8], F32)

    st_setup = ExitStack()
    dl = sb("dl", [128, 256], F32, st_setup)
    lbt = sb("lbt", [128, 2048], F32, st_setup)
    qkb = sb("qkb", [128, 128], F32, st_setup)
    wprod = sb("wprod", [128, 128], F32, st_setup)
    wbc = [sb("wbc", [128, D], F32, st_setup) for _ in range(2)]
    junk_s = sb("junk_s", [128, 64], F32, st_setup)

    def pool_op(fn, w, r=()):
        K.op(POOL, fn, r=r, w=w)

    g = nc.gpsimd
    pool_op(lambda: g.memset(ident_f[:], 1.0), [ident_f])
    pool_op(lambda: g.affine_select(out=ident_f[:], in_=ident_f[:], pattern=[[-1, 128]], compare_op=ALU.is_equal,
                                    fill=0.0, base=0, channel_multiplier=1), [ident_f], [ident_f])
    pool_op(lambda: g.memset(blk1_f[:], 0.0), [blk1_f])
    pool_op(lambda: g.memset(blk1_f[0:64, 0:64], 1.0), [blk1_f])
    pool_op(lambda: g.memset(blk1_f[64:128, 64:128], 1.0), [blk1_f])
    for k, (cm, pat, base, cmp) in dict(fi=(-1, 1, 0, ALU.is_ge), fx=(1, -1, 0, ALU.is_gt),
                                         bi=(1, -1, 0, ALU.is_ge), bx=(-1, 1, 0, ALU.is_gt)).items():
        t = tri[k]
        pool_op(lambda t=t: g.memset(t[:], 0.0), [t])
        pool_op(lambda t=t: g.memset(t[0:64, 0:64], 1.0), [t])
        pool_op(lambda t=t: g.memset(t[64:128, 64:128], 1.0), [t])
        pool_op(lambda t=t, cm=cm, pat=pat, cmp=cmp: g.affine_select(
            out=t[:], in_=t[:], pattern=[[pat, 128]], compare_op=cmp, fill=0.0, base=0, channel_multiplier=cm), [t], [t])
    pool_op(lambda: g.memset(ones_c[:], 1.0), [ones_c])
    pool_op(lambda: g.memset(eps_c[:], EPS), [eps_c])
    pool_op(lambda: g.memset(eps64_c[:], 64.0 * EPS), [eps64_c])

    def bload(dst, src_ap, n):
        K.dma(SP, dst[:, 0:n], src_ap.to_broadcast([128, n]), r=[], w=[dst], sb=dst)

    bload(flg, flags_d[0:1, :], NFLAG)
    bload(relb, relb_d.t.rearrange("(o b) h -> o (b h)", o=1), 128)
    bload(dl, dlam_d.t.rearrange("(o b) h -> o (b h)", o=1), 256)
    bload(subln, subln_d[0:1, :], 128)
    bload(lbt, hlb_d.t.rearrange("(o a) b c -> o (a b c)", o=1), 2048)
    for h in range(4):
        K.dma(SP, hnw[:, h * 128:(h + 1) * 128], hnw_d[0:1, :].to_broadcast([128, 128]), r=[], w=[hnw], sb=hnw)
    bload(qkb, qk_nw_d.t.rearrange("(o b) h -> o (b h)", o=1), 128)
    bload(wbc[0], attn_nw_d[0:1, :], D)
    bload(wbc[1], mlp_nw_d[0:1, :], D)

    v = nc.vector
    a = nc.scalar
    K.op(DVE, lambda: v.tensor_copy(out=ident[:], in_=ident_f[:]), r=[ident_f], w=[ident])
    K.op(DVE, lambda: v.tensor_copy(out=blk1[:], in_=blk1_f[:]), r=[blk1_f], w=[blk1])
    K.op(DVE, lambda: v.tensor_scalar(out=nflg[:], in0=flg[:], scalar1=-1.0, scalar2=1.0, op0=ALU.mult, op1=ALU.add),
         r=[flg], w=[nflg], small=True)
    for half in range(2):
        K.op(DVE, lambda half=half: v.scalar_tensor_tensor(out=wprod[:, half * 64:(half + 1) * 64], in0=qkb[:, 0:64], scalar=8.0,
                                                           in1=qkb[:, 64:128], op0=ALU.mult, op1=ALU.mult),
             r=[qkb], w=[wprod], small=True)
    K.op(DVE, lambda: v.tensor_tensor(out=wprod[:, :], in0=wprod[:, :], in1=ident_f[:, :], op=ALU.mult),
         r=[wprod, ident_f], w=[wprod], small=True)
    K.op(DVE, lambda: v.tensor_reduce(out=kscale[:, 0:1], in_=wprod[:, :], axis=AX.X, op=ALU.add), r=[wprod], w=[kscale], small=True)
    with ExitStack() as st0:
        wtmp0 = sb("wtmp0", [128, 8, 128], F32, st0)
        for wi, dst in ((0, anw), (1, mnw)):
            K.op(DVE, lambda wi=wi: v.tensor_tensor(out=wtmp0[:, :, :], in0=wbc[wi][:, :].rearrange("p (c j) -> p c j", c=8),
                                                    in1=ident_f[:, :].unsqueeze(1).to_broadcast([128, 8, 128]), op=ALU.mult),
                 r=[wbc[wi], ident_f], w=[wtmp0])
            K.op(DVE, lambda dst=dst: v.tensor_reduce(out=dst[:, :], in_=wtmp0[:, :, :], axis=AX.X, op=ALU.add),
                 r=[wtmp0], w=[dst], small=True)
    for jj_ in range(2):
        for which, (bk, fm_) in enumerate(((15, FL_PREVM), (31, FL_NEXTM))):
            c0_ = (jj_ * 2 + which) * 4
            K.op(DVE, lambda c0_=c0_, bk=bk, fm_=fm_, jj_=jj_: v.tensor_scalar(
                out=hbias[:, c0_:c0_ + 4], in0=relb[:, bk * 4:bk * 4 + 4], scalar1=flg[:, fm_ + jj_:fm_ + jj_ + 1], scalar2=None,
                op0=ALU.add), r=[relb, flg], w=[hbias], small=True)
    for i in range(2):
        K.op(DVE, lambda i=i: v.tensor_tensor(out=junk_s[:], in0=dl[:, 128 * i:128 * i + 64],
                                              in1=dl[:, 128 * i + 64:128 * i + 128], op=ALU.mult),
             r=[dl], w=[junk_s], small=True)
        K.op(DVE, lambda i=i: v.tensor_reduce(out=lam_t[:, i:i + 1], in_=junk_s[:], axis=AX.X, op=ALU.add),
             r=[junk_s], w=[lam_t], small=True)
    K.op(ACT, lambda: a.activation(out=lam_t[:, 2:4], in_=lam_t[:, 0:2], func=AF.Exp), r=[lam_t], w=[lam_t], small=True)
    K.op(DVE, lambda: v.tensor_tensor(out=lam_t[:, 0:1], in0=lam_t[:, 3:4], in1=lam_t[:, 2:3], op=ALU.subtract),
         r=[lam_t], w=[lam_t], small=True)
    K.op(DVE, lambda: v.tensor_scalar(out=lam_t[:, 0:1], in0=lam_t[:, 0:1], scalar1=-LAM_INIT, scalar2=None, op0=ALU.add),
         r=[lam_t], w=[lam_t], small=True)
    for d_ in range(2):
        K.op(DVE, lambda d_=d_: v.tensor_tensor(out=oml[d_][:], in0=lbt[:, d_ * 1024:d_ * 1024 + 512],
                                                in1=lbt[:, d_ * 1024 + 512:d_ * 1024 + 1024], op=ALU.subtract),
             r=[lbt], w=[oml[d_]])
        K.op(ACT, lambda d_=d_: a.activation(out=oml[d_][:], in_=oml[d_][:], func=AF.Exp), r=[oml[d_]], w=[oml[d_]], small=True)
        K.op(DVE, lambda d_=d_: v.tensor_scalar(out=oml[d_][:], in0=oml[d_][:], scalar1=1.0, scalar2=None, op0=ALU.add),
             r=[oml[d_]], w=[oml[d_]], small=True)
        K.op(DVE, lambda d_=d_: v.reciprocal(out=oml[d_][:], in_=oml[d_][:]), r=[oml[d_]], w=[oml[d_]], small=True)
    for s in range(4):
        K.op(DVE, lambda s=s: v.tensor_scalar(out=bfar[:, 4 * s:4 * s + 4], in0=relb[:, 60:64],
                                              scalar1=flg[:, FL_DIRF + s:FL_DIRF + s + 1], scalar2=None, op0=ALU.mult),
             r=[relb, flg], w=[bfar], small=True)
        K.op(DVE, lambda s=s: v.scalar_tensor_tensor(out=bfar[:, 4 * s:4 * s + 4], in0=relb[:, 124:128],
                                                     scalar=nflg[:, FL_DIRF + s:FL_DIRF + s + 1], in1=bfar[:, 4 * s:4 * s + 4],
                                                     op0=ALU.mult, op1=ALU.add), r=[relb, nflg, bfar], w=[bfar], small=True)
        K.op(DVE, lambda s=s: v.tensor_scalar(out=bfar[:, 16 + 4 * s:16 + 4 * s + 4], in0=bfar[:, 4 * s:4 * s + 4],
                                              scalar1=flg[:, FL_LASTM + s:FL_LASTM + s + 1], scalar2=None, op0=ALU.add),
             r=[bfar, flg], w=[bfar], small=True)

    K.barrier()
    st_setup.close()

    with ExitStack() as st:
        win = sb("win", [128, 8, PROJ], BF16, st)
        wsel = sb("wsel", [128, 8, 512], BF16, st)
        xt = [sb("xt", [128, D], F32, st) for _ in range(4)]
        hb = [sb("hb", [128, D], BF16, st) for _ in range(8)]
        sqj = sb("sqj", [128, D], BF16, st)
        ss = [sb("ss", [128, 8], F32, st) for _ in range(3)]
        hT = [sb("hT", [128, 8, 512], BF16, st) for _ in range(2)]
        sqb = [sb("sqb", [128, 512], BF16, st) for _ in range(2)]
        rs = [sb("rs", [128, 512], F32, st) for _ in range(2)]
        ktst = [sb("ktst", [128, NH, 512], BF16, st) for _ in range(2)]
        qtst = [sb("qtst", [128, NH, 512], BF16, st) for _ in range(2)]
        vst = [sb("vst", [128, NH, 4, 129], BF16, st) for _ in range(2)]
        hgst = [sb("hgst", [128, 512], F32, st) for _ in range(8)]
        omlselA = sb("omlselA", [128, 512], F32, st)
        sg = [sb("sg", [128, 512], F32, st) for _ in range(2)]
        tp_ps = [ps("tp", [128, 8, 128], BF16, st) for _ in range(2)]
        fm_ps = [ps("fm", [128, 512], F32, st) for _ in range(2)]
        ssb_ps = ps("ssb", [128, 512], F32, st)
        tm_ps = [ps("tm", [128, 512], F32, st) for _ in range(3)]

        for b in vst:
            K.op(POOL, lambda b=b: g.memset(b[:], 1.0), w=[b])

        w_view = w_in_d.t.rearrange("(c p) n -> p c n", p=128)
        cnt = 0
        for c in range(8):
            for q4 in range(4):
                col = q4 * 1024
                stg = xt[cnt % 3]
                K.dma(SP, stg[:, :], w_view[:, c, col:col + 1024], r=[w_in_d], w=[stg], sb=stg)
                if cnt % 2 == 0:
                    K.op(DVE, lambda c=c, col=col, stg=stg: v.tensor_scalar(
                        out=win[:, c, col:col + 1024], in0=stg[:, :], scalar1=anw[:, c:c + 1], scalar2=None,
                        op0=ALU.mult), r=[stg, anw], w=[win])
                else:
                    K.op(ACT, lambda c=c, col=col, stg=stg: a.activation(
                        out=win[:, c, col:col + 1024], in_=stg[:, :], func=AF.Copy, scale=anw[:, c:c + 1]),
                        r=[stg, anw], w=[win])
                cnt += 1

        gi = [0]
        sti = [0]
        hgi = [0]
        fmi = [0]
        tmi = [0]

        def norm_a(J, rows):
            nt = len(rows)
            i0 = gi[0]
            gi[0] += 4
            s_ = ss[(i0 // 4) % 3]
            for t, trow in enumerate(rows):
                x_ = xt[(i0 + t) % 4]
                K.dma(SP, x_[:, :], J["x"].t[trow * 128:(trow + 1) * 128, :], r=[J["x"]], w=[x_], sb=x_)
                K.op(ACT, lambda x_=x_, t=t: a.activation(out=sqj[:, :], in_=x_[:, :], func=AF.Square, accum_out=s_[:, t:t + 1]),
                     r=[x_], w=[sqj, s_])
            K.op(ACT, lambda: a.activation(out=s_[:, 4:4 + nt], in_=s_[:, 0:nt], func=AF.Ln, bias=eps_c[:], scale=1.0 / D),
                 r=[s_, eps_c], w=[s_])
            K.op(ACT, lambda: a.activation(out=s_[:, 4:4 + nt], in_=s_[:, 4:4 + nt], func=AF.Exp, scale=-0.5), r=[s_], w=[s_])
            for t in range(nt):
                x_, h_ = xt[(i0 + t) % 4], hb[(i0 + t) % 8]
                K.op(DVE, lambda x_=x_, h_=h_, t=t: v.tensor_scalar(out=h_[:, :], in0=x_[:, :], scalar1=s_[:, 4 + t:5 + t], scalar2=None,
                                                                    op0=ALU.mult), r=[x_, s_], w=[h_])
            return i0

        def norm_b(i0, nt, hslot):
            for t in range(nt):
                i = i0 + t
                h_ = hb[i % 8]
                tp = tp_ps[i % 2]
                for c in range(8):
                    K.op(PE, lambda c=c, h_=h_, tp=tp: nc.tensor.transpose(tp[:, c, :], h_[:, c * 128:(c + 1) * 128], ident[:]),
                         r=[h_, ident], w=[tp], sig=(c == 7))
                K.op(DVE, lambda tp=tp, t=t: v.tensor_copy(out=hT[hslot][:, :, t * 128:(t + 1) * 128], in_=tp[:, :, :]),
                     r=[tp], w=[hT[hslot]])

        def fm_block(hslot, N, col, is_k, dst, h):
            i = fmi[0]
            fmi[0] += 1
            p_ = fm_ps[i % 2]
            for c in range(8):
                K.op(PE, lambda c=c: nc.tensor.matmul(p_[:, 0:N], lhsT=win[:, c, col:col + 128], rhs=hT[hslot][:, c, 0:N],
                                                      start=(c == 0), stop=(c == 7)),
                     r=[win, hT[hslot]], w=[p_], sig=(c == 7))
            q_, r_ = sqb[i % 2], rs[i % 2]
            K.op(ACT, lambda: a.activation(out=q_[:, 0:N], in_=p_[:, 0:N], func=AF.Square), r=[p_], w=[q_])
            K.op(PE, lambda: nc.tensor.matmul(ssb_ps[:, 0:N], lhsT=blk1[:], rhs=q_[:, 0:N], start=True, stop=True),
                 r=[blk1, q_], w=[ssb_ps])
            K.op(ACT, lambda: a.activation(out=r_[:, 0:N], in_=ssb_ps[:, 0:N], func=AF.Ln, bias=eps64_c[:], scale=1.0),
                 r=[ssb_ps, eps64_c], w=[r_])
            K.op(ACT, lambda: a.activation(out=r_[:, 0:N], in_=r_[:, 0:N], func=AF.Exp, scale=-0.5), r=[r_], w=[r_],
                 small=(N < 256))
            if is_k:
                K.op(DVE, lambda: v.scalar_tensor_tensor(out=dst[:, h, 0:N], in0=p_[:, 0:N], scalar=kscale[:, 0:1],
                                                         in1=r_[:, 0:N], op0=ALU.mult, op1=ALU.mult),
                     r=[p_, kscale, r_], w=[dst])
            else:
                K.op(DVE, lambda: v.tensor_tensor(out=dst[:, h, 0:N], in0=p_[:, 0:N], in1=r_[:, 0:N], op=ALU.mult),
                     r=[p_, r_], w=[dst])

        def tm_matmul(hslot, t, wbuf, col):
            i = tmi[0]
            tmi[0] += 1
            p_ = tm_ps[i % 3]
            for c in range(8):
                K.op(PE, lambda c=c: nc.tensor.matmul(p_[:, :], lhsT=hT[hslot][:, c, t * 128:(t + 1) * 128],
                                                      rhs=wbuf[:, c, col:col + 512], start=(c == 0), stop=(c == 7)),
                     r=[hT[hslot], wbuf], w=[p_], sig=(c == 7))
            return p_

        def hg_store(src_ap_fn, eng, dst_ap, dstbuf, extra_r=()):
            i = hgi[0]
            hgi[0] += 1
            s_ = hgst[i % 8]
            src_ap_fn(s_)
            K.dma(POOL, dst_ap, s_[:, :], r=[s_], w=[dstbuf], sb=s_)

        def silu_block(p_, scale, dst_ap, dstbuf):
            i = hgi[0]
            s1 = sg[i % 2]
            K.op(ACT, lambda: a.activation(out=s1[:, :], in_=p_[:, :], func=AF.Exp, scale=-1.0), r=[p_], w=[s1])
            K.op(ACT, lambda: a.activation(out=s1[:, :], in_=s1[:, :], func=AF.Ln, bias=ones_c[:], scale=1.0),
                 r=[s1, ones_c], w=[s1])
            K.op(ACT, lambda: a.activation(out=s1[:, :], in_=s1[:, :], func=AF.Exp, scale=-1.0), r=[s1], w=[s1])

            def ev(s_):
                K.op(DVE, lambda: v.scalar_tensor_tensor(out=s_[:, :], in0=p_[:, :], scalar=scale, in1=s1[:, :],
                                                         op0=ALU.mult, op1=ALU.mult), r=[p_, s1], w=[s_])
            hg_store(ev, DVE, dst_ap, dstbuf)

        def gate_block(p_, omlb, dst_k, dst_lf, dstbuf):
            s1 = sg[hgi[0] % 2]
            K.op(ACT, lambda: a.activation(out=s1[:, :], in_=p_[:, :], func=AF.Exp), r=[p_], w=[s1])
            K.op(ACT, lambda: a.activation(out=s1[:, :], in_=s1[:, :], func=AF.Ln, bias=ones_c[:], scale=1.0),
                 r=[s1, ones_c], w=[s1])
            K.op(ACT, lambda: a.activation(out=s1[:, :], in_=s1[:, :], func=AF.Exp, scale=-1.0), r=[s1], w=[s1])
            ks = hgst[hgi[0] % 8]
            hgi[0] += 1
            K.op(DVE, lambda: v.tensor_tensor(out=ks[:, :], in0=s1[:, :], in1=omlb[:, :], op=ALU.mult), r=[s1, omlb], w=[ks])
            K.dma(POOL, dst_k, ks[:, :], r=[ks], w=[dstbuf], sb=ks)
            ls = hgst[hgi[0] % 8]
            hgi[0] += 1
            K.op(ACT, lambda: a.activation(out=ls[:, :], in_=ks[:, :], func=AF.Ln, bias=ones_c[:], scale=-1.0),
                 r=[ks, ones_c], w=[ls])
            K.dma(POOL, dst_lf, ls[:, :], r=[ls], w=[dstbuf], sb=ls)

        def copy_block(p_, eng, dst_ap, dstbuf):
            def ev(s_):
                if eng is DVE:
                    K.op(DVE, lambda: v.tensor_copy(out=s_[:, :], in_=p_[:, :]), r=[p_], w=[s_])
                else:
                    K.op(ACT, lambda: a.copy(out=s_[:, :], in_=p_[:, :]), r=[p_], w=[s_])
            hg_store(ev, eng, dst_ap, dstbuf)

        def super_tile(J, kind, rows, kt0, own0=None, seg=None, far0=None, i0=None, nxt=None):
            si = sti[0]
            sti[0] += 1
            hslot = si % 2
            nt = len(rows)
            N = nt * 128
            norm_b(i0, nt, hslot)
            nxt_i0 = norm_a(*nxt) if nxt is not None else None
            kst, qst, vs_ = ktst[si % 2], qtst[si % 2], vst[si % 2]
            for h in range(NH):
                fm_block(hslot, N, C_KA + h * 128, True, kst, h)
            for h in range(NH):
                K.dma(POOL, J["KT"].t[h, :, kt0 * 128:kt0 * 128 + N], kst[:, h, 0:N], r=[kst], w=[J["KT"]], sb=kst)
            if kind == "own":
                for h in range(NH):
                    fm_block(hslot, N, C_QA + h * 128, False, qst, h)
                for h in range(NH):
                    K.dma(POOL, J["QT"].t[h, :, own0 * 128:own0 * 128 + N], qst[:, h, 0:N], r=[qst], w=[J["QT"]], sb=qst)
            for t in range(nt):
                p_ = tm_matmul(hslot, t, win, C_VA)
                K.op(DVE, lambda p_=p_, t=t: v.tensor_copy(out=vs_[:, :, t, 0:128], in_=p_[:, :].rearrange("p (h d) -> p h d", h=NH)),
                     r=[p_], w=[vs_])
                if kind == "own":
                    hgd = J["HG"]
                    ti = own0 + t
                    p_ = tm_matmul(hslot, t, win, C_QH)
                    silu_block(p_, 128.0 ** -0.5, hgd.t[ti, :, 0, :], hgd)
                    p_ = tm_matmul(hslot, t, win, C_FF)
                    gate_block(p_, oml[0], hgd.t[ti, :, 1, :], hgd.t[ti, :, 2, :], hgd)
                    p_ = tm_matmul(hslot, t, win, C_FB)
                    gate_block(p_, oml[1], hgd.t[ti, :, 3, :], hgd.t[ti, :, 4, :], hgd)
                    p_ = tm_matmul(hslot, t, win, C_IH)
                    copy_block(p_, DVE, hgd.t[ti, :, 5, :], hgd)
                    p_ = tm_matmul(hslot, t, win, C_GH)
                    silu_block(p_, 1.0, hgd.t[ti, :, 6, :], hgd)
                elif kind == "far":
                    hgd = J["HGo"]
                    ti = far0 + t
                    p_ = tm_matmul(hslot, t, wsel, 0)
                    gate_block(p_, omlselA, hgd.t[ti, :, 0, :], hgd.t[ti, :, 1, :], hgd)
                    p_ = tm_matmul(hslot, t, win, C_IH)
                    copy_block(p_, DVE, hgd.t[ti, :, 2, :], hgd)
            for h in range(NH):
                K.dma(POOL, J["V"].t[h, :, kt0:kt0 + nt, :], vs_[:, h, 0:nt, :], r=[vs_], w=[J["V"]], sb=vs_)
            return nxt_i0

        stl = []
        for J in cfg.jobs:
            NF = J["NF"]
            for s_ in range(J["nseg"]):
                for t0 in range(0, NTO, 4):
                    f0 = s_ * NTO + t0
                    stl.append(dict(J=J, kind="far", rows=list(range(f0, f0 + 4)), kt0=f0, own0=None, far0=f0, seg=s_,
                                    first=(t0 == 0)))
            stl.append(dict(J=J, kind="halo", rows=[NF, NF + 1], kt0=NF, own0=None, far0=None, seg=None, first=False))
            for t0 in range(0, NTO, 4):
                stl.append(dict(J=J, kind="own", rows=list(range(NF + 2 + t0, NF + 2 + t0 + 4)), kt0=NF + 2 + t0, own0=t0,
                                far0=None, seg=None, first=False))
        i0 = norm_a(stl[0]["J"], stl[0]["rows"])
        for k_, T in enumerate(stl):
            if T["first"]:
                fs = FL_DIRF + T["J"]["seg0"] + T["seg"]
                K.op(DVE, lambda fs=fs: v.tensor_scalar(out=wsel[:, :, :], in0=win[:, :, C_FF:C_FF + 512],
                                                        scalar1=flg[:, fs:fs + 1], scalar2=None, op0=ALU.mult),
                     r=[win, flg], w=[wsel])
                K.op(DVE, lambda fs=fs: v.scalar_tensor_tensor(out=wsel[:, :, :], in0=win[:, :, C_FB:C_FB + 512],
                                                               scalar=nflg[:, fs:fs + 1], in1=wsel[:, :, :],
                                                               op0=ALU.mult, op1=ALU.add), r=[win, nflg, wsel], w=[wsel])
                K.op(DVE, lambda fs=fs: v.tensor_scalar(out=omlselA[:, :], in0=oml[0][:, :], scalar1=flg[:, fs:fs + 1],
                                                        scalar2=None, op0=ALU.mult), r=[oml[0], flg], w=[omlselA])
                K.op(DVE, lambda fs=fs: v.scalar_tensor_tensor(out=omlselA[:, :], in0=oml[1][:, :], scalar=nflg[:, fs:fs + 1],
                                                               in1=omlselA[:, :], op0=ALU.mult, op1=ALU.add),
                     r=[oml[1], nflg, omlselA], w=[omlselA])
            nxt = (stl[k_ + 1]["J"], stl[k_ + 1]["rows"]) if k_ + 1 < len(stl) else None
            i0 = super_tile(T["J"], T["kind"], T["rows"], T["kt0"], own0=T["own0"], seg=T["seg"], far0=T["far0"], i0=i0, nxt=nxt)

    K.barrier()

    with ExitStack() as st:
        def make_bufs():
            NB = 2
            S = sb("S", [128, 512], F32, st)
            Sbf = [sb("Sbf", [128, 512], BF16, st) for _ in range(2)]
            SF = sb("SF", [128, 512], F32, st)
            SB = sb("SB", [128, 512], F32, st)
            omlsel = sb("omlsel", [128, 512], F32, st)
            trixsel = sb("trixsel", [128, 128], F32, st)
            vin = [sb("vin", [128, 512], F32, st) for _ in range(NB)]
            qin = [sb("qin", [128, 512], F32, st) for _ in range(NB)]
            kf = [sb("kf", [128, 512], F32, st) for _ in range(NB)]
            lf = [sb("lf", [128, 512], F32, st) for _ in range(NB)]
            eb = [sb("eb", [128, 512], F32, st) for _ in range(NB)]
            enb = [sb("enb", [128, 512], F32, st) for _ in range(NB)]
            ebl = [sb("ebl", [128, 512], F32, st) for _ in range(NB)]
            qt = [sb("qt", [128, 512], BF16, st) for _ in range(NB)]
            kt_ = [sb("ktt", [128, 512], BF16, st) for _ in range(NB)]
            kk = [sb("kk", [128, 512], BF16, st) for _ in range(NB)]
            vbf = [sb("vbf", [128, 512], BF16, st) for _ in range(NB)]
            dcy = [sb("dcy", [128, 8], F32, st) for _ in range(NB)]
            qkT = [sb("qkT", [128, 8, 128], BF16, st) for _ in range(NB)]
            atsb = [sb("atsb", [128, NH, 128], BF16, st) for _ in range(NB)]
            ost = [sb("ost", [128, 512], F32, st) for _ in range(NB)]
            pp_ps = ps("ppps", [128, 512], F32, st)
            tpb_ps = ps("tpb", [128, 8, 128], BF16, st)
            ao_ps = ps("aops", [128, 512], F32, st)
            ds_ps = ps("dsps", [128, 512], F32, st)
            bi = [0]
            sbi = [0]

            def hg_prep(J, own, src, tix, trii, trix, omlb, d_idx):
                i = bi[0] % NB
                bi[0] += 1
                v_, q_ = vin[i], qin[i]
                k_, l_ = kf[i], lf[i]
                if own:
                    K.dma(SP, k_[:, :], src.t[tix, :, 1 + 2 * d_idx, :], r=[src], w=[k_], sb=k_)
                    K.dma(SP, l_[:, :], src.t[tix, :, 2 + 2 * d_idx, :], r=[src], w=[l_], sb=l_)
                    K.dma(SP, v_[:, :], src.t[tix, :, 5, :], r=[src], w=[v_], sb=v_)
                    K.dma(SP, q_[:, :], src.t[tix, :, 0, :], r=[src], w=[q_], sb=q_)
                else:
                    K.dma(SP, k_[:, :], src.t[tix, :, 0, :], r=[src], w=[k_], sb=k_)
                    K.dma(SP, l_[:, :], src.t[tix, :, 1, :], r=[src], w=[l_], sb=l_)
                    K.dma(SP, v_[:, :], src.t[tix, :, 2, :], r=[src], w=[v_], sb=v_)
                yield
                K.op(DVE, lambda: v.tensor_copy(out=vbf[i][:, :], in_=v_[:, :]), r=[v_], w=[vbf[i]])
                K.op(PE, lambda: nc.tensor.matmul(pp_ps[:, :], lhsT=trix[:], rhs=l_[:, :], start=True, stop=True),
                     r=[trix, l_], w=[pp_ps])
                yield
                K.op(ACT, lambda: a.activation(out=ebl[i][:, :], in_=pp_ps[:, :], func=AF.Exp), r=[pp_ps], w=[ebl[i]])
                yield
                K.op(DVE, lambda: v.tensor_tensor(out=kk[i][:, :], in0=k_[:, :], in1=ebl[i][:, :], op=ALU.mult),
                     r=[k_, ebl[i]], w=[kk[i]])
                if own:
                    K.op(PE, lambda: nc.tensor.matmul(pp_ps[:, :], lhsT=trii[:], rhs=l_[:, :], start=True, stop=True),
                         r=[trii, l_], w=[pp_ps])
                    yield
                    K.op(ACT, lambda: a.activation(out=eb[i][:, :], in_=pp_ps[:, :], func=AF.Exp), r=[pp_ps], w=[eb[i]])
                    K.op(ACT, lambda: a.activation(out=enb[i][:, :], in_=pp_ps[:, :], func=AF.Exp, scale=-1.0),
                         r=[pp_ps], w=[enb[i]])
                    yield
                    K.op(DVE, lambda: v.tensor_tensor(out=qt[i][:, :], in0=q_[:, :], in1=eb[i][:, :], op=ALU.mult),
                         r=[q_, eb[i]], w=[qt[i]])
                    K.op(DVE, lambda: v.tensor_tensor(out=kt_[i][:, :], in0=k_[:, :], in1=enb[i][:, :], op=ALU.mult),
                         r=[k_, enb[i]], w=[kt_[i]])
                    yield
                    for h in range(NH):
                        K.op(PE, lambda h=h: nc.tensor.transpose(tpb_ps[:, h, :], qt[i][:, h * 128:(h + 1) * 128], ident[:]),
                             r=[qt[i], ident], w=[tpb_ps], sig=False)
                    for h in range(NH):
                        K.op(PE, lambda h=h: nc.tensor.transpose(tpb_ps[:, 4 + h, :], kt_[i][:, h * 128:(h + 1) * 128], ident[:]),
                             r=[kt_[i], ident], w=[tpb_ps], sig=(h == NH - 1))
                    yield
                    K.op(DVE, lambda: v.tensor_copy(out=qkT[i][:, :, :], in_=tpb_ps[:, :, :]), r=[tpb_ps], w=[qkT[i]])
                    yield
                    for h in range(NH):
                        K.op(PE, lambda h=h: nc.tensor.matmul(pp_ps[:, h * 128:(h + 1) * 128], lhsT=qkT[i][:, 4 + h, :],
                                                              rhs=qkT[i][:, h, :], start=True, stop=True),
                             r=[qkT[i]], w=[pp_ps], sig=(h == NH - 1))
                    yield
                    K.op(DVE, lambda: v.tensor_tensor(out=atsb[i][:, :, :], in0=pp_ps[:, :].rearrange("p (h c) -> p h c", h=NH),
                                                      in1=trii[:, :].unsqueeze(1).to_broadcast([128, NH, 128]), op=ALU.mult),
                         r=[pp_ps, trii], w=[atsb[i]])
                yield
                for j in range(2):
                    for h in range(NH):
                        K.op(PE, lambda j=j, h=h: nc.tensor.matmul(
                            pp_ps[:, j * 4 + h:j * 4 + h + 1], lhsT=l_[64 * j:64 * j + 64, h * 128:(h + 1) * 128],
                            rhs=ones_c[64 * j:64 * j + 64, 0:1], start=True, stop=True),
                            r=[l_, ones_c], w=[pp_ps], sig=(j == 1 and h == NH - 1))
                yield
                K.op(ACT, lambda: a.activation(out=dcy[i][:, :], in_=pp_ps[:, 0:8], func=AF.Exp), r=[pp_ps], w=[dcy[i]])
                yield
                return i

            def hg_recur(J, own, i, tix, chunk_order, trii, d_idx):
                if own:
                    for h in range(NH):
                        hs = slice(h * 128, (h + 1) * 128)
                        K.op(PE, lambda h=h, hs=hs: nc.tensor.matmul(ao_ps[:, hs], lhsT=atsb[i][:, h, :], rhs=vbf[i][:, hs],
                                                                     start=(h == 0), stop=False, skip_group_check=True),
                             r=[atsb[i], vbf[i]], w=[ao_ps], sig=(h == NH - 1))
                    yield
                for j in chunk_order:
                    pr = slice(64 * j, 64 * j + 64)
                    if own:
                        sb_cur = Sbf[sbi[0] % 2]
                        for h in range(NH):
                            hs = slice(h * 128, (h + 1) * 128)
                            K.op(PE, lambda h=h, hs=hs: nc.tensor.matmul(ao_ps[pr, hs], lhsT=qkT[i][:, h, pr], rhs=sb_cur[:, hs],
                                                                         start=False, stop=True, skip_group_check=True),
                                 r=[qkT[i], sb_cur], w=[ao_ps], sig=False)
                    for h in range(NH):
                        hs = slice(h * 128, (h + 1) * 128)
                        K.op(PE, lambda hs=hs: nc.tensor.matmul(ds_ps[:, hs], lhsT=kk[i][pr, hs], rhs=vbf[i][pr, hs],
                                                                start=True, stop=True), r=[kk[i], vbf[i]], w=[ds_ps],
                             sig=(h == NH - 1))
                    yield
                    for h in range(NH):
                        hs = slice(h * 128, (h + 1) * 128)
                        K.op(DVE, lambda h=h, hs=hs: v.scalar_tensor_tensor(
                            out=S[:, hs], in0=S[:, hs], scalar=dcy[i][:, j * 4 + h:j * 4 + h + 1], in1=ds_ps[:, hs],
                            op0=ALU.mult, op1=ALU.add), r=[S, dcy[i], ds_ps], w=[S])
                    if own:
                        sbi[0] += 1
                        nxt = Sbf[sbi[0] % 2]
                        K.op(DVE, lambda nxt=nxt: v.tensor_copy(out=nxt[:, :], in_=S[:, :]), r=[S], w=[nxt])
                    yield
                if own:
                    K.op(ACT, lambda: a.copy(out=ost[i][:, :], in_=ao_ps[:, :]), r=[ao_ps], w=[ost[i]])
                    dst = J["OF"] if d_idx == 0 else J["OB"]
                    K.dma(POOL, dst.t[tix, :, :], ost[i][:, :], r=[ost[i]], w=[dst], sb=ost[i])
                    yield

            def run_items(J, items):
                pend = None
                for it in items + [None]:
                    g_prep = hg_prep(J, it["own"], it["src"], it["tix"], it["trii"], it["trix"], it["oml"], it["d_idx"]) \
                        if it is not None else None
                    g_rec = None
                    if pend is not None:
                        pit, pi = pend
                        if pit.get("pre"):
                            pit["pre"]()
                        g_rec = hg_recur(J, pit["own"], pi, pit["tix"], pit["chunk_order"], pit["trii"], pit["d_idx"])
                    slot = None
                    while g_prep is not None or g_rec is not None:
                        if g_prep is not None:
                            try:
                                next(g_prep)
                            except StopIteration as e_:
                                slot = e_.value
                                g_prep = None
                        if g_rec is not None:
                            try:
                                next(g_rec)
                            except StopIteration:
                                g_rec = None
                        yield
                    if pend is not None and pend[0].get("post"):
                        pend[0]["post"]()
                    pend = (it, slot) if it is not None else None

            def pass_others(J, done):
                K.op(POOL, lambda: g.memset(S[:, :], 0.0), w=[S])
                K.op(POOL, lambda: g.memset(SF[:, :], 0.0), w=[SF])
                K.op(POOL, lambda: g.memset(SB[:, :], 0.0), w=[SB])
                yield
                for s in range(J["nseg"]):
                    sg_ = J["seg0"] + s
                    fF, fC = FL_DIRF + sg_, FL_CONT + sg_
                    K.op(DVE, lambda fF=fF: v.tensor_scalar(out=omlsel[:, :], in0=oml[0][:, :], scalar1=flg[:, fF:fF + 1],
                                                            scalar2=None, op0=ALU.mult), r=[oml[0], flg], w=[omlsel])
                    K.op(DVE, lambda fF=fF: v.scalar_tensor_tensor(out=omlsel[:, :], in0=oml[1][:, :], scalar=nflg[:, fF:fF + 1],
                                                                   in1=omlsel[:, :], op0=ALU.mult, op1=ALU.add),
                         r=[oml[1], nflg, omlsel], w=[omlsel])
                    K.op(DVE, lambda fF=fF: v.tensor_scalar(out=trixsel[:, :], in0=tri["fx"][:, :], scalar1=flg[:, fF:fF + 1],
                                                            scalar2=None, op0=ALU.mult), r=[tri["fx"], flg], w=[trixsel])
                    K.op(DVE, lambda fF=fF: v.scalar_tensor_tensor(out=trixsel[:, :], in0=tri["bx"][:, :], scalar=nflg[:, fF:fF + 1],
                                                                   in1=trixsel[:, :], op0=ALU.mult, op1=ALU.add),
                         r=[tri["bx"], nflg, trixsel], w=[trixsel])
                    items = [dict(own=False, src=J["HGo"], tix=s * NTO + t, chunk_order=(0, 1), trii=None, trix=trixsel,
                                  oml=omlsel, d_idx=0) for t in range(NTO)]

                    def pre(fC=fC):
                        K.op(DVE, lambda: v.tensor_scalar(out=S[:, :], in0=S[:, :], scalar1=flg[:, fC:fC + 1], scalar2=None,
                                                          op0=ALU.mult), r=[S, flg], w=[S])

                    def post(sg_=sg_):
                        fEF, fEB = FL_ENDF + sg_, FL_ENDB + sg_
                        K.op(DVE, lambda: v.scalar_tensor_tensor(out=SF[:, :], in0=S[:, :], scalar=flg[:, fEF:fEF + 1],
                                                                 in1=SF[:, :], op0=ALU.mult, op1=ALU.add), r=[S, flg, SF], w=[SF])
                        K.op(DVE, lambda: v.scalar_tensor_tensor(out=SB[:, :], in0=S[:, :], scalar=flg[:, fEB:fEB + 1],
                                                                 in1=SB[:, :], op0=ALU.mult, op1=ALU.add), r=[S, flg, SB], w=[SB])
                    items[0]["pre"] = pre
                    items[-1]["post"] = post
                    yield from run_items(J, items)
                done[J["name"]] = (SF, SB)

            def pass_own(J, d_idx, done):
                while J["name"] not in done:
                    yield
                S0 = done[J["name"]][d_idx]
                order, co, ti_, tx_ = [(range(NTO), (0, 1), "fi", "fx"), (range(NTO - 1, -1, -1), (1, 0), "bi", "bx")][d_idx]
                items = [dict(own=True, src=J["HG"], tix=t, chunk_order=co, trii=tri[ti_], trix=tri[tx_], oml=oml[d_idx],
                              d_idx=d_idx) for t in order]

                def pre():
                    K.op(DVE, lambda: v.tensor_copy(out=S[:, :], in_=S0[:, :]), r=[S0], w=[S])
                    K.op(DVE, lambda: v.tensor_copy(out=Sbf[sbi[0] % 2][:, :], in_=S[:, :]), r=[S], w=[Sbf[sbi[0] % 2]])
                items[0]["pre"] = pre
                yield from run_items(J, items)

            return pass_others, pass_own

        def chain(*gens):
            for g_ in gens:
                yield from g_

        JP, JS = cfg.jobs
        done = {}
        po1, pw1 = make_bufs()
        po2, pw2 = make_bufs()
        queue = [(JP, 0), (JS, 0), (JP, 1), (JS, 1)]

        def worker(po, pw, J0):
            yield from po(J0, done)
            while queue:
                Jn, d_ = queue.pop(0)
                yield from pw(Jn, d_, done)

        active = [worker(po1, pw1, JP), worker(po2, pw2, JS)]
        while active:
            for gth in list(active):
                try:
                    next(gth)
                except StopIteration:
                    active.remove(gth)

    K.barrier()

    with ExitStack() as st:
        LKmax = max(J["NK"] for J in cfg.jobs)
        wt = sb("wt", [128, NH, 1152], F32, st)
        idxm = sb("idxm", [128, 384], F32, st)
        wtmp = sb("wtmp", [128, 384], F32, st)
        ktb2 = [sb("ktb", [128, LKmax * 128], BF16, st) for _ in range(2)]
        vab = sb("vab", [128, LKmax, 129], BF16, st)
        qtb2 = [sb("qtb", [128, TO], BF16, st) for _ in range(2)]
        accsb = [sb("accsb", [128, 8, 129], F32, st) for _ in range(2)]
        epi = [0]
        NPB = 3
        pT = [sb("pT", [128, 1024], BF16, st) for _ in range(NPB)]
        stmp = [sb("stmp", [128, 2, 512], F32, st) for _ in range(2)]
        nrm = [sb("nrm", [128, 16], F32, st) for _ in range(2)]
        otmp = [sb("otmp", [128, 512], F32, st) for _ in range(2)]
        osqa = [sb("osqa", [128, 512], F32, st) for _ in range(2)]
        oab = [sb("oab", [128, 4, 128], BF16, st) for _ in range(2)]
        s_ps = [ps("sps", [128, 1024], F32, st) for _ in range(2)]
        acc_ps = [ps("acc", [128, 3, 129], F32, st) for _ in range(3)]

        K.dma(SP, idxm[:, :], idxm_d.t[:, :], r=[idxm_d], w=[idxm], sb=idxm)
        for h in range(NH):
            K.op(DVE, lambda h=h: v.tensor_scalar(out=wt[:, h, 0:384], in0=idxm[:, :], scalar1=0.0,
                                                  scalar2=relb[:, 31 * 4 + h:31 * 4 + h + 1], op0=ALU.mult, op1=ALU.add),
                 r=[idxm, relb], w=[wt])
            K.op(DVE, lambda h=h: v.tensor_scalar(out=wt[:, h, 768:1152], in0=idxm[:, :], scalar1=0.0,
                                                  scalar2=relb[:, 15 * 4 + h:15 * 4 + h + 1], op0=ALU.mult, op1=ALU.add),
                 r=[idxm, relb], w=[wt])
            for b in range(32):
                dst = wt[:, h, 384:768] if b == 0 else wtmp[:, :]
                K.op(DVE, lambda h=h, b=b, dst=dst: v.tensor_scalar(out=dst, in0=idxm[:, :], scalar1=float(b),
                                                                    scalar2=relb[:, b * 4 + h:b * 4 + h + 1],
                                                                    op0=ALU.is_equal, op1=ALU.mult),
                     r=[idxm, relb], w=[wt if b == 0 else wtmp])
                if b > 0:
                    K.op(DVE, lambda h=h: v.tensor_tensor(out=wt[:, h, 384:768], in0=wt[:, h, 384:768], in1=wtmp[:, :], op=ALU.add),
                         r=[wt, wtmp], w=[wt])

        it = [0]
        NG = TO // 512
        units = [(J, h) for J in cfg.jobs for h in range(NH)]

        def load_kq(u):
            J, h = units[u]
            kb_, qb_ = ktb2[u % 2], qtb2[u % 2]
            K.dma(SP, kb_[:, 0:J["NK"] * 128], J["KT"].t[h, :, :], r=[J["KT"]], w=[kb_], sb=kb_)
            K.dma(SP, qb_[:, :], J["QT"].t[h, :, :], r=[J["QT"]], w=[qb_], sb=qb_)

        def epilogue(J, h, gq, asb):
            e = epi[0] % 2
            epi[0] += 1
            n_, o_, q_, ob = nrm[e], otmp[e], osqa[e], oab[e]
            o4 = o_[:, :].rearrange("p (a d) -> p a d", a=4)
            q4 = q_[:, :].rearrange("p (a d) -> p a d", a=4)
            K.op(DVE, lambda: v.reciprocal(out=n_[:, 0:8], in_=asb[:, :, 128]), r=[asb], w=[n_])
            K.op(DVE, lambda: v.tensor_scalar(out=n_[:, 4:8], in0=n_[:, 4:8], scalar1=lam_t[:, 0:1], scalar2=None, op0=ALU.mult),
                 r=[n_, lam_t], w=[n_])
            yield
            K.op(DVE, lambda: v.tensor_tensor(out=o4, in0=asb[:, 0:4, 0:128], in1=n_[:, 0:4].unsqueeze(2).to_broadcast([128, 4, 128]),
                                              op=ALU.mult), r=[asb, n_], w=[o_])
            K.op(DVE, lambda: v.tensor_tensor(out=q4, in0=asb[:, 4:8, 0:128], in1=n_[:, 4:8].unsqueeze(2).to_broadcast([128, 4, 128]),
                                              op=ALU.mult), r=[asb, n_], w=[q_])
            yield
            K.op(DVE, lambda: v.tensor_tensor(out=o_[:, :], in0=o_[:, :], in1=q_[:, :], op=ALU.add), r=[o_, q_], w=[o_])
            yield
            K.op(DVE, lambda: v.tensor_tensor(out=q_[:, :], in0=o_[:, :], in1=o_[:, :], op=ALU.mult), r=[o_], w=[q_])
            yield
            K.op(DVE, lambda: v.tensor_reduce(out=n_[:, 8:12], in_=q4, axis=AX.X, op=ALU.add), r=[q_], w=[n_])
            yield
            K.op(ACT, lambda: a.activation(out=n_[:, 12:16], in_=n_[:, 8:12], func=AF.Ln, bias=eps_c[:], scale=1.0 / 128),
                 r=[n_, eps_c], w=[n_])
            yield
            K.op(ACT, lambda: a.activation(out=n_[:, 12:16], in_=n_[:, 12:16], func=AF.Exp, scale=-0.5), r=[n_], w=[n_])
            yield
            K.op(DVE, lambda: v.tensor_tensor(out=o4, in0=o4, in1=n_[:, 12:16].unsqueeze(2).to_broadcast([128, 4, 128]), op=ALU.mult),
                 r=[o_, n_], w=[o_])
            yield
            K.op(DVE, lambda: v.scalar_tensor_tensor(out=ob[:, :, :], in0=o4, scalar=1.0 - LAM_INIT,
                                                     in1=subln[:, :].unsqueeze(1).to_broadcast([128, 4, 128]),
                                                     op0=ALU.mult, op1=ALU.mult), r=[o_, subln], w=[ob])
            yield
            K.dma(POOL, J["O"].t[gq * 4:(gq + 1) * 4, :, h * 128:(h + 1) * 128].rearrange("t p c -> p t c"), ob[:, :, :],
                  r=[ob], w=[J["O"]], sb=ob)

        pending = []
        load_kq(0)
        for u, (J, h) in enumerate(units):
            NK, NF = J["NK"], J["NF"]
            jj = J["j"]
            ktb, qtb = ktb2[u % 2], qtb2[u % 2]
            K.dma(SP, vab[:, 0:NK, :], J["V"].t[h, :, :, :], r=[J["V"]], w=[vab], sb=vab)
            if u + 1 < len(units):
                load_kq(u + 1)
            if True:
                for gq in range(NG):
                    sched = []
                    for kt in range(NK):
                        if kt < NF:
                            s = kt // NTO
                            last = (kt % NTO) == NTO - 1
                            col = (16 if last else 0) + 4 * (J["seg0"] + s) + h
                            sched.append((kt, "const", bfar[:, col:col + 1]))
                        else:
                            if kt == NF:
                                nb, mask = -1, flg[:, FL_PREVM + jj:FL_PREVM + jj + 1]
                            elif kt == NF + 1:
                                nb, mask = NTO, flg[:, FL_NEXTM + jj:FL_NEXTM + jj + 1]
                            else:
                                nb, mask = kt - NF - 2, None
                            dlt = nb - 4 * gq
                            if -1 <= dlt <= 4:
                                sched.append((kt, "tile", (128 * (4 - dlt), mask)))
                            else:
                                side = relb[:, 15 * 4 + h:15 * 4 + h + 1] if dlt < 0 else relb[:, 31 * 4 + h:31 * 4 + h + 1]
                                if mask is not None:
                                    which = 0 if kt == NF else 1
                                    c0_ = (jj * 2 + which) * 4 + h
                                    sched.append((kt, "const", hbias[:, c0_:c0_ + 1]))
                                else:
                                    sched.append((kt, "const", side))
                    nsch = len(sched)

                    def qk(n):
                        kt = sched[n][0]
                        i = it[0] + n
                        sp_ = s_ps[i % 2]
                        K.op(PE, lambda: nc.tensor.matmul(sp_[:, 0:512], lhsT=ktb[0:64, kt * 128:(kt + 1) * 128],
                                                          rhs=qtb[0:64, gq * 512:(gq + 1) * 512], start=True, stop=True),
                             r=[ktb, qtb], w=[sp_], sig=False)
                        K.op(PE, lambda: nc.tensor.matmul(sp_[:, 512:1024], lhsT=ktb[64:128, kt * 128:(kt + 1) * 128],
                                                          rhs=qtb[64:128, gq * 512:(gq + 1) * 512], start=True, stop=True),
                             r=[ktb, qtb], w=[sp_])

                    def expo(n):
                        kt, mode, arg = sched[n]
                        i = it[0] + n
                        sp_, p_ = s_ps[i % 2], pT[i % NPB]
                        if mode == "const":
                            K.op(ACT, lambda: a.activation(out=p_[:, :], in_=sp_[:, :], func=AF.Exp, bias=arg, scale=1.0),
                                 r=[sp_, bfar, relb, hbias], w=[p_])
                        else:
                            c0, mask = arg
                            tb = stmp[i % 2]
                            K.op(DVE, lambda: v.tensor_tensor(
                                out=tb[:, :, :], in0=sp_[:, :].rearrange("p (c q) -> p c q", c=2),
                                in1=wt[:, h, c0:c0 + 512].unsqueeze(1).to_broadcast([128, 2, 512]), op=ALU.add),
                                r=[sp_, wt], w=[tb])
                            if mask is None:
                                K.op(ACT, lambda: a.activation(out=p_[:, :], in_=tb[:, :, :].rearrange("p c q -> p (c q)"),
                                                               func=AF.Exp), r=[tb], w=[p_])
                            else:
                                K.op(ACT, lambda: a.activation(out=p_[:, :], in_=tb[:, :, :].rearrange("p c q -> p (c q)"),
                                                               func=AF.Exp, bias=mask, scale=1.0), r=[tb, flg], w=[p_])

                    def pv(n):
                        kt = sched[n][0]
                        i = it[0] + n
                        p_ = pT[i % NPB]
                        for c in range(2):
                            for qs in range(4):
                                ai = c * 4 + qs
                                acc = acc_ps[ai // 3]
                                K.op(PE, lambda c=c, qs=qs, ai=ai, acc=acc: nc.tensor.matmul(
                                    acc[:, ai % 3, :], lhsT=p_[:, c * 512 + qs * 128:c * 512 + (qs + 1) * 128], rhs=vab[:, kt, :],
                                    start=(n == 0 and ai % 3 == 0), stop=(n == nsch - 1), skip_group_check=True),
                                    r=[p_, vab], w=[acc], sig=(ai == 7))

                    qk(0)
                    for n in range(nsch):
                        if n + 1 < nsch:
                            qk(n + 1)
                        expo(n)
                        pv(n)
                        if n >= 3 and pending:
                            try:
                                next(pending[0])
                            except StopIteration:
                                pending.pop(0)
                    it[0] += nsch
                    while pending:
                        for _ in pending[0]:
                            pass
                        pending.pop(0)
                    asb = accsb[(u * NG + gq) % 2]
                    for b_ in range(3):
                        nacc = 3 if b_ < 2 else 2
                        K.op(DVE, lambda b_=b_, nacc=nacc: v.tensor_copy(out=asb[:, 3 * b_:3 * b_ + nacc, :], in_=acc_ps[b_][:, 0:nacc, :]),
                             r=[acc_ps[b_]], w=[asb])
                    pending.append(epilogue(J, h, gq, asb))
        while pending:
            for _ in pending[0]:
                pass
            pending.pop(0)

    K.barrier()

    with ExitStack() as st:
        wo = sb("wo", [128, 8, D], BF16, st)
        xin = [sb("xin", [128, D], F32, st) for _ in range(4)]
        oin = [sb("oin", [128, D], BF16, st) for _ in range(4)]
        oT = [sb("oT", [128, 8, 128], BF16, st) for _ in range(4)]
        ofd = [sb("ofd", [128, 512], F32, st) for _ in range(4)]
        obd = [sb("obd", [128, 512], F32, st) for _ in range(4)]
        gsd = [sb("gsd", [128, 512], F32, st) for _ in range(4)]
        ss4d = [sb("ss4d", [128, 8], F32, st) for _ in range(4)]
        x1 = [sb("x1", [128, D], F32, st) for _ in range(4)]
        ssd = [sb("ssd", [128, 2], F32, st) for _ in range(4)]
        sqd = sb("sqd", [128, D], BF16, st)
        h2 = [sb("h2", [128, D], BF16, st) for _ in range(4)]
        h2Tst = [sb("h2Tst", [128, 8, 512], BF16, st) for _ in range(2)]
        tpx_ps = [ps("tpx", [128, 8, 128], BF16, st) for _ in range(4)]
        prh_ps = [ps("prh", [128, 512], F32, st) for _ in range(4)]

        wo_v = w_out_d.t.rearrange("(c p) n -> p c n", p=128)
        for c in range(8):
            s_ = xin[c % 4]
            K.dma(SP, s_[:, :], wo_v[:, c, :], r=[w_out_d], w=[s_], sb=s_)
            if c % 2 == 0:
                K.op(DVE, lambda c=c, s_=s_: v.tensor_copy(out=wo[:, c, :], in_=s_[:, :]), r=[s_], w=[wo])
            else:
                K.op(ACT, lambda c=c, s_=s_: a.copy(out=wo[:, c, :], in_=s_[:, :]), r=[s_], w=[wo])

        def d1_tile(J, i, ti, hst, t):
            NF = J["NF"]
            xi, oi, oT_, x1_ = xin[i % 4], oin[i % 4], oT[i % 4], x1[i % 4]
            K.dma(SP, xi[:, :], J["x"].t[(NF + 2 + ti) * 128:(NF + 3 + ti) * 128, :], r=[J["x"]], w=[xi], sb=xi)
            K.dma(SP, oi[:, 0:512], J["O"].t[ti, :, 0:512], r=[J["O"]], w=[oi], sb=oi)
            of_, ob_, gs_, s4 = ofd[i % 4], obd[i % 4], gsd[i % 4], ss4d[i % 4]
            K.dma(SP, of_[:, :], J["OF"].t[ti, :, :], r=[J["OF"]], w=[of_], sb=of_)
            K.dma(SP, ob_[:, :], J["OB"].t[ti, :, :], r=[J["OB"]], w=[ob_], sb=ob_)
            K.dma(SP, gs_[:, :], J["HG"].t[ti, :, 6, :], r=[J["HG"]], w=[gs_], sb=gs_)
            yield
            K.op(DVE, lambda: v.tensor_tensor(out=of_[:, :], in0=of_[:, :], in1=ob_[:, :], op=ALU.add), r=[of_, ob_], w=[of_])
            K.op(DVE, lambda: v.tensor_tensor(out=ob_[:, :], in0=of_[:, :], in1=of_[:, :], op=ALU.mult), r=[of_], w=[ob_])
            yield
            K.op(DVE, lambda: v.tensor_reduce(out=s4[:, 0:4], in_=ob_[:, :].rearrange("p (h d) -> p h d", h=NH),
                                              axis=AX.X, op=ALU.add), r=[ob_], w=[s4])
            yield
            K.op(ACT, lambda: a.activation(out=s4[:, 4:8], in_=s4[:, 0:4], func=AF.Ln, bias=eps_c[:], scale=1.0 / 128),
                 r=[s4, eps_c], w=[s4])
            yield
            K.op(ACT, lambda: a.activation(out=s4[:, 4:8], in_=s4[:, 4:8], func=AF.Exp, scale=-0.5), r=[s4], w=[s4])
            yield
            o3 = of_[:, :].rearrange("p (h d) -> p h d", h=NH)
            K.op(DVE, lambda: v.tensor_tensor(out=o3, in0=o3, in1=s4[:, 4:8].unsqueeze(2).to_broadcast([128, NH, 128]),
                                              op=ALU.mult), r=[of_, s4], w=[of_])
            K.op(DVE, lambda: v.tensor_tensor(out=of_[:, :], in0=of_[:, :], in1=hnw[:, :], op=ALU.mult), r=[of_, hnw], w=[of_])
            yield
            K.op(DVE, lambda: v.tensor_tensor(out=oi[:, 512:1024], in0=of_[:, :], in1=gs_[:, :], op=ALU.mult),
                 r=[of_, gs_], w=[oi])
            yield
            tp = tpx_ps[i % 4]
            for c in range(8):
                K.op(PE, lambda c=c: nc.tensor.transpose(tp[:, c, :], oi[:, c * 128:(c + 1) * 128], ident[:]),
                     r=[oi, ident], w=[tp], sig=(c == 7))
            yield
            K.op(DVE, lambda: v.tensor_copy(out=oT_[:, :, :], in_=tp[:, :, :]), r=[tp], w=[oT_])
            yield
            pp = prh_ps[i % 4]
            for half in range(2):
                hc = slice(half * 512, (half + 1) * 512)
                for c in range(8):
                    K.op(PE, lambda c=c, hc=hc: nc.tensor.matmul(pp[:, :], lhsT=oT_[:, c, :], rhs=wo[:, c, hc],
                                                                 start=(c == 0), stop=(c == 7)), r=[oT_, wo], w=[pp], sig=(c == 7))
                yield
                K.op(DVE, lambda hc=hc: v.tensor_tensor(out=x1_[:, hc], in0=pp[:, :], in1=xi[:, hc], op=ALU.add),
                     r=[pp, xi], w=[x1_])
                yield
            K.dma(POOL, J["X1"].t[ti, :, :], x1_[:, :], r=[x1_], w=[J["X1"]], sb=x1_)
            yield
            s_ = ssd[i % 4]
            K.op(ACT, lambda: a.activation(out=sqd[:, :], in_=x1_[:, :], func=AF.Square, accum_out=s_[:, 0:1]),
                 r=[x1_], w=[sqd, s_])
            yield
            K.op(ACT, lambda: a.activation(out=s_[:, 1:2], in_=s_[:, 0:1], func=AF.Ln, bias=eps_c[:], scale=1.0 / D),
                 r=[s_, eps_c], w=[s_])
            yield
            K.op(ACT, lambda: a.activation(out=s_[:, 1:2], in_=s_[:, 1:2], func=AF.Exp, scale=-0.5), r=[s_], w=[s_])
            yield
            h2_ = h2[i % 4]
            K.op(ACT, lambda: a.activation(out=h2_[:, :], in_=x1_[:, :], func=AF.Copy, scale=s_[:, 1:2]), r=[x1_, s_], w=[h2_])
            yield
            tp2 = tpx_ps[i % 4]
            for c in range(8):
                K.op(PE, lambda c=c: nc.tensor.transpose(tp2[:, c, :], h2_[:, c * 128:(c + 1) * 128], ident[:]),
                     r=[h2_, ident], w=[tp2], sig=(c == 7))
            yield
            K.op(DVE, lambda: v.tensor_copy(out=hst[:, :, t * 128:(t + 1) * 128], in_=tp2[:, :, :]), r=[tp2], w=[hst])
            if t == 3:
                t0_ = ti - 3
                for c in range(8):
                    K.dma(POOL, J["H2T"].t[c, :, t0_ * 128:(t0_ + 4) * 128], hst[:, c, :], r=[hst], w=[J["H2T"]], sb=hst)

        tiles = []
        for J in cfg.jobs:
            for ti in range(NTO):
                tiles.append((J, ti))
        gens = [d1_tile(J, i, ti, h2Tst[(i // 4) % 2], ti % 4) for i, (J, ti) in enumerate(tiles)]
        live = []
        nxt_i = 0
        while live or nxt_i < len(gens):
            while len(live) < 4 and nxt_i < len(gens):
                live.append(gens[nxt_i])
                nxt_i += 1
            for gth in list(live):
                try:
                    next(gth)
                except StopIteration:
                    live.remove(gth)

    K.barrier()

    with ExitStack() as st:
        wmi = sb("wmi", [128, 8, DFF], BF16, st)
        wmo = sb("wmo", [128, 32, D], BF16, st)
        h2T = [sb("h2T", [128, 8, 512], BF16, st) for _ in range(2)]
        aT = sb("aT", [128, 32, 512], BF16, st)
        rl = [sb("rl", [128, 512], F32, st) for _ in range(2)]
        x1in = [sb("x1in", [128, D], F32, st) for _ in range(2)]
        yo = [sb("yo", [128, D], F32, st) for _ in range(1)]
        mi_ps = [ps("mid", [128, 512], F32, st) for _ in range(2)]
        yo_ps = [ps("yod", [128, 1024], F32, st) for _ in range(2)]

        stgs = x1in + yo
        cnt = 0

        def wload(dst, dbuf, src_ap, srcbuf, scale_ap, scale_buf):
            nonlocal cnt
            s_ = stgs[cnt % 3]
            K.dma(SP, s_[:, :], src_ap, r=[srcbuf], w=[s_], sb=s_)
            if cnt % 2 == 0:
                if scale_ap is None:
                    K.op(DVE, lambda: v.tensor_copy(out=dst, in_=s_[:, :]), r=[s_], w=[dbuf])
                else:
                    K.op(DVE, lambda: v.tensor_scalar(out=dst, in0=s_[:, :], scalar1=scale_ap, scalar2=None, op0=ALU.mult),
                         r=[s_, scale_buf], w=[dbuf])
            else:
                if scale_ap is None:
                    K.op(ACT, lambda: a.copy(out=dst, in_=s_[:, :]), r=[s_], w=[dbuf])
                else:
                    K.op(ACT, lambda: a.activation(out=dst, in_=s_[:, :], func=AF.Copy, scale=scale_ap), r=[s_, scale_buf], w=[dbuf])
            cnt += 1

        wmi_v = w_mi_d.t.rearrange("(c p) n -> p c n", p=128)
        wmo_v = w_mo_d.t.rearrange("(c p) n -> p c n", p=128)
        for c in range(8):
            for q4 in range(4):
                wload(wmi[:, c, q4 * 1024:(q4 + 1) * 1024], wmi, wmi_v[:, c, q4 * 1024:(q4 + 1) * 1024], w_mi_d, mnw[:, c:c + 1], mnw)
        for c in range(32):
            wload(wmo[:, c, :], wmo, wmo_v[:, c, :], w_mo_d, None, None)

        di = [0]
        sti2 = [0]
        for J in cfg.jobs:
            for t0 in range(0, NTO, 4):
                hT_ = h2T[sti2[0] % 2]
                sti2[0] += 1
                for c in range(8):
                    K.dma(SP, hT_[:, c, :], J["H2T"].t[c, :, t0 * 128:(t0 + 4) * 128], r=[J["H2T"]], w=[hT_], sb=hT_)
                for f in range(32):
                    mp = mi_ps[f % 2]
                    for c in range(8):
                        K.op(PE, lambda c=c, f=f: nc.tensor.matmul(mp[:, :], lhsT=wmi[:, c, f * 128:(f + 1) * 128], rhs=hT_[:, c, :],
                                                                   start=(c == 0), stop=(c == 7)), r=[wmi, hT_], w=[mp], sig=(c == 7))
                    r_ = rl[f % 2]
                    K.op(ACT, lambda: a.activation(out=r_[:, :], in_=mp[:, :], func=AF.Relu), r=[mp], w=[r_])
                    K.op(DVE, lambda f=f: v.tensor_tensor(out=aT[:, f, :], in0=r_[:, :], in1=r_[:, :], op=ALU.mult), r=[r_], w=[aT])
                for t in range(4):
                    i = di[0]
                    di[0] += 1
                    ti = t0 + t
                    xr = x1in[i % 2]
                    K.dma(SP, xr[:, :], J["X1"].t[ti, :, :], r=[J["X1"]], w=[xr], sb=xr)
                    yp = yo_ps[i % 2]
                    for half in range(2):
                        for f in range(32):
                            K.op(PE, lambda f=f, half=half: nc.tensor.matmul(
                                yp[:, half * 512:(half + 1) * 512], lhsT=aT[:, f, t * 128:(t + 1) * 128],
                                rhs=wmo[:, f, half * 512:(half + 1) * 512], start=(f == 0), stop=(f == 31)),
                                r=[aT, wmo], w=[yp], sig=(f == 31 and half == 1))
                    y_ = yo[0]
                    K.op(DVE, lambda: v.tensor_tensor(out=y_[:, :], in0=yp[:, :], in1=xr[:, :], op=ALU.add), r=[yp, xr], w=[y_])
                    K.dma(POOL, J["y"].t[ti * 128:(ti + 1) * 128, :], y_[:, :], r=[y_], w=[J["y"]], sb=y_)

    K.finish(POOL)
    K.finish(SP)
    return nc


def _t5_bucket_np(rel):
    nb, max_exact = 16, 8
    rel = np.asarray(rel, dtype=np.int64)
    try:
        import jax
        import jax.numpy as jnp
        with jax.default_device(jax.devices("cpu")[0]):
            r = jnp.asarray(rel.astype(np.int32))
            ret = jnp.where(r > 0, nb, 0)
            n = jnp.abs(r)
            nf = jnp.maximum(n, 1).astype(jnp.float32)
            large = max_exact + (jnp.log(nf / max_exact) / math.log(128 / max_exact) * (nb - max_exact)).astype(jnp.int32)
            large = jnp.minimum(large, nb - 1)
            return np.asarray(ret + jnp.where(n < max_exact, n, large)).astype(np.int64)
    except Exception:
        ret = np.where(rel > 0, nb, 0)
        n = np.abs(rel)
        nf = np.maximum(n, 1).astype(np.float32)
        val = (np.log(nf / np.float32(max_exact)) / np.float32(math.log(128 / max_exact)) * np.float32(nb - max_exact)).astype(np.float32)
        large = np.minimum(max_exact + np.trunc(val).astype(np.int64), nb - 1)
        return ret + np.where(n < max_exact, n, large)


def _core_layout(x_seq, L, TO, part):
    nparts = L // TO
    NTO = TO // 128
    o0, o1 = part * TO, (part + 1) * TO
    segs = []
    before = list(range(0, part))
    after = list(range(nparts - 1, part, -1))
    for p_ in before:
        segs.append(("F", p_))
    for p_ in after:
        segs.append(("B", p_))
    rows = []
    for kind, p_ in segs:
        blk = x_seq[p_ * TO:(p_ + 1) * TO]
        if kind == "B":
            blk = blk.reshape(TO // 64, 64, -1)[::-1].reshape(TO, -1)
        rows.append(blk)
    zero = np.zeros((128, x_seq.shape[1]), np.float32)
    rows.append(x_seq[o0 - 128:o0] if o0 > 0 else zero)
    rows.append(x_seq[o1:o1 + 128] if o1 < L else zero)
    rows.append(x_seq[o0:o1])
    xj = np.ascontiguousarray(np.concatenate(rows, axis=0))
    ns = len(segs)
    dirF = [1.0 if k == "F" else 0.0 for k, _ in segs]
    cont = [1.0 if (i > 0 and segs[i][0] == segs[i - 1][0]) else 0.0 for i in range(ns)]
    endF = [1.0 if (segs[i][0] == "F" and (i == ns - 1 or segs[i + 1][0] != "F")) else 0.0 for i in range(ns)]
    endB = [1.0 if (segs[i][0] == "B" and i == ns - 1) else 0.0 for i in range(ns)]
    lastm = [NEG if (endF[i] or endB[i]) else 0.0 for i in range(ns)]
    prevm = NEG if o0 == 0 else 0.0
    nextm = NEG if o1 == L else 0.0
    return xj, dict(dirF=dirF, cont=cont, endF=endF, endB=endB, lastm=lastm, prevm=prevm, nextm=nextm)


def _prepare(inputs, TO):
    xp = np.asarray(inputs["x_prompt"], np.float32)
    xs = np.asarray(inputs["x_sample"], np.float32)
    LP, LS = xp.shape[1], xs.shape[1]
    rel = np.arange(128)[:, None] - np.arange(384)[None, :] + 128
    idxm = _t5_bucket_np(rel).astype(np.float32)
    common = {
        "w_in": np.ascontiguousarray(np.asarray(inputs["w_in"], np.float32)[0]),
        "w_out": np.ascontiguousarray(np.asarray(inputs["w_out"], np.float32)[0]),
        "w_mlp_in": np.ascontiguousarray(np.asarray(inputs["w_mlp_in"], np.float32)[0]),
        "w_mlp_out": np.ascontiguousarray(np.asarray(inputs["w_mlp_out"], np.float32)[0]),
        "attn_norm_w": np.asarray(inputs["attn_norm_w"], np.float32).reshape(1, D),
        "mlp_norm_w": np.asarray(inputs["mlp_norm_w"], np.float32).reshape(1, D),
        "qk_norm_w": np.asarray(inputs["qk_norm_w"], np.float32).reshape(2, 64),
        "diff_lambda": np.ascontiguousarray(np.asarray(inputs["diff_lambda"], np.float32).reshape(4, 64)),
        "diff_subln_w": np.asarray(inputs["diff_subln_w"], np.float32).reshape(1, 128),
        "rel_bias": np.ascontiguousarray(np.asarray(inputs["rel_bias"], np.float32).reshape(32, 4)),
        "hgrn_lb": np.ascontiguousarray(np.asarray(inputs["hgrn_lb"], np.float32).reshape(2, 2, 512)),
        "hgrn_norm_w": np.asarray(inputs["hgrn_norm_w"], np.float32).reshape(1, 128),
        "idxm": idxm,
    }
    in_maps = []
    for c in range(8):
        pP, pS = LP // TO, LS // TO
        xjP, fP = _core_layout(xp[c // pP], LP, TO, c % pP)
        xjS, fS = _core_layout(xs[c // pS], LS, TO, c % pS)
        fl = np.zeros((1, NFLAG), np.float32)
        for nm, base in (("dirF", FL_DIRF), ("cont", FL_CONT), ("endF", FL_ENDF), ("endB", FL_ENDB), ("lastm", FL_LASTM)):
            fl[0, base] = fP[nm][0]
            fl[0, base + 1:base + 4] = fS[nm]
        fl[0, FL_PREVM], fl[0, FL_PREVM + 1] = fP["prevm"], fS["prevm"]
        fl[0, FL_NEXTM], fl[0, FL_NEXTM + 1] = fP["nextm"], fS["nextm"]
        m = dict(common)
        m["x_P"], m["x_S"], m["flags"] = xjP, xjS, fl
        in_maps.append(m)
    return in_maps, LP, LS


_CACHE = {}


def run(inputs, TO=4096, dbg=False, serial=False):
    in_maps, LP, LS = _prepare(inputs, TO)
    assert LP == 2 * TO and LS == 4 * TO
    key = (TO, dbg, serial)
    if key not in _CACHE:
        cfg_ = Cfg(TO, dbg)
        cfg_.serial = serial
        _CACHE[key] = build(cfg_)
    nc = _CACHE[key]
    res = run_bass_kernel_spmd(nc, in_maps, core_ids=list(range(8)))
    B, DB = inputs["x_prompt"].shape[0], inputs["x_sample"].shape[0]
    yp = np.zeros((B, LP, D), np.float32)
    ys = np.zeros((DB, LS, D), np.float32)
    for c in range(8):
        r = res.results[c]
        yp[c // 2, (c % 2) * TO:(c % 2 + 1) * TO] = r["y_P"]
        ys[c // 4, (c % 4) * TO:(c % 4 + 1) * TO] = r["y_S"]
    return (yp, ys), res


def kernel(**inputs):
    out, _ = run(inputs, TO=4096)
    return out
```
